# Optimizing a Trainium2 kernel written in Bass

```python
import math
import jax, jax.numpy as jnp
from jax import lax
import numpy as np


D_MODEL = 1024
BATCH = 8
SEQ = 4096
DEPTH = 1

CHUNK = 64
MIX_WIDTH = D_MODEL
RET_WIDTH = MIX_WIDTH // 2
CONV_WIDTH_CH = MIX_WIDTH - RET_WIDTH
RET_HEADS = 4
RET_HEAD_DIM = RET_WIDTH // RET_HEADS
CONV_GROUPS = 4
CONV_K = 3
D_FF = 2816
ROPE_BASE = 10000.0
N_IN_COLS = 4 * RET_WIDTH + 3 * CONV_WIDTH_CH
N_ADA = 6
EPS = 1e-6

kernel_name = 'hybrid_retention_shortconv_convffn_adaln'


def rms_norm(x, g):
    xf = x.astype(jnp.float32)
    y = xf * lax.rsqrt(jnp.mean(xf * xf, axis=-1, keepdims=True) + EPS)
    return (y * g.astype(jnp.float32)).astype(x.dtype)


def modulate(h, shift, scale):
    return h * (1.0 + scale[:, None, :]) + shift[:, None, :]


def causal_dwconv(x, w, b):
    ch = x.shape[-1]
    y = lax.conv_general_dilated(
        x, w[:, None, :].astype(x.dtype), window_strides=(1,),
        padding=[(CONV_K - 1, 0)], dimension_numbers=('NWC', 'WIO', 'NWC'),
        feature_group_count=ch)
    return y + b.astype(x.dtype)


def rotary(x, positions):
    dh = x.shape[-1]
    inv_freq = ROPE_BASE ** (-jnp.arange(0, dh, 2, dtype=jnp.float32) / dh)
    ang = positions.astype(jnp.float32)[..., None] * inv_freq
    cos = jnp.cos(ang)[:, :, None, :].astype(x.dtype)
    sin = jnp.sin(ang)[:, :, None, :].astype(x.dtype)
    x1, x2 = jnp.split(x, 2, axis=-1)
    return jnp.concatenate([x1 * cos - x2 * sin, x2 * cos + x1 * sin], axis=-1)


def chunk_retention(q, k, v, log_gamma):
    b, s, h, dk = q.shape
    dv = v.shape[-1]
    n = s // CHUNK
    q = q.reshape(b, n, CHUNK, h, dk).astype(jnp.float32) * (dk ** -0.5)
    k = k.reshape(b, n, CHUNK, h, dk).astype(jnp.float32)
    v = v.reshape(b, n, CHUNK, h, dv).astype(jnp.float32)
    j = jnp.arange(CHUNK, dtype=jnp.float32)
    intra_decay = jnp.exp(jnp.abs(j[:, None] - j[None, :])[None] * log_gamma[:, None, None])
    scores = jnp.einsum('bnihd,bnjhd->bnhij', q, k) * intra_decay
    intra = jnp.einsum('bnhij,bnjhv->bnihv', scores, v)
    k_w = jnp.exp((CHUNK - 1 - j)[:, None] * log_gamma[None, :])
    kv = jnp.einsum('bnjhd,jh,bnjhv->nbhdv', k, k_w, v)
    chunk_decay = jnp.exp(CHUNK * log_gamma)[None, :, None, None]

    def step(state, kv_n):
        return state * chunk_decay + kv_n, state

    _, prev = lax.scan(step, jnp.zeros((b, h, dk, dv), jnp.float32), kv)
    q_w = jnp.exp((j + 1.0)[:, None] * log_gamma[None, :])
    cross = jnp.einsum('bnihd,ih,nbhdv->bnihv', q, q_w, prev)
    return (intra + cross).reshape(b, s, h, dv)


def setup_inputs(seed: int = 0) -> dict:
    key = jax.random.key(seed)
    ks = jax.random.split(key, 20)
    f32 = jnp.float32
    nrm = lambda k, shape, scale: jax.random.normal(k, shape, f32) * scale
    x = jax.random.normal(ks[0], (BATCH, SEQ, D_MODEL), f32)
    c = jax.random.normal(ks[1], (BATCH, D_MODEL), f32)
    positions = jnp.broadcast_to(jnp.arange(SEQ, dtype=jnp.int32)[None, :], (BATCH, SEQ))
    return {
        'x': x,
        'c': c,
        'positions': positions,
        'w_ada': nrm(ks[2], (D_MODEL, N_ADA * D_MODEL), D_MODEL ** -0.5),
        'b_ada': nrm(ks[3], (N_ADA * D_MODEL,), 0.02),
        'norm1_g': 1.0 + nrm(ks[4], (D_MODEL,), 0.02),
        'w_in': nrm(ks[5], (D_MODEL, N_IN_COLS), D_MODEL ** -0.5),
        'conv_mix_w': nrm(ks[6], (CONV_K, CONV_WIDTH_CH), CONV_K ** -0.5),
        'conv_mix_b': nrm(ks[7], (CONV_WIDTH_CH,), 0.02),
        'ret_norm_g': 1.0 + nrm(ks[8], (RET_WIDTH,), 0.02),
        'conv_norm_g': 1.0 + nrm(ks[9], (CONV_WIDTH_CH,), 0.02),
        'w_out': nrm(ks[10], (MIX_WIDTH, D_MODEL), MIX_WIDTH ** -0.5),
        'norm2_g': 1.0 + nrm(ks[11], (D_MODEL,), 0.02),
        'w_up': nrm(ks[12], (D_MODEL, 2 * D_FF), D_MODEL ** -0.5),
        'conv_ffn_w': nrm(ks[13], (CONV_K, 2 * D_FF), CONV_K ** -0.5),
        'conv_ffn_b': nrm(ks[14], (2 * D_FF,), 0.02),
        'w_down': nrm(ks[15], (D_FF, D_MODEL), D_FF ** -0.5),
        'final_g': 1.0 + nrm(ks[16], (D_MODEL,), 0.02),
    }


def reference(x, c, positions, w_ada, b_ada, norm1_g, w_in, conv_mix_w, conv_mix_b,
              ret_norm_g, conv_norm_g, w_out, norm2_g, w_up, conv_ffn_w, conv_ffn_b,
              w_down, final_g):
    b, s, _ = x.shape
    log_gamma = jnp.log(1.0 - 2.0 ** (-5.0 - jnp.arange(RET_HEADS, dtype=jnp.float32)))
    for _layer in range(DEPTH):
        mod = jax.nn.silu(c) @ w_ada + b_ada
        sh1, sc1, gt1, sh2, sc2, gt2 = jnp.split(mod, N_ADA, axis=-1)

        h = modulate(rms_norm(x, norm1_g), sh1, sc1)
        proj = h @ w_in
        q, k, v, g, hc, gb, gc = jnp.split(proj, 7, axis=-1)

        q = rotary(q.reshape(b, s, RET_HEADS, RET_HEAD_DIM), positions)
        k = rotary(k.reshape(b, s, RET_HEADS, RET_HEAD_DIM), positions)
        v = v.reshape(b, s, RET_HEADS, RET_HEAD_DIM)
        o = chunk_retention(q, k, v, log_gamma)
        mu = jnp.mean(o, axis=-1, keepdims=True)
        var = jnp.mean(jnp.square(o - mu), axis=-1, keepdims=True)
        o = ((o - mu) * lax.rsqrt(var + EPS)).reshape(b, s, RET_WIDTH)
        ret_out = jax.nn.silu(g) * (o * ret_norm_g).astype(x.dtype)

        conv_y = gb * causal_dwconv(gc * hc, conv_mix_w, conv_mix_b)
        conv_out = rms_norm(conv_y, conv_norm_g)

        mixed = jnp.concatenate([ret_out, conv_out], axis=-1) @ w_out
        x = x + gt1[:, None, :] * mixed

        h = modulate(rms_norm(x, norm2_g), sh2, sc2)
        u = causal_dwconv(h @ w_up, conv_ffn_w, conv_ffn_b)
        ua, ub = jnp.split(u, 2, axis=-1)
        x = x + gt2[:, None, :] * ((jax.nn.silu(ua) * ub) @ w_down)
    return rms_norm(x, final_g)
```

```python
import contextlib
import numpy as np
import concourse.bass as bass
import concourse.mybir as mybir
from concourse.bass_utils import run_bass_kernel_spmd

F32 = mybir.dt.float32
BF16 = mybir.dt.bfloat16
I32 = mybir.dt.int32
AF = mybir.ActivationFunctionType
ALU = mybir.AluOpType

ENGS = ("pe", "act", "dve", "pool", "sp")
import os as _os
_KP = _os.environ.get("KPOOL", "rot,halo,gmul,res,sg").split(",")


def _pe(group):
    return "pool" if group in _KP else "dve"
DMA_POOL = 16

D = 1024
SEQ = 4096
NB = 8
T = 512
NT = SEQ // T
DFF = 2816
NFF = DFF // 128
EPS = 1e-6
HEADS = 4


class Buf:
    def __init__(self, t, space, lo, hi):
        self.t = t
        self.space = space
        self.lo = lo
        self.hi = hi

    def __getitem__(self, k):
        return self.t[k]

    def sub(self, lo, hi):
        return Buf(self.t, self.space, self.lo + lo, self.lo + hi)


class _Op:
    __slots__ = ("eng", "fn", "idx", "waits", "inc", "dma", "dsem", "dcnt", "cnt", "selfwait")

    def __init__(self, eng, fn, dma):
        self.eng = eng
        self.fn = fn
        self.dma = dma
        self.waits = {}
        self.inc = False
        self.dsem = None
        self.dcnt = 0
        self.cnt = 0
        self.selfwait = None


class Sched:
    def __init__(self, nc):
        self.nc = nc
        self.ops = {e: [] for e in ENGS}
        self.spaces = {}
        self.dma_rr = {e: 0 for e in ENGS}
        self.dma_cnt = {}
        self.dma_last = {}
        self.psum_rd = {}

    def _dep(self, op, prod):
        if prod is None or prod is op:
            return
        if prod.dma:
            key = ("d", prod.eng, prod.dsem)
            val = prod.dcnt * 16
            cur = op.waits.get(key)
            op.waits[key] = val if cur is None else max(cur, val)
            return
        if prod.eng == "pe" and op.eng == "pe":
            return
        prod.inc = True
        key = ("c", prod.eng)
        cur = op.waits.get(key)
        if cur is None or cur.idx < prod.idx:
            op.waits[key] = prod

    def add(self, eng, fn, reads=(), writes=(), dma=False):
        op = _Op(eng, fn, dma)
        op.idx = len(self.ops[eng])
        self.ops[eng].append(op)
        if dma:
            slot = self.dma_rr[eng] % DMA_POOL
            self.dma_rr[eng] += 1
            k = (eng, slot)
            self.dma_cnt[k] = self.dma_cnt.get(k, 0) + 1
            op.dsem = slot
            op.dcnt = self.dma_cnt[k]
            prev = self.dma_last.get(k)
            if prev is not None:
                op.selfwait = (slot, prev.dcnt * 16)
            self.dma_last[k] = op
        for b in reads:
            if b.space == "psum":
                for bank in range(b.lo // 2048, (b.hi - 1) // 2048 + 1):
                    rd = self.psum_rd.setdefault(bank, {})
                    for oe, oop in rd.items():
                        if oe != eng:
                            self._dep(op, oop)
                    rd[eng] = op
            recs = self.spaces.setdefault(b.space, [])
            for r in recs:
                if r[0] < b.hi and b.lo < r[1]:
                    self._dep(op, r[2])
                    if dma:
                        r[4].append(op)
                    else:
                        r[3][eng] = op
        for b in writes:
            recs = self.spaces.setdefault(b.space, [])
            keep = []
            for r in recs:
                if r[0] < b.hi and b.lo < r[1]:
                    self._dep(op, r[2])
                    for rd in r[3].values():
                        self._dep(op, rd)
                    for rd in r[4]:
                        self._dep(op, rd)
                    if r[0] < b.lo:
                        keep.append([r[0], b.lo, r[2], dict(r[3]), list(r[4])])
                    if b.hi < r[1]:
                        keep.append([b.hi, r[1], r[2], dict(r[3]), list(r[4])])
                else:
                    keep.append(r)
            keep.append([b.lo, b.hi, op, {}, []])
            self.spaces[b.space] = keep
        return op

    def emit(self, final_waits):
        nc = self.nc
        for e in ENGS:
            c = 0
            for op in self.ops[e]:
                if op.inc and not op.dma:
                    c += 1
                op.cnt = c
        with contextlib.ExitStack() as st:
            csem = {e: st.enter_context(nc.semaphore("c_" + e)) for e in ENGS}
            dsem = {}
            for e in ENGS:
                for s in range(min(DMA_POOL, self.dma_rr[e])):
                    dsem[(e, s)] = st.enter_context(nc.semaphore("d_%s_%d" % (e, s)))
            block = st.enter_context(nc.Block())

            def run(e, engine):
                waited = {}
                for op in self.ops[e]:
                    for key, val in op.waits.items():
                        if key[0] == "d":
                            sem = dsem[(key[1], key[2])]
                            v = val
                        else:
                            sem = csem[key[1]]
                            v = val.cnt
                        if waited.get(key, 0) >= v:
                            continue
                        waited[key] = v
                        engine.wait_ge(sem, v)
                    if op.dma:
                        if op.selfwait is not None:
                            key = ("d", e, op.selfwait[0])
                            if waited.get(key, 0) < op.selfwait[1]:
                                waited[key] = op.selfwait[1]
                                engine.wait_ge(dsem[(e, op.selfwait[0])], op.selfwait[1])
                        op.fn(engine).then_inc(dsem[(e, op.dsem)], 16)
                    else:
                        ins = op.fn(engine)
                        if op.inc:
                            ins.then_inc(csem[e], 1)
                if e in final_waits:
                    for (qe, slot), op in self.dma_last.items():
                        if qe in final_waits[e]:
                            engine.wait_ge(dsem[(qe, slot)], op.dcnt * 16)

            @block.tensor
            def _(t):
                run("pe", t)

            @block.scalar
            def _(a):
                run("act", a)

            @block.vector
            def _(v):
                run("dve", v)

            @block.gpsimd
            def _(g):
                run("pool", g)

            @block.sync
            def _(s):
                run("sp", s)


class Arena:
    def __init__(self, nc, base, limit):
        self.nc = nc
        self.off = base
        self.limit = limit
        self.n = 0

    def alloc(self, shape, dtype, at=None):
        size = mybir.dt.size(dtype)
        for s in shape[1:]:
            size *= s
        if at is None:
            at = (self.off + 63) // 64 * 64
            self.off = at + size
        assert at + size <= self.limit, ("SBUF overflow", at + size, self.limit)
        self.n += 1
        t = self.nc.alloc_sbuf_tensor_at("sb%d" % self.n, list(shape), dtype, offset=at)
        return Buf(t, "sbuf", at, at + size)


R_C = 0
R_BADA = 8
R_G1 = 56
R_G2 = 64
R_CMW = 72
R_CMB = 84
R_CNG = 88
R_CFW = 92
R_CFB = 224
R_TOT = 268
C_RNG = 0
C_FG = 512
C_BG1 = 1536
C_BG2 = 2560
C_TOT = 3584


def _host_consts():
    h = np.arange(HEADS, dtype=np.float64)
    gam = 1.0 - 2.0 ** (-5.0 - h)
    lg = np.log(gam)
    i = np.arange(128)
    ci, cj = i[:, None] // 64, i[None, :] // 64
    dist = i[:, None] - i[None, :]
    W = np.zeros((HEADS, 128, 128))
    for hh in range(HEADS):
        same = np.exp(np.abs(dist) * lg[hh])
        causal = np.exp(dist * lg[hh])
        W[hh] = np.where(ci == cj, same, np.where(ci > cj, causal, 0.0))
    qw = np.exp((i[None, :] + 1) * lg[:, None])
    maskT = np.transpose(W / qw[:, :, None], (2, 0, 1))
    qwT = np.broadcast_to((qw * 128 ** -0.5)[None], (128, HEADS, 128))
    kw = np.exp((127 - i)[:, None] * lg[None, :])
    dec = np.exp(128 * lg)
    invf = (np.float32(10000.0) ** (-(np.arange(0, 128, 2, dtype=np.float32)) / np.float32(128))).astype(np.float32)
    cst = np.zeros((128, 128 + 512 + 512 + 4 + 64), np.float32)
    cst[:, 0:128] = np.eye(128)
    cst[:, 128:640] = maskT.reshape(128, 512)
    cst[:, 640:1152] = qwT.reshape(128, 512)
    cst[:, 1152:1156] = kw
    cst[:, 1156:1220] = invf[None, :]
    return cst, [float(v) for v in dec]


N_CST = 1220


class Ring:
    def __init__(self, slots, total, issue_fn):
        self.slots = slots
        self.total = total
        self.issue_fn = issue_fn
        self.issued = 0
        self.consumed = 0

    def prefetch(self):
        lim = min(self.total, self.consumed + len(self.slots))
        while self.issued < lim:
            self.issue_fn(self.issued, self.slots[self.issued % len(self.slots)])
            self.issued += 1

    def get(self):
        self.prefetch()
        slot = self.slots[self.consumed % len(self.slots)]
        self.consumed += 1
        return slot


class _Stop(Exception):
    pass


def build_nc(dec128, debug=False, kstop=99, ntiles=NT):
    def chk(n):
        if kstop <= n:
            raise _Stop()

    nc = bass.Bass("TRN2", target_bir_lowering=False)

    def din(name, shape, dt=F32):
        return nc.dram_tensor(name, list(shape), dt, kind="ExternalInput").ap()

    x_d = din("x", [SEQ, D])
    pos_d = din("pos", [32, 128], I32)
    pv_d = din("pv", [R_TOT, 128])
    bro_d = din("bro", [128, C_TOT])
    cst_d = din("cst", [128, N_CST])
    wada_d = din("w_ada", [12, 128, 8, 512])
    winb_d = din("w_in_b", [4, 128, 8, 512])
    wina_d = din("w_in_a", [12, 128, 8, 128])
    wout_d = din("w_out", [2, 128, 8, 512])
    wup_d = din("w_up", [NFF, 128, 8, 2, 128])
    wdn_d = din("w_down", [2, 11, 128, 2, 512])
    y_d = nc.dram_tensor("y", [SEQ, D], F32, kind="ExternalOutput").ap()

    S = Sched(nc)
    A = Arena(nc, 16640, 229376)
    add = S.add

    tra = Buf(nc.alloc_psum_tensor("tra", [128, 1024], BF16), "psum", 0, 2048)
    trb = Buf(nc.alloc_psum_tensor("trb", [128, 1024], BF16), "psum", 2048, 4096)
    pb = {}
    for i in range(2, 8):
        pb[i] = Buf(nc.alloc_psum_tensor("pb%d" % i, [128, 512], F32), "psum", i * 2048, (i + 1) * 2048)

    cst = A.alloc([128, N_CST], F32)
    ident_f = cst.t[:, 0:128]
    maskT = cst.t[:, 128:640].rearrange("p (h i) -> p h i", h=4)
    qwT = cst.t[:, 640:1152].rearrange("p (h i) -> p h i", h=4)
    kwtab = cst.t[:, 1152:1156]
    invf = cst.t[:, 1156:1220]
    ident_b = A.alloc([128, 128], BF16)
    ones_b = A.alloc([128, 128], BF16)
    epsc = A.alloc([128, 4], F32)
    pvT = A.alloc([128, R_TOT + 4], F32)
    modT = A.alloc([128, 48], F32)
    ab = A.alloc([128, 32], F32)
    bro = A.alloc([128, C_TOT], F32)
    posf = A.alloc([128, 32], F32)
    x_res = A.alloc([128, 4, D], F32)
    hT = A.alloc([128, 8, T], BF16)
    xn = A.alloc([128, 4, D], BF16)
    w_inb = A.alloc([128, 4, 8, 512], BF16)
    w_outs = A.alloc([128, 2, 8, 512], BF16)
    NRA, NRU, NRD = 3, 3, 4
    ring_a = [A.alloc([128, 8, 128], BF16) for _ in range(NRA)]
    ring_u = [A.alloc([128, 8, 2, 128], BF16) for _ in range(NRU)]
    ring_d = [A.alloc([128, 2, 512], BF16) for _ in range(NRD)]
    state_f = A.alloc([128, 4, 128], F32)
    state_b = [A.alloc([128, 4, 128], BF16) for _ in range(2)]
    pbuf = A.alloc([128, 4, 514], F32)
    uhalo = A.alloc([128, NFF, 2, 2], F32)
    stat = A.alloc([128, 64], F32)
    junk = A.alloc([128, D], BF16)

    region0 = A.off
    cs_t = A.alloc([128, 4, 64], F32)
    sn_t = A.alloc([128, 4, 64], F32)
    rtmp = A.alloc([128, 4, 64], F32)
    rki = A.alloc([128, 4, 64], I32)
    qf = [A.alloc([128, 512], F32) for _ in range(2)]
    kf = [A.alloc([128, 512], F32) for _ in range(2)]
    ta = [A.alloc([128, 512], F32) for _ in range(2)]
    tb = [A.alloc([128, 512], F32) for _ in range(2)]
    q_rot = [A.alloc([128, 512], BF16) for _ in range(2)]
    k_rot = [A.alloc([128, 512], BF16) for _ in range(2)]
    v_bf = [A.alloc([128, 512], BF16) for _ in range(2)]
    v_kw = [A.alloc([128, 512], BF16) for _ in range(2)]
    sg = [A.alloc([128, 512], F32) for _ in range(2)]
    qTs = A.alloc([128, 4, 128], BF16)
    kTs = A.alloc([128, 4, 128], BF16)
    smT = A.alloc([128, 4, 128], BF16)
    o_n = A.alloc([128, 512], F32)
    ret_o = A.alloc([128, 512], BF16)
    bnst = A.alloc([128, 4, 6], F32)
    bnag = A.alloc([128, 4, 2], F32)
    catT = A.alloc([128, 8, T], BF16)
    gc_sb = A.alloc([128, 512], F32)
    cy = A.alloc([128, 4, 512], F32)
    sq = A.alloc([128, 512], BF16)
    rstd_bc = A.alloc([128, 512], F32)
    gtmp = A.alloc([128, 512], F32)
    region_m_end = A.off
    A.off = region0
    u_sb = [A.alloc([128, 2, 514], F32) for _ in range(3)]
    yab = [A.alloc([128, 2, 512], F32) for _ in range(3)]
    gT = A.alloc([128, NFF, T], BF16)
    ftmp = A.alloc([128, 512], F32)
    region_f_end = A.off
    A.off = region0
    pstage = A.alloc([128, 3, 128], F32)
    posi = A.alloc([32, 128], I32)
    posff = A.alloc([32, 128], F32)
    s_col = A.alloc([128, 8], F32)
    s_bc = A.alloc([128, 8, 128], F32)
    wada = [A.alloc([128, 8, 512], F32) for _ in range(2)]
    A.off = max(region_m_end, region_f_end, A.off)
    assert A.off <= A.limit, A.off

    def dram(name, lo=0, hi=1):
        return Buf(None, "dram_" + name, lo, hi)

    add("sp", lambda e: e.dma_start(out=cst[:, :], in_=cst_d[:, :]), writes=[cst], dma=True)
    add("sp", lambda e: e.dma_start(out=bro[:, :], in_=bro_d[:, :]), writes=[bro], dma=True)
    add("dve", lambda e: e.memset(pstage[:, :, :], 0.0), writes=[pstage])
    for g in range(3):
        r0, r1 = g * 128, min(R_TOT, (g + 1) * 128)
        add("sp", lambda e, g=g, r0=r0, r1=r1: e.dma_start(out=pstage[0:r1 - r0, g, :], in_=pv_d[r0:r1, :]),
            writes=[pstage.sub(g * 512, (g + 1) * 512)], dma=True)
    add("sp", lambda e: e.dma_start(out=posi[:, :], in_=pos_d[:, :]), writes=[posi], dma=True)
    add("dve", lambda e: e.tensor_copy(ident_b[:, :], ident_f), reads=[cst], writes=[ident_b])
    add("dve", lambda e: e.memset(ones_b[:, :], 1.0), writes=[ones_b])
    add("dve", lambda e: e.memset(epsc[:, 0:1], 1024 * EPS), writes=[epsc.sub(0, 4)])
    add("dve", lambda e: e.memset(epsc[:, 1:2], EPS), writes=[epsc.sub(4, 8)])
    add("dve", lambda e: e.memset(epsc[:, 2:3], float(np.pi / 2)), writes=[epsc.sub(8, 12)])
    add("dve", lambda e: e.memset(epsc[:, 3:4], 512 * EPS), writes=[epsc.sub(12, 16)])
    add("dve", lambda e: e.memset(state_f[:, :, :], 0.0), writes=[state_f])
    add("dve", lambda e: e.memset(state_b[0][:, :, :], 0.0), writes=[state_b[0]])
    add("dve", lambda e: e.memset(pbuf[:, :, :], 0.0), writes=[pbuf])
    add("dve", lambda e: e.memset(uhalo[:, :, :, :], 0.0), writes=[uhalo])
    for g in range(3):
        n = min(R_TOT, (g + 1) * 128) - g * 128
        add("pe", lambda e, g=g: e.matmul(pb[2].t[:, g * 128:(g + 1) * 128], pstage[:, g, :], ident_f,
                                            start=True, stop=True),
            reads=[pstage, cst], writes=[pb[2]])
    add("act", lambda e: e.copy(pvT[:, 0:R_TOT], pb[2].t[:, 0:R_TOT]), reads=[pb[2]], writes=[pvT])
    add("dve", lambda e: e.tensor_copy(posff[:, :], posi[:, :]), reads=[posi], writes=[posff])
    add("pe", lambda e: e.matmul(pb[3].t[:, 0:32], posff[:, :], ident_f[0:32, 0:32], start=True, stop=True),
        reads=[posff, cst], writes=[pb[3]])
    add("act", lambda e: e.copy(posf[:, :], pb[3].t[:, 0:32]), reads=[pb[3]], writes=[posf])
    add("act", lambda e: e.activation(out=s_col[:, :], in_=pvT[:, R_C:R_C + 8], func=AF.Silu),
        reads=[pvT], writes=[s_col])
    add("dve", lambda e: e.tensor_copy(s_bc[:, :, :], s_col[:, :].unsqueeze(2).to_broadcast([128, 8, 128])),
        reads=[s_col], writes=[s_bc])
    for g in range(12):
        wb = wada[g % 2]
        add("sp", lambda e, g=g, wb=wb: e.dma_start(out=wb[:, :, :], in_=wada_d[g]), writes=[wb], dma=True)
        if g in (4, 5, 10, 11):
            bank = pb[4 + (g % 2)]
            for k in range(8):
                add("pe", lambda e, k=k, wb=wb, bank=bank: e.matmul(bank.t[:, :], s_bc[:, k, :], wb[:, k, :],
                                                                    start=(k == 0), stop=(k == 7)),
                    reads=[s_bc, wb], writes=[bank])
            c0 = (C_BG1 if g < 6 else C_BG2) + (g % 2) * 512
            add("dve", lambda e, bank=bank, c0=c0: e.tensor_tensor(bro[:, c0:c0 + 512], bank.t[:, :],
                                                                    bro[:, c0:c0 + 512], ALU.add),
                reads=[bank, bro.sub(c0 * 4, (c0 + 512) * 4)], writes=[bro.sub(c0 * 4, (c0 + 512) * 4)])
        else:
            for m in range(4):
                col = g * 4 + m
                for k in range(8):
                    add("pe", lambda e, k=k, m=m, wb=wb, col=col: e.matmul(
                        pb[6].t[:, col:col + 1], wb[:, k, m * 128:(m + 1) * 128], s_col[:, k:k + 1],
                        start=(k == 0), stop=(k == 7)),
                        reads=[wb, s_col], writes=[pb[6].sub(col * 4, col * 4 + 4)])
    add("dve", lambda e: e.tensor_tensor(modT[:, :], pb[6].t[:, 0:48], pvT[:, R_BADA:R_BADA + 48], ALU.add),
        reads=[pb[6], pvT], writes=[modT])
    add("dve", lambda e: e.scalar_tensor_tensor(out=ab[:, 0:8], in0=modT[:, 8:16], scalar=1.0, in1=pvT[:, R_G1:R_G1 + 8],
                                                op0=ALU.add, op1=ALU.mult), reads=[modT, pvT], writes=[ab.sub(0, 32)])
    add("dve", lambda e: e.tensor_copy(ab[:, 8:16], modT[:, 0:8]), reads=[modT], writes=[ab.sub(32, 64)])
    add("dve", lambda e: e.scalar_tensor_tensor(out=ab[:, 16:24], in0=modT[:, 32:40], scalar=1.0,
                                                in1=pvT[:, R_G2:R_G2 + 8], op0=ALU.add, op1=ALU.mult),
        reads=[modT, pvT], writes=[ab.sub(64, 96)])
    add("dve", lambda e: e.tensor_copy(ab[:, 24:32], modT[:, 24:32]), reads=[modT], writes=[ab.sub(96, 128)])
    add("dve", lambda e: e.tensor_scalar(bro[:, C_FG:C_FG + 1024], bro[:, C_FG:C_FG + 1024], 32.0, None, ALU.mult),
        reads=[bro.sub(C_FG * 4, (C_FG + 1024) * 4)], writes=[bro.sub(C_FG * 4, (C_FG + 1024) * 4)])

    rng_bc = bro.t[:, C_RNG:C_RNG + 512]
    kst0 = kstop <= 0
    fg_bc = bro.t[:, C_FG:C_FG + 1024]
    gt1_bc = bro.t[:, C_BG1:C_BG1 + 1024]
    gt2_bc = bro.t[:, C_BG2:C_BG2 + 1024]

    def _issue_a(g, slot):
        i = g % 12
        cc, part = i // 3, i % 3
        add("pool", lambda e: e.dma_start(out=slot[:, :, :], in_=wina_d[part * 4 + cc]), writes=[slot], dma=True)

    def _issue_u(g, slot):
        c = g % NFF
        add("pool", lambda e: e.dma_start(out=slot[:, :, :, :], in_=wup_d[c]), writes=[slot], dma=True)

    def _issue_d(g, slot):
        i = g % 22
        hf, cg = i // 11, i % 11
        add("pool", lambda e: e.dma_start(out=slot[:, :, :], in_=wdn_d[hf, cg]), writes=[slot], dma=True)

    rg_a = Ring(ring_a, 12 * ntiles, _issue_a)
    rg_u = Ring(ring_u, NFF * ntiles, _issue_u)
    rg_d = Ring(ring_d, 22 * ntiles, _issue_d)

    def load_mixer_weights():
        for g in range(4):
            add("pool", lambda e, g=g: e.dma_start(out=w_inb[:, g, :, :], in_=winb_d[g]),
                writes=[w_inb.sub(g * 8192, (g + 1) * 8192)], dma=True)
        for hf in range(2):
            add("pool", lambda e, hf=hf: e.dma_start(out=w_outs[:, hf, :, :], in_=wout_d[hf]),
                writes=[w_outs.sub(hf * 8192, (hf + 1) * 8192)], dma=True)

    def rms_rstd32(src_buf, src_ap, col):
        add("act", lambda e: e.activation(out=junk[:, :], in_=src_ap, func=AF.Square, accum_out=stat[:, col:col + 1]),
            reads=[src_buf], writes=[junk, stat.sub(col * 4, col * 4 + 4)])
        add("act", lambda e: e.activation(out=stat[:, col:col + 1], in_=stat[:, col:col + 1], func=AF.Sqrt,
                                          bias=epsc[:, 0:1], scale=1.0),
            reads=[stat.sub(col * 4, col * 4 + 4), epsc], writes=[stat.sub(col * 4, col * 4 + 4)])
        add("dve", lambda e: e.reciprocal(stat[:, col:col + 1], stat[:, col:col + 1]),
            reads=[stat.sub(col * 4, col * 4 + 4)], writes=[stat.sub(col * 4, col * 4 + 4)])

    def norm_to_hT(aoff):
        for j in range(4):
            rms_rstd32(x_res.sub(j * 4096, (j + 1) * 4096), x_res[:, j, :], j)
            add("dve", lambda e, j=j: e.tensor_scalar(xn[:, j, :], x_res[:, j, :], stat[:, j:j + 1], 32.0,
                                                       ALU.mult, ALU.mult),
                reads=[x_res.sub(j * 4096, (j + 1) * 4096), stat.sub(j * 4, j * 4 + 4)],
                writes=[xn.sub(j * 2048, (j + 1) * 2048)])
        chk(0.6)
        for c in range(8):
            if c == 1:
                chk(0.7)
            if c == 2:
                chk(0.8)
            tr = tra if c % 2 == 0 else trb
            for j in range(4):
                add("pe", lambda e, c=c, j=j, tr=tr: e.transpose(
                    tr.t[:, j * 128:(j + 1) * 128], xn[:, j, c * 128:(c + 1) * 128], ident_b[:, :]),
                    reads=[xn.sub(j * 2048, (j + 1) * 2048), ident_b], writes=[tr])
            add("act", lambda e, c=c, tr=tr: e.activation(
                out=hT[:, c, :], in_=tr.t[:, 0:512], func=AF.Identity,
                bias=ab[:, aoff + 8 + c:aoff + 9 + c], scale=ab[:, aoff + c:aoff + c + 1]),
                reads=[tr, ab], writes=[hT.sub(c * 1024, (c + 1) * 1024)])

    def tile(tau):
        t0 = tau * T
        for j in range(4):
            add("sp", lambda e, j=j: e.dma_start(out=x_res[:, j, :], in_=x_d[t0 + j * 128:t0 + (j + 1) * 128, :]),
                writes=[x_res.sub(j * 4096, (j + 1) * 4096)], dma=True)
        if tau == 0:
            load_mixer_weights()
        rg_a.prefetch()
        rg_u.prefetch()
        rg_d.prefetch()
        chk(0.2)
        pj = posf.t[:, tau * 4:tau * 4 + 4].unsqueeze(2).to_broadcast([128, 4, 64])
        iv = invf.unsqueeze(1).to_broadcast([128, 4, 64])
        add("dve", lambda e: e.tensor_tensor(cs_t[:, :, :], pj, iv, ALU.mult), reads=[posf, cst], writes=[cs_t])
        add("dve", lambda e: e.tensor_scalar(rtmp[:, :, :], cs_t[:, :, :], float(1.0 / (2 * np.pi)), None, ALU.mult),
            reads=[cs_t], writes=[rtmp])
        add("dve", lambda e: e.tensor_copy(rki[:, :, :], rtmp[:, :, :]), reads=[rtmp], writes=[rki])
        add("dve", lambda e: e.tensor_copy(rtmp[:, :, :], rki[:, :, :]), reads=[rki], writes=[rtmp])
        add("dve", lambda e: e.scalar_tensor_tensor(out=cs_t[:, :, :], in0=rtmp[:, :, :], scalar=-6.28125,
                                                    in1=cs_t[:, :, :], op0=ALU.mult, op1=ALU.add),
            reads=[rtmp, cs_t], writes=[cs_t])
        add("dve", lambda e: e.scalar_tensor_tensor(out=cs_t[:, :, :], in0=rtmp[:, :, :],
                                                    scalar=-0.0019353071795864769, in1=cs_t[:, :, :],
                                                    op0=ALU.mult, op1=ALU.add), reads=[rtmp, cs_t], writes=[cs_t])
        add("dve", lambda e: e.tensor_scalar(cs_t[:, :, :], cs_t[:, :, :], -3.141592, 3.141592, ALU.max, ALU.min),
            reads=[cs_t], writes=[cs_t])
        add("act", lambda e: e.activation(out=sn_t[:, :, :], in_=cs_t[:, :, :], func=AF.Sin), reads=[cs_t], writes=[sn_t])
        add("act", lambda e: e.activation(out=rtmp[:, :, :], in_=cs_t[:, :, :], func=AF.Abs), reads=[cs_t], writes=[rtmp])
        add("act", lambda e: e.activation(out=cs_t[:, :, :], in_=rtmp[:, :, :], func=AF.Sin, scale=-1.0,
                                          bias=epsc[:, 2:3]), reads=[rtmp, epsc], writes=[cs_t])

        chk(0.5)
        norm_to_hT(0)
        chk(1)

        def stage_d(j):
            p = j % 2
            banks = [pb[2], pb[3], pb[4], pb[5]]
            for g in range(4):
                for k in range(8):
                    add("pe", lambda e, g=g, k=k: e.matmul(banks[g].t[:, :], hT[:, k, j * 128:(j + 1) * 128],
                                                            w_inb[:, g, k, :], start=(k == 0), stop=(k == 7)),
                        reads=[hT, w_inb.sub(g * 8192, (g + 1) * 8192)], writes=[banks[g]])
            chk(1.2)
            add("act", lambda e: e.copy(qf[p][:, :], pb[2].t[:, :]), reads=[pb[2]], writes=[qf[p]])
            add("act", lambda e: e.copy(kf[p][:, :], pb[3].t[:, :]), reads=[pb[3]], writes=[kf[p]])
            add("act", lambda e: e.copy(v_bf[p][:, :], pb[4].t[:, :]), reads=[pb[4]], writes=[v_bf[p]])
            chk(1.4)
            for h in range(4):
                add("dve", lambda e, h=h: e.tensor_scalar(v_kw[p][:, h * 128:(h + 1) * 128], pb[4].t[:, h * 128:(h + 1) * 128],
                                                           kwtab[:, h:h + 1], None, ALU.mult),
                    reads=[pb[4], cst], writes=[v_kw[p]])
            chk(1.5)
            add("act", lambda e: e.activation(out=sg[p][:, :], in_=pb[5].t[:, :], func=AF.Silu),
                reads=[pb[5]], writes=[sg[p]])
            chk(1.6)
            cosb = cs_t.t[:, j, :].unsqueeze(1).to_broadcast([128, 4, 64])
            sinb = sn_t.t[:, j, :].unsqueeze(1).to_broadcast([128, 4, 64])
            for src, dst, en, tt in ((qf[p], q_rot[p], "dve", 0), (kf[p], k_rot[p], _pe("rot"), 1)):
                s4 = src.t[:, :].rearrange("p (h s d) -> p h s d", h=4, s=2)
                a4 = ta[tt].t[:, :].rearrange("p (h s d) -> p h s d", h=4, s=2)
                b4 = tb[tt].t[:, :].rearrange("p (h s d) -> p h s d", h=4, s=2)
                d4 = dst.t[:, :].rearrange("p (h s d) -> p h s d", h=4, s=2)
                for sidx in range(2):
                    add(en, lambda e, s4=s4, a4=a4, sidx=sidx: e.tensor_tensor(a4[:, :, sidx, :], s4[:, :, sidx, :],
                                                                               cosb, ALU.mult),
                        reads=[src, cs_t], writes=[ta[tt]])
                add(en, lambda e, s4=s4, b4=b4: e.tensor_tensor(b4[:, :, 0, :], s4[:, :, 1, :], sinb, ALU.mult),
                    reads=[src, sn_t], writes=[tb[tt]])
                add(en, lambda e, s4=s4, b4=b4: e.tensor_tensor(b4[:, :, 1, :], s4[:, :, 0, :], sinb, ALU.mult),
                    reads=[src, sn_t], writes=[tb[tt]])
                add(en, lambda e, a4=a4, b4=b4, d4=d4: e.tensor_tensor(d4[:, :, 0, :], a4[:, :, 0, :], b4[:, :, 0, :],
                                                                       ALU.subtract),
                    reads=[ta[tt], tb[tt]], writes=[dst])
                add(en, lambda e, a4=a4, b4=b4, d4=d4: e.tensor_tensor(d4[:, :, 1, :], a4[:, :, 1, :], b4[:, :, 1, :],
                                                                       ALU.add),
                    reads=[ta[tt], tb[tt]], writes=[dst])

        def stage_e(j, blk):
            p = j % 2
            sb_cur = state_b[blk % 2]
            sb_nxt = state_b[(blk + 1) % 2]
            for h in range(4):
                add("pe", lambda e, h=h: e.transpose(trb.t[:, h * 128:(h + 1) * 128], q_rot[p][:, h * 128:(h + 1) * 128],
                                                      ident_b[:, :]), reads=[q_rot[p], ident_b], writes=[trb])
            for h in range(4):
                add("pe", lambda e, h=h: e.transpose(tra.t[:, h * 128:(h + 1) * 128],
                                                      k_rot[p][:, h * 128:(h + 1) * 128], ident_b[:, :]),
                    reads=[k_rot[p], ident_b], writes=[tra])
            add("dve", lambda e: e.tensor_tensor(qTs[:, :, :], trb.t[:, 0:512].rearrange("p (h i) -> p h i", h=4),
                                                 qwT, ALU.mult), reads=[trb, cst], writes=[qTs])
            add("act", lambda e: e.copy(kTs[:, :, :], tra.t[:, 0:512].rearrange("p (h i) -> p h i", h=4)),
                reads=[tra], writes=[kTs])
            for h in range(4):
                add("pe", lambda e, h=h: e.matmul(pb[6].t[:, h * 128:(h + 1) * 128], kTs[:, h, :], qTs[:, h, :],
                                                   start=True, stop=True), reads=[kTs, qTs], writes=[pb[6]])
            add("dve", lambda e: e.tensor_tensor(smT[:, :, :], pb[6].t[:, :].rearrange("p (h i) -> p h i", h=4),
                                                 maskT, ALU.mult), reads=[pb[6], cst], writes=[smT])
            for h in range(4):
                add("pe", lambda e, h=h: e.matmul(pb[7].t[:, h * 128:(h + 1) * 128], smT[:, h, :],
                                                   v_bf[p][:, h * 128:(h + 1) * 128], start=True, stop=False),
                    reads=[smT, v_bf[p]], writes=[pb[7]])
                add("pe", lambda e, h=h: e.matmul(pb[7].t[:, h * 128:(h + 1) * 128], qTs[:, h, :], sb_cur[:, h, :],
                                                   start=False, stop=True), reads=[qTs, sb_cur], writes=[pb[7]])
            for h in range(4):
                add("pe", lambda e, h=h: e.matmul(pb[6].t[:, h * 128:(h + 1) * 128], k_rot[p][:, h * 128:(h + 1) * 128],
                                                   v_kw[p][:, h * 128:(h + 1) * 128], start=True, stop=True),
                    reads=[k_rot[p], v_kw[p]], writes=[pb[6]])
            for h in range(4):
                add("dve", lambda e, h=h: e.scalar_tensor_tensor(
                    out=state_f[:, h, :], in0=state_f[:, h, :], scalar=dec128[h], in1=pb[6].t[:, h * 128:(h + 1) * 128],
                    op0=ALU.mult, op1=ALU.add), reads=[state_f, pb[6]], writes=[state_f])
            add("act", lambda e: e.copy(sb_nxt[:, :, :], state_f[:, :, :]), reads=[state_f], writes=[sb_nxt])
            for h in range(4):
                add("dve", lambda e, h=h: e.bn_stats(bnst[:, h, :], pb[7].t[:, h * 128:(h + 1) * 128]),
                    reads=[pb[7]], writes=[bnst])
            for h in range(4):
                add("dve", lambda e, h=h: e.bn_aggr(bnag[:, h, :], bnst[:, h, :]), reads=[bnst], writes=[bnag])
            add("act", lambda e: e.activation(out=stat[:, 8:12], in_=bnag[:, :, 1], func=AF.Sqrt, bias=epsc[:, 1:2],
                                              scale=1.0), reads=[bnag, epsc], writes=[stat.sub(32, 48)])
            add("dve", lambda e: e.reciprocal(stat[:, 8:12], stat[:, 8:12]), reads=[stat.sub(32, 48)],
                writes=[stat.sub(32, 48)])
            for h in range(4):
                add("dve", lambda e, h=h: e.tensor_scalar(o_n[:, h * 128:(h + 1) * 128], pb[7].t[:, h * 128:(h + 1) * 128],
                                                           bnag[:, h, 0:1], stat[:, 8 + h:9 + h], ALU.subtract, ALU.mult),
                    reads=[pb[7], bnag, stat.sub(32, 48)], writes=[o_n])
            add(_pe("sg"), lambda e: e.tensor_tensor(sg[p][:, :], sg[p][:, :], rng_bc, ALU.mult),
                reads=[sg[p], bro.sub(0, 2048)], writes=[sg[p]])
            add(_pe("sg"), lambda e: e.tensor_tensor(ret_o[:, :], o_n[:, :], sg[p][:, :], ALU.mult),
                reads=[o_n, sg[p]], writes=[ret_o])
            for h in range(4):
                add("pe", lambda e, h=h: e.transpose(tra.t[:, h * 128:(h + 1) * 128], ret_o[:, h * 128:(h + 1) * 128],
                                                      ident_b[:, :]), reads=[ret_o, ident_b], writes=[tra])
            add("act", lambda e: e.copy(catT[:, 0:4, j * 128:(j + 1) * 128],
                                        tra.t[:, 0:512].rearrange("p (h i) -> p h i", h=4)),
                reads=[tra], writes=[catT.sub(0, 4096)])

        stage_d(0)
        chk(2)
        stage_d(1)
        stage_e(0, tau * 4 + 0)
        stage_d(2)
        stage_e(1, tau * 4 + 1)
        stage_d(3)
        stage_e(2, tau * 4 + 2)
        stage_e(3, tau * 4 + 3)
        chk(3)

        def conv_chunk(cc):
            banks = [pb[2], pb[3], pb[4]]
            for part in range(3):
                slot = rg_a.get()
                for k in range(8):
                    add("pe", lambda e, part=part, k=k, slot=slot: e.matmul(banks[part].t[:, :], slot[:, k, :], hT[:, k, :],
                                                                             start=(k == 0), stop=(k == 7)),
                        reads=[slot, hT], writes=[banks[part]])
            pc = pbuf.sub(cc * 2056, (cc + 1) * 2056)
            add("act", lambda e: e.copy(gc_sb[:, :], pb[4].t[:, :]), reads=[pb[4]], writes=[gc_sb])
            add("dve", lambda e, cc=cc: e.tensor_copy(pbuf[:, cc, 0:2], pbuf[:, cc, 512:514]), reads=[pc], writes=[pc])
            add("dve", lambda e, cc=cc: e.tensor_tensor(pbuf[:, cc, 2:514], pb[2].t[:, :], gc_sb[:, :], ALU.mult),
                reads=[pb[2], gc_sb, pc], writes=[pc])
            cyc = cy.sub(cc * 2048, (cc + 1) * 2048)
            add("act", lambda e, cc=cc: e.activation(out=cy[:, cc, :], in_=pbuf[:, cc, 2:514], func=AF.Identity,
                                                     bias=pvT[:, R_CMB + cc:R_CMB + cc + 1],
                                                     scale=pvT[:, R_CMW + 8 + cc:R_CMW + 9 + cc]),
                reads=[pc, pvT], writes=[cyc])
            add("dve", lambda e, cc=cc: e.scalar_tensor_tensor(out=cy[:, cc, :], in0=pbuf[:, cc, 1:513],
                                                               scalar=pvT[:, R_CMW + 4 + cc:R_CMW + 5 + cc],
                                                               in1=cy[:, cc, :], op0=ALU.mult, op1=ALU.add),
                reads=[pc, pvT, cyc], writes=[cyc])
            add("dve", lambda e, cc=cc: e.scalar_tensor_tensor(out=cy[:, cc, :], in0=pbuf[:, cc, 0:512],
                                                               scalar=pvT[:, R_CMW + cc:R_CMW + cc + 1],
                                                               in1=cy[:, cc, :], op0=ALU.mult, op1=ALU.add),
                reads=[pc, pvT, cyc], writes=[cyc])
            add("dve", lambda e, cc=cc: e.tensor_tensor(cy[:, cc, :], pb[3].t[:, :], cy[:, cc, :], ALU.mult),
                reads=[pb[3], cyc], writes=[cyc])
            add("act", lambda e, cc=cc: e.activation(out=sq[:, :], in_=cy[:, cc, :], func=AF.Square),
                reads=[cyc], writes=[sq])
            add("pe", lambda e, cc=cc: e.matmul(pb[5].t[:, :], ones_b[:, :], sq[:, :], start=(cc == 0), stop=(cc == 3)),
                reads=[ones_b, sq], writes=[pb[5]])
        for cc in range(4):
            conv_chunk(cc)
        add("act", lambda e: e.activation(out=rstd_bc[:, :], in_=pb[5].t[:, :], func=AF.Sqrt, bias=epsc[:, 3:4], scale=1.0),
            reads=[pb[5], epsc], writes=[rstd_bc])
        add("dve", lambda e: e.reciprocal(rstd_bc[:, :], rstd_bc[:, :]), reads=[rstd_bc], writes=[rstd_bc])
        add("dve", lambda e: e.tensor_scalar(rstd_bc[:, :], rstd_bc[:, :], float(np.sqrt(512.0)), None, ALU.mult),
            reads=[rstd_bc], writes=[rstd_bc])
        for cc in range(4):
            add("dve", lambda e, cc=cc: e.scalar_tensor_tensor(out=catT[:, 4 + cc, :], in0=cy[:, cc, :],
                                                               scalar=pvT[:, R_CNG + cc:R_CNG + cc + 1],
                                                               in1=rstd_bc[:, :], op0=ALU.mult, op1=ALU.mult),
                reads=[cy.sub(cc * 2048, (cc + 1) * 2048), pvT, rstd_bc], writes=[catT.sub((4 + cc) * 1024, (5 + cc) * 1024)])

        def wout_block(j):
            for hf in range(2):
                bank = pb[6 + hf]
                for k in range(8):
                    add("pe", lambda e, k=k, hf=hf, bank=bank: e.matmul(bank.t[:, :], catT[:, k, j * 128:(j + 1) * 128],
                                                                        w_outs[:, hf, k, :], start=(k == 0), stop=(k == 7)),
                        reads=[catT, w_outs.sub(hf * 8192, (hf + 1) * 8192)], writes=[bank])
                xr = x_res.sub(j * 4096 + hf * 2048, j * 4096 + (hf + 1) * 2048)
                add("dve", lambda e, hf=hf, bank=bank: e.tensor_tensor(gtmp[:, :], bank.t[:, :],
                                                                       gt1_bc[:, hf * 512:(hf + 1) * 512], ALU.mult),
                    reads=[bank, bro], writes=[gtmp])
                add(_pe("res"), lambda e, hf=hf: e.tensor_tensor(x_res[:, j, hf * 512:(hf + 1) * 512],
                                                             x_res[:, j, hf * 512:(hf + 1) * 512], gtmp[:, :], ALU.add),
                    reads=[xr, gtmp], writes=[xr])

        chk(4)
        for j in range(4):
            wout_block(j)
        chk(5)

        norm_to_hT(16)
        chk(6)

        def up_chunk(c):
            slot = rg_u.get()
            pr = c % 2
            banks = [pb[2 + 2 * pr], pb[3 + 2 * pr]]
            for a in range(2):
                for k in range(8):
                    add("pe", lambda e, a=a, k=k, slot=slot: e.matmul(banks[a].t[:, :], slot[:, k, a, :], hT[:, k, :],
                                                                      start=(k == 0), stop=(k == 7)),
                        reads=[slot, hT], writes=[banks[a]])
            us = u_sb[c % 3]
            ys = yab[c % 3]
            uh = uhalo.sub(c * 16, (c + 1) * 16)
            add(_pe("halo"), lambda e, c=c, us=us: e.tensor_copy(us[:, :, 0:2], uhalo[:, c, :, :]), reads=[uh], writes=[us])
            for a in range(2):
                add("act", lambda e, a=a, us=us: e.copy(us[:, a, 2:514], banks[a].t[:, :]), reads=[banks[a]], writes=[us])
            add(_pe("halo"), lambda e, c=c, us=us: e.tensor_copy(uhalo[:, c, :, :], us[:, :, 512:514]), reads=[us], writes=[uh])
            for a in range(2):
                ch = a * NFF + c
                add("act", lambda e, a=a, ch=ch, us=us, ys=ys: e.activation(
                    out=ys[:, a, :], in_=us[:, a, 2:514], func=AF.Identity,
                    bias=pvT[:, R_CFB + ch:R_CFB + ch + 1], scale=pvT[:, R_CFW + 88 + ch:R_CFW + 89 + ch]),
                    reads=[us, pvT], writes=[ys])
                add("dve", lambda e, a=a, ch=ch, us=us, ys=ys: e.scalar_tensor_tensor(
                    out=ys[:, a, :], in0=us[:, a, 1:513], scalar=pvT[:, R_CFW + 44 + ch:R_CFW + 45 + ch],
                    in1=ys[:, a, :], op0=ALU.mult, op1=ALU.add), reads=[us, pvT, ys], writes=[ys])
                add("dve", lambda e, a=a, ch=ch, us=us, ys=ys: e.scalar_tensor_tensor(
                    out=ys[:, a, :], in0=us[:, a, 0:512], scalar=pvT[:, R_CFW + ch:R_CFW + ch + 1],
                    in1=ys[:, a, :], op0=ALU.mult, op1=ALU.add), reads=[us, pvT, ys], writes=[ys])
            add("act", lambda e, ys=ys: e.activation(out=ys[:, 0, :], in_=ys[:, 0, :], func=AF.Silu), reads=[ys], writes=[ys])
            add(_pe("gmul"), lambda e, c=c, ys=ys: e.tensor_tensor(gT[:, c, :], ys[:, 0, :], ys[:, 1, :], ALU.mult),
                reads=[ys], writes=[gT.sub(c * 1024, (c + 1) * 1024)])

        if tau + 1 < ntiles:
            load_mixer_weights()
        for c in range(NFF):
            up_chunk(c)
        chk(7)

        acc_banks = {0: [pb[2], pb[3], pb[4], pb[5]], 1: [pb[6], pb[7], pb[2], pb[3]]}
        for hf in range(2):
            for cg in range(11):
                slot = rg_d.get()
                for j in range(4):
                    bank = acc_banks[hf][j]
                    for cl in range(2):
                        c = cg * 2 + cl
                        add("pe", lambda e, j=j, cl=cl, c=c, slot=slot, bank=bank: e.matmul(
                            bank.t[:, :], gT[:, c, j * 128:(j + 1) * 128], slot[:, cl, :],
                            start=(c == 0), stop=(c == NFF - 1)), reads=[gT, slot], writes=[bank])
            for j in range(4):
                bank = acc_banks[hf][j]
                xr = x_res.sub(j * 4096 + hf * 2048, j * 4096 + (hf + 1) * 2048)
                add("dve", lambda e, hf=hf, bank=bank: e.tensor_tensor(ftmp[:, :], bank.t[:, :],
                                                                       gt2_bc[:, hf * 512:(hf + 1) * 512], ALU.mult),
                    reads=[bank, bro], writes=[ftmp])
                add(_pe("res"), lambda e, j=j, hf=hf: e.tensor_tensor(x_res[:, j, hf * 512:(hf + 1) * 512],
                                                                  x_res[:, j, hf * 512:(hf + 1) * 512], ftmp[:, :], ALU.add),
                    reads=[xr, ftmp], writes=[xr])

        chk(8)
        def final_block(j):
            xr = x_res.sub(j * 4096, (j + 1) * 4096)
            rms_rstd32(xr, x_res[:, j, :], 16 + j)
            add("dve", lambda e, j=j: e.scalar_tensor_tensor(out=x_res[:, j, :], in0=x_res[:, j, :],
                                                             scalar=stat[:, 16 + j:17 + j], in1=fg_bc,
                                                             op0=ALU.mult, op1=ALU.mult),
                reads=[xr, stat.sub((16 + j) * 4, (17 + j) * 4), bro], writes=[xr])
            add("sp", lambda e, j=j: e.dma_start(out=y_d[t0 + j * 128:t0 + (j + 1) * 128, :], in_=x_res[:, j, :]),
                reads=[xr], writes=[dram("y", t0 + j * 128, t0 + (j + 1) * 128)], dma=True)

        for j in range(4):
            final_block(j)

    try:
        if kst0:
            raise _Stop()
        for tau in range(ntiles):
            tile(tau)
    except _Stop:
        pass

    S.emit({"sp": ["sp"]})
    return nc


_CACHE = {}


def kernel(x, c, positions, w_ada, b_ada, norm1_g, w_in, conv_mix_w, conv_mix_b, ret_norm_g, conv_norm_g,
           w_out, norm2_g, w_up, conv_ffn_w, conv_ffn_b, w_down, final_g):
    f = lambda a: np.ascontiguousarray(np.asarray(a), dtype=np.float32)
    x, c, w_ada, b_ada, w_in, w_out, w_up, w_down = map(f, (x, c, w_ada, b_ada, w_in, w_out, w_up, w_down))
    positions = np.ascontiguousarray(np.asarray(positions), dtype=np.int32)
    cst, dec128 = _host_consts()
    if "nc" not in _CACHE:
        _CACHE["nc"] = build_nc(dec128)
    nc = _CACHE["nc"]

    def kmaj(w):
        K, N = w.shape
        return np.ascontiguousarray(w.reshape(K // 128, 128, N).transpose(1, 0, 2))

    wada_l = np.ascontiguousarray(kmaj(w_ada).reshape(128, 8, 12, 512).transpose(2, 0, 1, 3))
    win = kmaj(w_in)
    winb_l = np.ascontiguousarray(win[:, :, 0:2048].reshape(128, 8, 4, 512).transpose(2, 0, 1, 3))
    wina_l = np.ascontiguousarray(win[:, :, 2048:3584].reshape(128, 8, 12, 128).transpose(2, 0, 1, 3))
    wout_l = np.ascontiguousarray(kmaj(w_out).reshape(128, 8, 2, 512).transpose(2, 0, 1, 3))
    wup = kmaj(w_up).reshape(128, 8, 2, NFF, 128)
    wup_l = np.ascontiguousarray(wup.transpose(3, 0, 1, 2, 4))
    wdn = w_down.reshape(11, 2, 128, 2, 512)
    wdn_l = np.ascontiguousarray(wdn.transpose(3, 0, 2, 1, 4))

    pv = np.zeros((R_TOT, 128), np.float32)
    pv[R_BADA:R_BADA + 48] = b_ada.reshape(48, 128)
    pv[R_G1:R_G1 + 8] = f(norm1_g).reshape(8, 128)
    pv[R_G2:R_G2 + 8] = f(norm2_g).reshape(8, 128)
    pv[R_CMW:R_CMW + 12] = f(conv_mix_w).reshape(12, 128)
    pv[R_CMB:R_CMB + 4] = f(conv_mix_b).reshape(4, 128)
    pv[R_CNG:R_CNG + 4] = f(conv_norm_g).reshape(4, 128)
    pv[R_CFW:R_CFW + 132] = f(conv_ffn_w).reshape(132, 128)
    pv[R_CFB:R_CFB + 44] = f(conv_ffn_b).reshape(44, 128)
    brow = np.concatenate([f(ret_norm_g), f(final_g), b_ada[2048:3072], b_ada[5120:6144]])
    bro = np.ascontiguousarray(np.broadcast_to(brow[None, :], (128, C_TOT)))

    in_maps = []
    for b in range(NB):
        pvb = pv.copy()
        pvb[R_C:R_C + 8] = c[b].reshape(8, 128)
        in_maps.append({
            "x": x[b], "pos": positions[b].reshape(32, 128), "pv": pvb, "bro": bro, "cst": cst,
            "w_ada": wada_l, "w_in_b": winb_l, "w_in_a": wina_l, "w_out": wout_l, "w_up": wup_l, "w_down": wdn_l,
        })
    res = run_bass_kernel_spmd(nc, in_maps, core_ids=list(range(NB)))
    return np.stack([np.asarray(r["y"], dtype=np.float32) for r in res.results], axis=0)
```

```python
import contextlib
import numpy as np
import concourse.bass as bass
import concourse.mybir as mybir
from concourse.bass_utils import run_bass_kernel_spmd

F32 = mybir.dt.float32
BF16 = mybir.dt.bfloat16
I32 = mybir.dt.int32
AF = mybir.ActivationFunctionType
ALU = mybir.AluOpType

ENGS = ("pe", "act", "dve", "pool", "sp")
import os as _os
_KP = _os.environ.get("KPOOL", "rot,halo,gmul,res,sg").split(",")


def _pe(group):
    return "pool" if group in _KP else "dve"
DMA_POOL = 16

D = 1024
SEQ = 4096
NB = 8
T = 512
NT = SEQ // T
DFF = 2816
NFF = DFF // 128
EPS = 1e-6
HEADS = 4


class Buf:
    def __init__(self, t, space, lo, hi):
        self.t = t
        self.space = space
        self.lo = lo
        self.hi = hi

    def __getitem__(self, k):
        return self.t[k]

    def sub(self, lo, hi):
        return Buf(self.t, self.space, self.lo + lo, self.lo + hi)


class _Op:
    __slots__ = ("eng", "fn", "idx", "waits", "inc", "dma", "dsem", "dcnt", "cnt", "selfwait")

    def __init__(self, eng, fn, dma):
        self.eng = eng
        self.fn = fn
        self.dma = dma
        self.waits = {}
        self.inc = False
        self.dsem = None
        self.dcnt = 0
        self.cnt = 0
        self.selfwait = None


class Sched:
    def __init__(self, nc):
        self.nc = nc
        self.ops = {e: [] for e in ENGS}
        self.spaces = {}
        self.dma_rr = {e: 0 for e in ENGS}
        self.dma_cnt = {}
        self.dma_last = {}
        self.psum_rd = {}

    def _dep(self, op, prod):
        if prod is None or prod is op:
            return
        if prod.dma:
            key = ("d", prod.eng, prod.dsem)
            val = prod.dcnt * 16
            cur = op.waits.get(key)
            op.waits[key] = val if cur is None else max(cur, val)
            return
        if prod.eng == "pe" and op.eng == "pe":
            return
        prod.inc = True
        key = ("c", prod.eng)
        cur = op.waits.get(key)
        if cur is None or cur.idx < prod.idx:
            op.waits[key] = prod

    def add(self, eng, fn, reads=(), writes=(), dma=False):
        op = _Op(eng, fn, dma)
        op.idx = len(self.ops[eng])
        self.ops[eng].append(op)
        if dma:
            slot = self.dma_rr[eng] % DMA_POOL
            self.dma_rr[eng] += 1
            k = (eng, slot)
            self.dma_cnt[k] = self.dma_cnt.get(k, 0) + 1
            op.dsem = slot
            op.dcnt = self.dma_cnt[k]
            prev = self.dma_last.get(k)
            if prev is not None:
                op.selfwait = (slot, prev.dcnt * 16)
            self.dma_last[k] = op
        for b in reads:
            if b.space == "psum":
                for bank in range(b.lo // 2048, (b.hi - 1) // 2048 + 1):
                    rd = self.psum_rd.setdefault(bank, {})
                    for oe, oop in rd.items():
                        if oe != eng:
                            self._dep(op, oop)
                    rd[eng] = op
            recs = self.spaces.setdefault(b.space, [])
            for r in recs:
                if r[0] < b.hi and b.lo < r[1]:
                    self._dep(op, r[2])
                    if dma:
                        r[4].append(op)
                    else:
                        r[3][eng] = op
        for b in writes:
            recs = self.spaces.setdefault(b.space, [])
            keep = []
            for r in recs:
                if r[0] < b.hi and b.lo < r[1]:
                    self._dep(op, r[2])
                    for rd in r[3].values():
                        self._dep(op, rd)
                    for rd in r[4]:
                        self._dep(op, rd)
                    if r[0] < b.lo:
                        keep.append([r[0], b.lo, r[2], dict(r[3]), list(r[4])])
                    if b.hi < r[1]:
                        keep.append([b.hi, r[1], r[2], dict(r[3]), list(r[4])])
                else:
                    keep.append(r)
            keep.append([b.lo, b.hi, op, {}, []])
            self.spaces[b.space] = keep
        return op

    def emit(self, final_waits):
        nc = self.nc
        for e in ENGS:
            c = 0
            for op in self.ops[e]:
                if op.inc and not op.dma:
                    c += 1
                op.cnt = c
        with contextlib.ExitStack() as st:
            csem = {e: st.enter_context(nc.semaphore("c_" + e)) for e in ENGS}
            dsem = {}
            for e in ENGS:
                for s in range(min(DMA_POOL, self.dma_rr[e])):
                    dsem[(e, s)] = st.enter_context(nc.semaphore("d_%s_%d" % (e, s)))
            block = st.enter_context(nc.Block())

            def run(e, engine):
                waited = {}
                for op in self.ops[e]:
                    for key, val in op.waits.items():
                        if key[0] == "d":
                            sem = dsem[(key[1], key[2])]
                            v = val
                        else:
                            sem = csem[key[1]]
                            v = val.cnt
                        if waited.get(key, 0) >= v:
                            continue
                        waited[key] = v
                        engine.wait_ge(sem, v)
                    if op.dma:
                        if op.selfwait is not None:
                            key = ("d", e, op.selfwait[0])
                            if waited.get(key, 0) < op.selfwait[1]:
                                waited[key] = op.selfwait[1]
                                engine.wait_ge(dsem[(e, op.selfwait[0])], op.selfwait[1])
                        op.fn(engine).then_inc(dsem[(e, op.dsem)], 16)
                    else:
                        ins = op.fn(engine)
                        if op.inc:
                            ins.then_inc(csem[e], 1)
                if e in final_waits:
                    for (qe, slot), op in self.dma_last.items():
                        if qe in final_waits[e]:
                            engine.wait_ge(dsem[(qe, slot)], op.dcnt * 16)

            @block.tensor
            def _(t):
                run("pe", t)

            @block.scalar
            def _(a):
                run("act", a)

            @block.vector
            def _(v):
                run("dve", v)

            @block.gpsimd
            def _(g):
                run("pool", g)

            @block.sync
            def _(s):
                run("sp", s)


class Arena:
    def __init__(self, nc, base, limit):
        self.nc = nc
        self.off = base
        self.limit = limit
        self.n = 0

    def alloc(self, shape, dtype, at=None):
        size = mybir.dt.size(dtype)
        for s in shape[1:]:
            size *= s
        if at is None:
            at = (self.off + 63) // 64 * 64
            self.off = at + size
        assert at + size <= self.limit, ("SBUF overflow", at + size, self.limit)
        self.n += 1
        t = self.nc.alloc_sbuf_tensor_at("sb%d" % self.n, list(shape), dtype, offset=at)
        return Buf(t, "sbuf", at, at + size)


R_C = 0
R_BADA = 8
R_G1 = 56
R_G2 = 64
R_CMW = 72
R_CMB = 84
R_CNG = 88
R_CFW = 92
R_CFB = 224
R_TOT = 268
C_RNG = 0
C_FG = 512
C_BG1 = 1536
C_BG2 = 2560
C_TOT = 3584


def _host_consts():
    h = np.arange(HEADS, dtype=np.float64)
    gam = 1.0 - 2.0 ** (-5.0 - h)
    lg = np.log(gam)
    i = np.arange(128)
    ci, cj = i[:, None] // 64, i[None, :] // 64
    dist = i[:, None] - i[None, :]
    W = np.zeros((HEADS, 128, 128))
    for hh in range(HEADS):
        same = np.exp(np.abs(dist) * lg[hh])
        causal = np.exp(dist * lg[hh])
        W[hh] = np.where(ci == cj, same, np.where(ci > cj, causal, 0.0))
    qw = np.exp((i[None, :] + 1) * lg[:, None])
    maskT = np.transpose(W / qw[:, :, None], (2, 0, 1))
    qwT = np.broadcast_to((qw * 128 ** -0.5)[None], (128, HEADS, 128))
    kw = np.exp((127 - i)[:, None] * lg[None, :])
    dec = np.exp(128 * lg)
    invf = (np.float32(10000.0) ** (-(np.arange(0, 128, 2, dtype=np.float32)) / np.float32(128))).astype(np.float32)
    cst = np.zeros((128, 128 + 512 + 512 + 4 + 64), np.float32)
    cst[:, 0:128] = np.eye(128)
    cst[:, 128:640] = maskT.reshape(128, 512)
    cst[:, 640:1152] = qwT.reshape(128, 512)
    cst[:, 1152:1156] = kw
    cst[:, 1156:1220] = invf[None, :]
    return cst, [float(v) for v in dec]


N_CST = 1220


class Ring:
    def __init__(self, slots, total, issue_fn):
        self.slots = slots
        self.total = total
        self.issue_fn = issue_fn
        self.issued = 0
        self.consumed = 0

    def prefetch(self):
        lim = min(self.total, self.consumed + len(self.slots))
        while self.issued < lim:
            self.issue_fn(self.issued, self.slots[self.issued % len(self.slots)])
            self.issued += 1

    def get(self):
        self.prefetch()
        slot = self.slots[self.consumed % len(self.slots)]
        self.consumed += 1
        return slot


class _Stop(Exception):
    pass


def build_nc(dec128, debug=False, kstop=99, ntiles=NT):
    def chk(n):
        if kstop <= n:
            raise _Stop()

    nc = bass.Bass("TRN2", target_bir_lowering=False)

    def din(name, shape, dt=F32):
        return nc.dram_tensor(name, list(shape), dt, kind="ExternalInput").ap()

    x_d = din("x", [SEQ, D])
    pos_d = din("pos", [32, 128], I32)
    pv_d = din("pv", [R_TOT, 128])
    bro_d = din("bro", [128, C_TOT])
    cst_d = din("cst", [128, N_CST])
    wada_d = din("w_ada", [12, 128, 8, 512])
    winb_d = din("w_in_b", [4, 128, 8, 512])
    wina_d = din("w_in_a", [12, 128, 8, 128])
    wout_d = din("w_out", [2, 128, 8, 512])
    wup_d = din("w_up", [NFF, 128, 8, 2, 128])
    wdn_d = din("w_down", [2, 11, 128, 2, 512])
    y_d = nc.dram_tensor("y", [SEQ, D], F32, kind="ExternalOutput").ap()

    S = Sched(nc)
    A = Arena(nc, 16640, 229376)
    add = S.add

    tra = Buf(nc.alloc_psum_tensor("tra", [128, 1024], BF16), "psum", 0, 2048)
    trb = Buf(nc.alloc_psum_tensor("trb", [128, 1024], BF16), "psum", 2048, 4096)
    pb = {}
    for i in range(2, 8):
        pb[i] = Buf(nc.alloc_psum_tensor("pb%d" % i, [128, 512], F32), "psum", i * 2048, (i + 1) * 2048)

    cst = A.alloc([128, N_CST], F32)
    ident_f = cst.t[:, 0:128]
    maskT = cst.t[:, 128:640].rearrange("p (h i) -> p h i", h=4)
    qwT = cst.t[:, 640:1152].rearrange("p (h i) -> p h i", h=4)
    kwtab = cst.t[:, 1152:1156]
    invf = cst.t[:, 1156:1220]
    ident_b = A.alloc([128, 128], BF16)
    ones_b = A.alloc([128, 128], BF16)
    epsc = A.alloc([128, 4], F32)
    pvT = A.alloc([128, R_TOT + 4], F32)
    modT = A.alloc([128, 48], F32)
    ab = A.alloc([128, 32], F32)
    bro = A.alloc([128, C_TOT], F32)
    posf = A.alloc([128, 32], F32)
    x_res = A.alloc([128, 4, D], F32)
    hT = A.alloc([128, 8, T], BF16)
    xn = A.alloc([128, 4, D], BF16)
    w_inb = A.alloc([128, 4, 8, 512], BF16)
    w_outs = A.alloc([128, 2, 8, 512], BF16)
    NRA, NRU, NRD = 3, 3, 4
    ring_a = [A.alloc([128, 8, 128], BF16) for _ in range(NRA)]
    ring_u = [A.alloc([128, 8, 2, 128], BF16) for _ in range(NRU)]
    ring_d = [A.alloc([128, 2, 512], BF16) for _ in range(NRD)]
    state_f = A.alloc([128, 4, 128], F32)
    state_b = [A.alloc([128, 4, 128], BF16) for _ in range(2)]
    pbuf = A.alloc([128, 4, 514], F32)
    uhalo = A.alloc([128, NFF, 2, 2], F32)
    stat = A.alloc([128, 64], F32)
    junk = A.alloc([128, D], BF16)

    region0 = A.off
    cs_t = A.alloc([128, 4, 64], F32)
    sn_t = A.alloc([128, 4, 64], F32)
    rtmp = A.alloc([128, 4, 64], F32)
    rki = A.alloc([128, 4, 64], I32)
    qf = [A.alloc([128, 512], F32) for _ in range(2)]
    kf = [A.alloc([128, 512], F32) for _ in range(2)]
    ta = [A.alloc([128, 512], F32) for _ in range(2)]
    tb = [A.alloc([128, 512], F32) for _ in range(2)]
    q_rot = [A.alloc([128, 512], BF16) for _ in range(2)]
    k_rot = [A.alloc([128, 512], BF16) for _ in range(2)]
    v_bf = [A.alloc([128, 512], BF16) for _ in range(2)]
    v_kw = [A.alloc([128, 512], BF16) for _ in range(2)]
    sg = [A.alloc([128, 512], F32) for _ in range(2)]
    qTs = A.alloc([128, 4, 128], BF16)
    kTs = A.alloc([128, 4, 128], BF16)
    smT = A.alloc([128, 4, 128], BF16)
    o_n = A.alloc([128, 512], F32)
    ret_o = A.alloc([128, 512], BF16)
    bnst = A.alloc([128, 4, 6], F32)
    bnag = A.alloc([128, 4, 2], F32)
    catT = A.alloc([128, 8, T], BF16)
    gc_sb = A.alloc([128, 512], F32)
    cy = A.alloc([128, 4, 512], F32)
    sq = A.alloc([128, 512], BF16)
    rstd_bc = A.alloc([128, 512], F32)
    gtmp = A.alloc([128, 512], F32)
    region_m_end = A.off
    A.off = region0
    u_sb = [A.alloc([128, 2, 514], F32) for _ in range(3)]
    yab = [A.alloc([128, 2, 512], F32) for _ in range(3)]
    gT = A.alloc([128, NFF, T], BF16)
    ftmp = A.alloc([128, 512], F32)
    region_f_end = A.off
    A.off = region0
    pstage = A.alloc([128, 3, 128], F32)
    posi = A.alloc([32, 128], I32)
    posff = A.alloc([32, 128], F32)
    s_col = A.alloc([128, 8], F32)
    s_bc = A.alloc([128, 8, 128], F32)
    wada = [A.alloc([128, 8, 512], F32) for _ in range(2)]
    A.off = max(region_m_end, region_f_end, A.off)
    assert A.off <= A.limit, A.off

    def dram(name, lo=0, hi=1):
        return Buf(None, "dram_" + name, lo, hi)

    add("sp", lambda e: e.dma_start(out=cst[:, :], in_=cst_d[:, :]), writes=[cst], dma=True)
    add("sp", lambda e: e.dma_start(out=bro[:, :], in_=bro_d[:, :]), writes=[bro], dma=True)
    add("dve", lambda e: e.memset(pstage[:, :, :], 0.0), writes=[pstage])
    for g in range(3):
        r0, r1 = g * 128, min(R_TOT, (g + 1) * 128)
        add("sp", lambda e, g=g, r0=r0, r1=r1: e.dma_start(out=pstage[0:r1 - r0, g, :], in_=pv_d[r0:r1, :]),
            writes=[pstage.sub(g * 512, (g + 1) * 512)], dma=True)
    add("sp", lambda e: e.dma_start(out=posi[:, :], in_=pos_d[:, :]), writes=[posi], dma=True)
    add("dve", lambda e: e.tensor_copy(ident_b[:, :], ident_f), reads=[cst], writes=[ident_b])
    add("dve", lambda e: e.memset(ones_b[:, :], 1.0), writes=[ones_b])
    add("dve", lambda e: e.memset(epsc[:, 0:1], 1024 * EPS), writes=[epsc.sub(0, 4)])
    add("dve", lambda e: e.memset(epsc[:, 1:2], EPS), writes=[epsc.sub(4, 8)])
    add("dve", lambda e: e.memset(epsc[:, 2:3], float(np.pi / 2)), writes=[epsc.sub(8, 12)])
    add("dve", lambda e: e.memset(epsc[:, 3:4], 512 * EPS), writes=[epsc.sub(12, 16)])
    add("dve", lambda e: e.memset(state_f[:, :, :], 0.0), writes=[state_f])
    add("dve", lambda e: e.memset(state_b[0][:, :, :], 0.0), writes=[state_b[0]])
    add("dve", lambda e: e.memset(pbuf[:, :, :], 0.0), writes=[pbuf])
    add("dve", lambda e: e.memset(uhalo[:, :, :, :], 0.0), writes=[uhalo])
    for g in range(3):
        n = min(R_TOT, (g + 1) * 128) - g * 128
        add("pe", lambda e, g=g: e.matmul(pb[2].t[:, g * 128:(g + 1) * 128], pstage[:, g, :], ident_f,
                                            start=True, stop=True),
            reads=[pstage, cst], writes=[pb[2]])
    add("act", lambda e: e.copy(pvT[:, 0:R_TOT], pb[2].t[:, 0:R_TOT]), reads=[pb[2]], writes=[pvT])
    add("dve", lambda e: e.tensor_copy(posff[:, :], posi[:, :]), reads=[posi], writes=[posff])
    add("pe", lambda e: e.matmul(pb[3].t[:, 0:32], posff[:, :], ident_f[0:32, 0:32], start=True, stop=True),
        reads=[posff, cst], writes=[pb[3]])
    add("act", lambda e: e.copy(posf[:, :], pb[3].t[:, 0:32]), reads=[pb[3]], writes=[posf])
    add("act", lambda e: e.activation(out=s_col[:, :], in_=pvT[:, R_C:R_C + 8], func=AF.Silu),
        reads=[pvT], writes=[s_col])
    add("dve", lambda e: e.tensor_copy(s_bc[:, :, :], s_col[:, :].unsqueeze(2).to_broadcast([128, 8, 128])),
        reads=[s_col], writes=[s_bc])
    for g in range(12):
        wb = wada[g % 2]
        add("sp", lambda e, g=g, wb=wb: e.dma_start(out=wb[:, :, :], in_=wada_d[g]), writes=[wb], dma=True)
        if g in (4, 5, 10, 11):
            bank = pb[4 + (g % 2)]
            for k in range(8):
                add("pe", lambda e, k=k, wb=wb, bank=bank: e.matmul(bank.t[:, :], s_bc[:, k, :], wb[:, k, :],
                                                                    start=(k == 0), stop=(k == 7)),
                    reads=[s_bc, wb], writes=[bank])
            c0 = (C_BG1 if g < 6 else C_BG2) + (g % 2) * 512
            add("dve", lambda e, bank=bank, c0=c0: e.tensor_tensor(bro[:, c0:c0 + 512], bank.t[:, :],
                                                                    bro[:, c0:c0 + 512], ALU.add),
                reads=[bank, bro.sub(c0 * 4, (c0 + 512) * 4)], writes=[bro.sub(c0 * 4, (c0 + 512) * 4)])
        else:
            for m in range(4):
                col = g * 4 + m
                for k in range(8):
                    add("pe", lambda e, k=k, m=m, wb=wb, col=col: e.matmul(
                        pb[6].t[:, col:col + 1], wb[:, k, m * 128:(m + 1) * 128], s_col[:, k:k + 1],
                        start=(k == 0), stop=(k == 7)),
                        reads=[wb, s_col], writes=[pb[6].sub(col * 4, col * 4 + 4)])
    add("dve", lambda e: e.tensor_tensor(modT[:, :], pb[6].t[:, 0:48], pvT[:, R_BADA:R_BADA + 48], ALU.add),
        reads=[pb[6], pvT], writes=[modT])
    add("dve", lambda e: e.scalar_tensor_tensor(out=ab[:, 0:8], in0=modT[:, 8:16], scalar=1.0, in1=pvT[:, R_G1:R_G1 + 8],
                                                op0=ALU.add, op1=ALU.mult), reads=[modT, pvT], writes=[ab.sub(0, 32)])
    add("dve", lambda e: e.tensor_copy(ab[:, 8:16], modT[:, 0:8]), reads=[modT], writes=[ab.sub(32, 64)])
    add("dve", lambda e: e.scalar_tensor_tensor(out=ab[:, 16:24], in0=modT[:, 32:40], scalar=1.0,
                                                in1=pvT[:, R_G2:R_G2 + 8], op0=ALU.add, op1=ALU.mult),
        reads=[modT, pvT], writes=[ab.sub(64, 96)])
    add("dve", lambda e: e.tensor_copy(ab[:, 24:32], modT[:, 24:32]), reads=[modT], writes=[ab.sub(96, 128)])
    add("dve", lambda e: e.tensor_scalar(bro[:, C_FG:C_FG + 1024], bro[:, C_FG:C_FG + 1024], 32.0, None, ALU.mult),
        reads=[bro.sub(C_FG * 4, (C_FG + 1024) * 4)], writes=[bro.sub(C_FG * 4, (C_FG + 1024) * 4)])

    rng_bc = bro.t[:, C_RNG:C_RNG + 512]
    kst0 = kstop <= 0
    fg_bc = bro.t[:, C_FG:C_FG + 1024]
    gt1_bc = bro.t[:, C_BG1:C_BG1 + 1024]
    gt2_bc = bro.t[:, C_BG2:C_BG2 + 1024]

    sc_inb = nc.dram_tensor("sc_inb", [4, 128, 8, 512], BF16).ap()
    sc_ina = nc.dram_tensor("sc_ina", [12, 128, 8, 128], BF16).ap()
    sc_out = nc.dram_tensor("sc_out", [2, 128, 8, 512], BF16).ap()
    sc_up = nc.dram_tensor("sc_up", [NFF, 128, 8, 2, 128], BF16).ap()
    sc_dn = nc.dram_tensor("sc_dn", [2, 11, 128, 2, 512], BF16).ap()

    def stream(first, name, idx, dst_fn, src_f32, src_bf, wbuf):
        scr = Buf(None, "dram_" + name, idx, idx + 1)
        if first:
            add("pool", lambda e: e.dma_start(out=dst_fn(), in_=src_f32), writes=[wbuf], dma=True)
            add("sp", lambda e: e.dma_start(out=src_bf, in_=dst_fn()), reads=[wbuf], writes=[scr], dma=True)
        else:
            add("sp", lambda e: e.dma_start(out=dst_fn(), in_=src_bf), reads=[scr], writes=[wbuf], dma=True)

    def _issue_a(g, slot):
        i = g % 12
        cc, part = i // 3, i % 3
        k = part * 4 + cc
        stream(g < 12, "ina", k, lambda: slot[:, :, :], wina_d[k], sc_ina[k], slot)

    def _issue_u(g, slot):
        c = g % NFF
        stream(g < NFF, "up", c, lambda: slot[:, :, :, :], wup_d[c], sc_up[c], slot)

    def _issue_d(g, slot):
        i = g % 22
        hf, cg = i // 11, i % 11
        stream(g < 22, "dn", i, lambda: slot[:, :, :], wdn_d[hf, cg], sc_dn[hf, cg], slot)

    rg_a = Ring(ring_a, 12 * ntiles, _issue_a)
    rg_u = Ring(ring_u, NFF * ntiles, _issue_u)
    rg_d = Ring(ring_d, 22 * ntiles, _issue_d)

    def load_mixer_weights(first):
        for g in range(4):
            stream(first, "inb", g, lambda g=g: w_inb[:, g, :, :], winb_d[g], sc_inb[g], w_inb.sub(g * 8192, (g + 1) * 8192))
        for hf in range(2):
            stream(first, "out", hf, lambda hf=hf: w_outs[:, hf, :, :], wout_d[hf], sc_out[hf],
                   w_outs.sub(hf * 8192, (hf + 1) * 8192))

    def rms_rstd32(src_buf, src_ap, col):
        add("act", lambda e: e.activation(out=junk[:, :], in_=src_ap, func=AF.Square, accum_out=stat[:, col:col + 1]),
            reads=[src_buf], writes=[junk, stat.sub(col * 4, col * 4 + 4)])
        add("act", lambda e: e.activation(out=stat[:, col:col + 1], in_=stat[:, col:col + 1], func=AF.Sqrt,
                                          bias=epsc[:, 0:1], scale=1.0),
            reads=[stat.sub(col * 4, col * 4 + 4), epsc], writes=[stat.sub(col * 4, col * 4 + 4)])
        add("dve", lambda e: e.reciprocal(stat[:, col:col + 1], stat[:, col:col + 1]),
            reads=[stat.sub(col * 4, col * 4 + 4)], writes=[stat.sub(col * 4, col * 4 + 4)])

    def norm_to_hT(aoff):
        for j in range(4):
            rms_rstd32(x_res.sub(j * 4096, (j + 1) * 4096), x_res[:, j, :], j)
            add("dve", lambda e, j=j: e.tensor_scalar(xn[:, j, :], x_res[:, j, :], stat[:, j:j + 1], 32.0,
                                                       ALU.mult, ALU.mult),
                reads=[x_res.sub(j * 4096, (j + 1) * 4096), stat.sub(j * 4, j * 4 + 4)],
                writes=[xn.sub(j * 2048, (j + 1) * 2048)])
        chk(0.6)
        for c in range(8):
            if c == 1:
                chk(0.7)
            if c == 2:
                chk(0.8)
            tr = tra if c % 2 == 0 else trb
            for j in range(4):
                add("pe", lambda e, c=c, j=j, tr=tr: e.transpose(
                    tr.t[:, j * 128:(j + 1) * 128], xn[:, j, c * 128:(c + 1) * 128], ident_b[:, :]),
                    reads=[xn.sub(j * 2048, (j + 1) * 2048), ident_b], writes=[tr])
            add("act", lambda e, c=c, tr=tr: e.activation(
                out=hT[:, c, :], in_=tr.t[:, 0:512], func=AF.Identity,
                bias=ab[:, aoff + 8 + c:aoff + 9 + c], scale=ab[:, aoff + c:aoff + c + 1]),
                reads=[tr, ab], writes=[hT.sub(c * 1024, (c + 1) * 1024)])

    def tile(tau):
        t0 = tau * T
        for j in range(4):
            add("sp", lambda e, j=j: e.dma_start(out=x_res[:, j, :], in_=x_d[t0 + j * 128:t0 + (j + 1) * 128, :]),
                writes=[x_res.sub(j * 4096, (j + 1) * 4096)], dma=True)
        if tau == 0:
            load_mixer_weights(True)
        rg_a.prefetch()
        rg_u.prefetch()
        rg_d.prefetch()
        chk(0.2)
        pj = posf.t[:, tau * 4:tau * 4 + 4].unsqueeze(2).to_broadcast([128, 4, 64])
        iv = invf.unsqueeze(1).to_broadcast([128, 4, 64])
        add("dve", lambda e: e.tensor_tensor(cs_t[:, :, :], pj, iv, ALU.mult), reads=[posf, cst], writes=[cs_t])
        add("dve", lambda e: e.tensor_scalar(rtmp[:, :, :], cs_t[:, :, :], float(1.0 / (2 * np.pi)), None, ALU.mult),
            reads=[cs_t], writes=[rtmp])
        add("dve", lambda e: e.tensor_copy(rki[:, :, :], rtmp[:, :, :]), reads=[rtmp], writes=[rki])
        add("dve", lambda e: e.tensor_copy(rtmp[:, :, :], rki[:, :, :]), reads=[rki], writes=[rtmp])
        add("dve", lambda e: e.scalar_tensor_tensor(out=cs_t[:, :, :], in0=rtmp[:, :, :], scalar=-6.28125,
                                                    in1=cs_t[:, :, :], op0=ALU.mult, op1=ALU.add),
            reads=[rtmp, cs_t], writes=[cs_t])
        add("dve", lambda e: e.scalar_tensor_tensor(out=cs_t[:, :, :], in0=rtmp[:, :, :],
                                                    scalar=-0.0019353071795864769, in1=cs_t[:, :, :],
                                                    op0=ALU.mult, op1=ALU.add), reads=[rtmp, cs_t], writes=[cs_t])
        add("dve", lambda e: e.tensor_scalar(cs_t[:, :, :], cs_t[:, :, :], -3.141592, 3.141592, ALU.max, ALU.min),
            reads=[cs_t], writes=[cs_t])
        add("act", lambda e: e.activation(out=sn_t[:, :, :], in_=cs_t[:, :, :], func=AF.Sin), reads=[cs_t], writes=[sn_t])
        add("act", lambda e: e.activation(out=rtmp[:, :, :], in_=cs_t[:, :, :], func=AF.Abs), reads=[cs_t], writes=[rtmp])
        add("act", lambda e: e.activation(out=cs_t[:, :, :], in_=rtmp[:, :, :], func=AF.Sin, scale=-1.0,
                                          bias=epsc[:, 2:3]), reads=[rtmp, epsc], writes=[cs_t])

        chk(0.5)
        norm_to_hT(0)
        chk(1)

        def stage_d(j):
            p = j % 2
            banks = [pb[2], pb[3], pb[4], pb[5]]
            for g in range(4):
                for k in range(8):
                    add("pe", lambda e, g=g, k=k: e.matmul(banks[g].t[:, :], hT[:, k, j * 128:(j + 1) * 128],
                                                            w_inb[:, g, k, :], start=(k == 0), stop=(k == 7)),
                        reads=[hT, w_inb.sub(g * 8192, (g + 1) * 8192)], writes=[banks[g]])
            chk(1.2)
            add("act", lambda e: e.copy(qf[p][:, :], pb[2].t[:, :]), reads=[pb[2]], writes=[qf[p]])
            add("act", lambda e: e.copy(kf[p][:, :], pb[3].t[:, :]), reads=[pb[3]], writes=[kf[p]])
            add("act", lambda e: e.copy(v_bf[p][:, :], pb[4].t[:, :]), reads=[pb[4]], writes=[v_bf[p]])
            chk(1.4)
            for h in range(4):
                add("dve", lambda e, h=h: e.tensor_scalar(v_kw[p][:, h * 128:(h + 1) * 128], pb[4].t[:, h * 128:(h + 1) * 128],
                                                           kwtab[:, h:h + 1], None, ALU.mult),
                    reads=[pb[4], cst], writes=[v_kw[p]])
            chk(1.5)
            add("act", lambda e: e.activation(out=sg[p][:, :], in_=pb[5].t[:, :], func=AF.Silu),
                reads=[pb[5]], writes=[sg[p]])
            chk(1.6)
            cosb = cs_t.t[:, j, :].unsqueeze(1).to_broadcast([128, 4, 64])
            sinb = sn_t.t[:, j, :].unsqueeze(1).to_broadcast([128, 4, 64])
            for src, dst, en, tt in ((qf[p], q_rot[p], "dve", 0), (kf[p], k_rot[p], _pe("rot"), 1)):
                s4 = src.t[:, :].rearrange("p (h s d) -> p h s d", h=4, s=2)
                a4 = ta[tt].t[:, :].rearrange("p (h s d) -> p h s d", h=4, s=2)
                b4 = tb[tt].t[:, :].rearrange("p (h s d) -> p h s d", h=4, s=2)
                d4 = dst.t[:, :].rearrange("p (h s d) -> p h s d", h=4, s=2)
                for sidx in range(2):
                    add(en, lambda e, s4=s4, a4=a4, sidx=sidx: e.tensor_tensor(a4[:, :, sidx, :], s4[:, :, sidx, :],
                                                                               cosb, ALU.mult),
                        reads=[src, cs_t], writes=[ta[tt]])
                add(en, lambda e, s4=s4, b4=b4: e.tensor_tensor(b4[:, :, 0, :], s4[:, :, 1, :], sinb, ALU.mult),
                    reads=[src, sn_t], writes=[tb[tt]])
                add(en, lambda e, s4=s4, b4=b4: e.tensor_tensor(b4[:, :, 1, :], s4[:, :, 0, :], sinb, ALU.mult),
                    reads=[src, sn_t], writes=[tb[tt]])
                add(en, lambda e, a4=a4, b4=b4, d4=d4: e.tensor_tensor(d4[:, :, 0, :], a4[:, :, 0, :], b4[:, :, 0, :],
                                                                       ALU.subtract),
                    reads=[ta[tt], tb[tt]], writes=[dst])
                add(en, lambda e, a4=a4, b4=b4, d4=d4: e.tensor_tensor(d4[:, :, 1, :], a4[:, :, 1, :], b4[:, :, 1, :],
                                                                       ALU.add),
                    reads=[ta[tt], tb[tt]], writes=[dst])

        def stage_e(j, blk):
            p = j % 2
            sb_cur = state_b[blk % 2]
            sb_nxt = state_b[(blk + 1) % 2]
            for h in range(4):
                add("pe", lambda e, h=h: e.transpose(trb.t[:, h * 128:(h + 1) * 128], q_rot[p][:, h * 128:(h + 1) * 128],
                                                      ident_b[:, :]), reads=[q_rot[p], ident_b], writes=[trb])
            for h in range(4):
                add("pe", lambda e, h=h: e.transpose(tra.t[:, h * 128:(h + 1) * 128],
                                                      k_rot[p][:, h * 128:(h + 1) * 128], ident_b[:, :]),
                    reads=[k_rot[p], ident_b], writes=[tra])
            add("dve", lambda e: e.tensor_tensor(qTs[:, :, :], trb.t[:, 0:512].rearrange("p (h i) -> p h i", h=4),
                                                 qwT, ALU.mult), reads=[trb, cst], writes=[qTs])
            add("act", lambda e: e.copy(kTs[:, :, :], tra.t[:, 0:512].rearrange("p (h i) -> p h i", h=4)),
                reads=[tra], writes=[kTs])
            for h in range(4):
                add("pe", lambda e, h=h: e.matmul(pb[6].t[:, h * 128:(h + 1) * 128], kTs[:, h, :], qTs[:, h, :],
                                                   start=True, stop=True), reads=[kTs, qTs], writes=[pb[6]])
            add("dve", lambda e: e.tensor_tensor(smT[:, :, :], pb[6].t[:, :].rearrange("p (h i) -> p h i", h=4),
                                                 maskT, ALU.mult), reads=[pb[6], cst], writes=[smT])
            for h in range(4):
                add("pe", lambda e, h=h: e.matmul(pb[7].t[:, h * 128:(h + 1) * 128], smT[:, h, :],
                                                   v_bf[p][:, h * 128:(h + 1) * 128], start=True, stop=False),
                    reads=[smT, v_bf[p]], writes=[pb[7]])
                add("pe", lambda e, h=h: e.matmul(pb[7].t[:, h * 128:(h + 1) * 128], qTs[:, h, :], sb_cur[:, h, :],
                                                   start=False, stop=True), reads=[qTs, sb_cur], writes=[pb[7]])
            for h in range(4):
                add("pe", lambda e, h=h: e.matmul(pb[6].t[:, h * 128:(h + 1) * 128], k_rot[p][:, h * 128:(h + 1) * 128],
                                                   v_kw[p][:, h * 128:(h + 1) * 128], start=True, stop=True),
                    reads=[k_rot[p], v_kw[p]], writes=[pb[6]])
            for h in range(4):
                add("dve", lambda e, h=h: e.scalar_tensor_tensor(
                    out=state_f[:, h, :], in0=state_f[:, h, :], scalar=dec128[h], in1=pb[6].t[:, h * 128:(h + 1) * 128],
                    op0=ALU.mult, op1=ALU.add), reads=[state_f, pb[6]], writes=[state_f])
            add("act", lambda e: e.copy(sb_nxt[:, :, :], state_f[:, :, :]), reads=[state_f], writes=[sb_nxt])
            for h in range(4):
                add("dve", lambda e, h=h: e.bn_stats(bnst[:, h, :], pb[7].t[:, h * 128:(h + 1) * 128]),
                    reads=[pb[7]], writes=[bnst])
            for h in range(4):
                add("dve", lambda e, h=h: e.bn_aggr(bnag[:, h, :], bnst[:, h, :]), reads=[bnst], writes=[bnag])
            add("act", lambda e: e.activation(out=stat[:, 8:12], in_=bnag[:, :, 1], func=AF.Sqrt, bias=epsc[:, 1:2],
                                              scale=1.0), reads=[bnag, epsc], writes=[stat.sub(32, 48)])
            add("dve", lambda e: e.reciprocal(stat[:, 8:12], stat[:, 8:12]), reads=[stat.sub(32, 48)],
                writes=[stat.sub(32, 48)])
            for h in range(4):
                add("dve", lambda e, h=h: e.tensor_scalar(o_n[:, h * 128:(h + 1) * 128], pb[7].t[:, h * 128:(h + 1) * 128],
                                                           bnag[:, h, 0:1], stat[:, 8 + h:9 + h], ALU.subtract, ALU.mult),
                    reads=[pb[7], bnag, stat.sub(32, 48)], writes=[o_n])
            add(_pe("sg"), lambda e: e.tensor_tensor(sg[p][:, :], sg[p][:, :], rng_bc, ALU.mult),
                reads=[sg[p], bro.sub(0, 2048)], writes=[sg[p]])
            add(_pe("sg"), lambda e: e.tensor_tensor(ret_o[:, :], o_n[:, :], sg[p][:, :], ALU.mult),
                reads=[o_n, sg[p]], writes=[ret_o])
            for h in range(4):
                add("pe", lambda e, h=h: e.transpose(tra.t[:, h * 128:(h + 1) * 128], ret_o[:, h * 128:(h + 1) * 128],
                                                      ident_b[:, :]), reads=[ret_o, ident_b], writes=[tra])
            add("act", lambda e: e.copy(catT[:, 0:4, j * 128:(j + 1) * 128],
                                        tra.t[:, 0:512].rearrange("p (h i) -> p h i", h=4)),
                reads=[tra], writes=[catT.sub(0, 4096)])

        stage_d(0)
        chk(2)
        stage_d(1)
        stage_e(0, tau * 4 + 0)
        stage_d(2)
        stage_e(1, tau * 4 + 1)
        stage_d(3)
        stage_e(2, tau * 4 + 2)
        stage_e(3, tau * 4 + 3)
        chk(3)

        def conv_chunk(cc):
            banks = [pb[2], pb[3], pb[4]]
            for part in range(3):
                slot = rg_a.get()
                for k in range(8):
                    add("pe", lambda e, part=part, k=k, slot=slot: e.matmul(banks[part].t[:, :], slot[:, k, :], hT[:, k, :],
                                                                             start=(k == 0), stop=(k == 7)),
                        reads=[slot, hT], writes=[banks[part]])
            pc = pbuf.sub(cc * 2056, (cc + 1) * 2056)
            add("act", lambda e: e.copy(gc_sb[:, :], pb[4].t[:, :]), reads=[pb[4]], writes=[gc_sb])
            add("dve", lambda e, cc=cc: e.tensor_copy(pbuf[:, cc, 0:2], pbuf[:, cc, 512:514]), reads=[pc], writes=[pc])
            add("dve", lambda e, cc=cc: e.tensor_tensor(pbuf[:, cc, 2:514], pb[2].t[:, :], gc_sb[:, :], ALU.mult),
                reads=[pb[2], gc_sb, pc], writes=[pc])
            cyc = cy.sub(cc * 2048, (cc + 1) * 2048)
            add("act", lambda e, cc=cc: e.activation(out=cy[:, cc, :], in_=pbuf[:, cc, 2:514], func=AF.Identity,
                                                     bias=pvT[:, R_CMB + cc:R_CMB + cc + 1],
                                                     scale=pvT[:, R_CMW + 8 + cc:R_CMW + 9 + cc]),
                reads=[pc, pvT], writes=[cyc])
            add("dve", lambda e, cc=cc: e.scalar_tensor_tensor(out=cy[:, cc, :], in0=pbuf[:, cc, 1:513],
                                                               scalar=pvT[:, R_CMW + 4 + cc:R_CMW + 5 + cc],
                                                               in1=cy[:, cc, :], op0=ALU.mult, op1=ALU.add),
                reads=[pc, pvT, cyc], writes=[cyc])
            add("dve", lambda e, cc=cc: e.scalar_tensor_tensor(out=cy[:, cc, :], in0=pbuf[:, cc, 0:512],
                                                               scalar=pvT[:, R_CMW + cc:R_CMW + cc + 1],
                                                               in1=cy[:, cc, :], op0=ALU.mult, op1=ALU.add),
                reads=[pc, pvT, cyc], writes=[cyc])
            add("dve", lambda e, cc=cc: e.tensor_tensor(cy[:, cc, :], pb[3].t[:, :], cy[:, cc, :], ALU.mult),
                reads=[pb[3], cyc], writes=[cyc])
            add("act", lambda e, cc=cc: e.activation(out=sq[:, :], in_=cy[:, cc, :], func=AF.Square),
                reads=[cyc], writes=[sq])
            add("pe", lambda e, cc=cc: e.matmul(pb[5].t[:, :], ones_b[:, :], sq[:, :], start=(cc == 0), stop=(cc == 3)),
                reads=[ones_b, sq], writes=[pb[5]])
        for cc in range(4):
            conv_chunk(cc)
        add("act", lambda e: e.activation(out=rstd_bc[:, :], in_=pb[5].t[:, :], func=AF.Sqrt, bias=epsc[:, 3:4], scale=1.0),
            reads=[pb[5], epsc], writes=[rstd_bc])
        add("dve", lambda e: e.reciprocal(rstd_bc[:, :], rstd_bc[:, :]), reads=[rstd_bc], writes=[rstd_bc])
        add("dve", lambda e: e.tensor_scalar(rstd_bc[:, :], rstd_bc[:, :], float(np.sqrt(512.0)), None, ALU.mult),
            reads=[rstd_bc], writes=[rstd_bc])
        for cc in range(4):
            add("dve", lambda e, cc=cc: e.scalar_tensor_tensor(out=catT[:, 4 + cc, :], in0=cy[:, cc, :],
                                                               scalar=pvT[:, R_CNG + cc:R_CNG + cc + 1],
                                                               in1=rstd_bc[:, :], op0=ALU.mult, op1=ALU.mult),
                reads=[cy.sub(cc * 2048, (cc + 1) * 2048), pvT, rstd_bc], writes=[catT.sub((4 + cc) * 1024, (5 + cc) * 1024)])

        def wout_block(j):
            for hf in range(2):
                bank = pb[6 + hf]
                for k in range(8):
                    add("pe", lambda e, k=k, hf=hf, bank=bank: e.matmul(bank.t[:, :], catT[:, k, j * 128:(j + 1) * 128],
                                                                        w_outs[:, hf, k, :], start=(k == 0), stop=(k == 7)),
                        reads=[catT, w_outs.sub(hf * 8192, (hf + 1) * 8192)], writes=[bank])
                xr = x_res.sub(j * 4096 + hf * 2048, j * 4096 + (hf + 1) * 2048)
                add("dve", lambda e, hf=hf, bank=bank: e.tensor_tensor(gtmp[:, :], bank.t[:, :],
                                                                       gt1_bc[:, hf * 512:(hf + 1) * 512], ALU.mult),
                    reads=[bank, bro], writes=[gtmp])
                add(_pe("res"), lambda e, hf=hf: e.tensor_tensor(x_res[:, j, hf * 512:(hf + 1) * 512],
                                                             x_res[:, j, hf * 512:(hf + 1) * 512], gtmp[:, :], ALU.add),
                    reads=[xr, gtmp], writes=[xr])

        chk(4)
        for j in range(4):
            wout_block(j)
        chk(5)

        norm_to_hT(16)
        chk(6)

        def up_chunk(c):
            slot = rg_u.get()
            pr = c % 2
            banks = [pb[2 + 2 * pr], pb[3 + 2 * pr]]
            for a in range(2):
                for k in range(8):
                    add("pe", lambda e, a=a, k=k, slot=slot: e.matmul(banks[a].t[:, :], slot[:, k, a, :], hT[:, k, :],
                                                                      start=(k == 0), stop=(k == 7)),
                        reads=[slot, hT], writes=[banks[a]])
            us = u_sb[c % 3]
            ys = yab[c % 3]
            uh = uhalo.sub(c * 16, (c + 1) * 16)
            add(_pe("halo"), lambda e, c=c, us=us: e.tensor_copy(us[:, :, 0:2], uhalo[:, c, :, :]), reads=[uh], writes=[us])
            for a in range(2):
                add("act", lambda e, a=a, us=us: e.copy(us[:, a, 2:514], banks[a].t[:, :]), reads=[banks[a]], writes=[us])
            add(_pe("halo"), lambda e, c=c, us=us: e.tensor_copy(uhalo[:, c, :, :], us[:, :, 512:514]), reads=[us], writes=[uh])
            for a in range(2):
                ch = a * NFF + c
                add("act", lambda e, a=a, ch=ch, us=us, ys=ys: e.activation(
                    out=ys[:, a, :], in_=us[:, a, 2:514], func=AF.Identity,
                    bias=pvT[:, R_CFB + ch:R_CFB + ch + 1], scale=pvT[:, R_CFW + 88 + ch:R_CFW + 89 + ch]),
                    reads=[us, pvT], writes=[ys])
                add("dve", lambda e, a=a, ch=ch, us=us, ys=ys: e.scalar_tensor_tensor(
                    out=ys[:, a, :], in0=us[:, a, 1:513], scalar=pvT[:, R_CFW + 44 + ch:R_CFW + 45 + ch],
                    in1=ys[:, a, :], op0=ALU.mult, op1=ALU.add), reads=[us, pvT, ys], writes=[ys])
                add("dve", lambda e, a=a, ch=ch, us=us, ys=ys: e.scalar_tensor_tensor(
                    out=ys[:, a, :], in0=us[:, a, 0:512], scalar=pvT[:, R_CFW + ch:R_CFW + ch + 1],
                    in1=ys[:, a, :], op0=ALU.mult, op1=ALU.add), reads=[us, pvT, ys], writes=[ys])
            add("act", lambda e, ys=ys: e.activation(out=ys[:, 0, :], in_=ys[:, 0, :], func=AF.Silu), reads=[ys], writes=[ys])
            add(_pe("gmul"), lambda e, c=c, ys=ys: e.tensor_tensor(gT[:, c, :], ys[:, 0, :], ys[:, 1, :], ALU.mult),
                reads=[ys], writes=[gT.sub(c * 1024, (c + 1) * 1024)])

        if tau + 1 < ntiles:
            load_mixer_weights(False)
        for c in range(NFF):
            up_chunk(c)
        chk(7)

        acc_banks = {0: [pb[2], pb[3], pb[4], pb[5]], 1: [pb[6], pb[7], pb[2], pb[3]]}
        for hf in range(2):
            for cg in range(11):
                slot = rg_d.get()
                for j in range(4):
                    bank = acc_banks[hf][j]
                    for cl in range(2):
                        c = cg * 2 + cl
                        add("pe", lambda e, j=j, cl=cl, c=c, slot=slot, bank=bank: e.matmul(
                            bank.t[:, :], gT[:, c, j * 128:(j + 1) * 128], slot[:, cl, :],
                            start=(c == 0), stop=(c == NFF - 1)), reads=[gT, slot], writes=[bank])
            for j in range(4):
                bank = acc_banks[hf][j]
                xr = x_res.sub(j * 4096 + hf * 2048, j * 4096 + (hf + 1) * 2048)
                add("dve", lambda e, hf=hf, bank=bank: e.tensor_tensor(ftmp[:, :], bank.t[:, :],
                                                                       gt2_bc[:, hf * 512:(hf + 1) * 512], ALU.mult),
                    reads=[bank, bro], writes=[ftmp])
                add(_pe("res"), lambda e, j=j, hf=hf: e.tensor_tensor(x_res[:, j, hf * 512:(hf + 1) * 512],
                                                                  x_res[:, j, hf * 512:(hf + 1) * 512], ftmp[:, :], ALU.add),
                    reads=[xr, ftmp], writes=[xr])

        chk(8)
        def final_block(j):
            xr = x_res.sub(j * 4096, (j + 1) * 4096)
            rms_rstd32(xr, x_res[:, j, :], 16 + j)
            add("dve", lambda e, j=j: e.scalar_tensor_tensor(out=x_res[:, j, :], in0=x_res[:, j, :],
                                                             scalar=stat[:, 16 + j:17 + j], in1=fg_bc,
                                                             op0=ALU.mult, op1=ALU.mult),
                reads=[xr, stat.sub((16 + j) * 4, (17 + j) * 4), bro], writes=[xr])
            add("sp", lambda e, j=j: e.dma_start(out=y_d[t0 + j * 128:t0 + (j + 1) * 128, :], in_=x_res[:, j, :]),
                reads=[xr], writes=[dram("y", t0 + j * 128, t0 + (j + 1) * 128)], dma=True)

        for j in range(4):
            final_block(j)

    try:
        if kst0:
            raise _Stop()
        for tau in range(ntiles):
            tile(tau)
    except _Stop:
        pass

    S.emit({"sp": ["sp"]})
    return nc


_CACHE = {}


def kernel(x, c, positions, w_ada, b_ada, norm1_g, w_in, conv_mix_w, conv_mix_b, ret_norm_g, conv_norm_g,
           w_out, norm2_g, w_up, conv_ffn_w, conv_ffn_b, w_down, final_g):
    f = lambda a: np.ascontiguousarray(np.asarray(a), dtype=np.float32)
    x, c, w_ada, b_ada, w_in, w_out, w_up, w_down = map(f, (x, c, w_ada, b_ada, w_in, w_out, w_up, w_down))
    positions = np.ascontiguousarray(np.asarray(positions), dtype=np.int32)
    cst, dec128 = _host_consts()
    if "nc" not in _CACHE:
        _CACHE["nc"] = build_nc(dec128)
    nc = _CACHE["nc"]

    def kmaj(w):
        K, N = w.shape
        return np.ascontiguousarray(w.reshape(K // 128, 128, N).transpose(1, 0, 2))

    wada_l = np.ascontiguousarray(kmaj(w_ada).reshape(128, 8, 12, 512).transpose(2, 0, 1, 3))
    win = kmaj(w_in)
    winb_l = np.ascontiguousarray(win[:, :, 0:2048].reshape(128, 8, 4, 512).transpose(2, 0, 1, 3))
    wina_l = np.ascontiguousarray(win[:, :, 2048:3584].reshape(128, 8, 12, 128).transpose(2, 0, 1, 3))
    wout_l = np.ascontiguousarray(kmaj(w_out).reshape(128, 8, 2, 512).transpose(2, 0, 1, 3))
    wup = kmaj(w_up).reshape(128, 8, 2, NFF, 128)
    wup_l = np.ascontiguousarray(wup.transpose(3, 0, 1, 2, 4))
    wdn = w_down.reshape(11, 2, 128, 2, 512)
    wdn_l = np.ascontiguousarray(wdn.transpose(3, 0, 2, 1, 4))

    pv = np.zeros((R_TOT, 128), np.float32)
    pv[R_BADA:R_BADA + 48] = b_ada.reshape(48, 128)
    pv[R_G1:R_G1 + 8] = f(norm1_g).reshape(8, 128)
    pv[R_G2:R_G2 + 8] = f(norm2_g).reshape(8, 128)
    pv[R_CMW:R_CMW + 12] = f(conv_mix_w).reshape(12, 128)
    pv[R_CMB:R_CMB + 4] = f(conv_mix_b).reshape(4, 128)
    pv[R_CNG:R_CNG + 4] = f(conv_norm_g).reshape(4, 128)
    pv[R_CFW:R_CFW + 132] = f(conv_ffn_w).reshape(132, 128)
    pv[R_CFB:R_CFB + 44] = f(conv_ffn_b).reshape(44, 128)
    brow = np.concatenate([f(ret_norm_g), f(final_g), b_ada[2048:3072], b_ada[5120:6144]])
    bro = np.ascontiguousarray(np.broadcast_to(brow[None, :], (128, C_TOT)))

    in_maps = []
    for b in range(NB):
        pvb = pv.copy()
        pvb[R_C:R_C + 8] = c[b].reshape(8, 128)
        in_maps.append({
            "x": x[b], "pos": positions[b].reshape(32, 128), "pv": pvb, "bro": bro, "cst": cst,
            "w_ada": wada_l, "w_in_b": winb_l, "w_in_a": wina_l, "w_out": wout_l, "w_up": wup_l, "w_down": wdn_l,
        })
    res = run_bass_kernel_spmd(nc, in_maps, core_ids=list(range(NB)))
    return np.stack([np.asarray(r["y"], dtype=np.float32) for r in res.results], axis=0)
```

```python
import contextlib
import numpy as np
import concourse.bass as bass
import concourse.mybir as mybir
from concourse.bass_utils import run_bass_kernel_spmd

F32 = mybir.dt.float32
BF16 = mybir.dt.bfloat16
I32 = mybir.dt.int32
AF = mybir.ActivationFunctionType
ALU = mybir.AluOpType

ENGS = ("pe", "act", "dve", "pool", "sp")
import os as _os
_KP = _os.environ.get("KPOOL", "rot,halo,gmul,res,sg").split(",")


def _pe(group):
    return "pool" if group in _KP else "dve"
DMA_POOL = 16

D = 1024
SEQ = 4096
NB = 8
T = 512
NT = SEQ // T
DFF = 2816
NFF = DFF // 128
EPS = 1e-6
HEADS = 4


class Buf:
    def __init__(self, t, space, lo, hi):
        self.t = t
        self.space = space
        self.lo = lo
        self.hi = hi

    def __getitem__(self, k):
        return self.t[k]

    def sub(self, lo, hi):
        return Buf(self.t, self.space, self.lo + lo, self.lo + hi)


class _Op:
    __slots__ = ("eng", "fn", "idx", "waits", "inc", "dma", "dsem", "dcnt", "cnt", "selfwait")

    def __init__(self, eng, fn, dma):
        self.eng = eng
        self.fn = fn
        self.dma = dma
        self.waits = {}
        self.inc = False
        self.dsem = None
        self.dcnt = 0
        self.cnt = 0
        self.selfwait = None


class Sched:
    def __init__(self, nc):
        self.nc = nc
        self.ops = {e: [] for e in ENGS}
        self.spaces = {}
        self.dma_rr = {e: 0 for e in ENGS}
        self.dma_cnt = {}
        self.dma_last = {}
        self.psum_rd = {}

    def _dep(self, op, prod):
        if prod is None or prod is op:
            return
        if prod.dma:
            key = ("d", prod.eng, prod.dsem)
            val = prod.dcnt * 16
            cur = op.waits.get(key)
            op.waits[key] = val if cur is None else max(cur, val)
            return
        if prod.eng == "pe" and op.eng == "pe":
            return
        prod.inc = True
        key = ("c", prod.eng)
        cur = op.waits.get(key)
        if cur is None or cur.idx < prod.idx:
            op.waits[key] = prod

    def add(self, eng, fn, reads=(), writes=(), dma=False):
        op = _Op(eng, fn, dma)
        op.idx = len(self.ops[eng])
        self.ops[eng].append(op)
        if dma:
            slot = self.dma_rr[eng] % DMA_POOL
            self.dma_rr[eng] += 1
            k = (eng, slot)
            self.dma_cnt[k] = self.dma_cnt.get(k, 0) + 1
            op.dsem = slot
            op.dcnt = self.dma_cnt[k]
            prev = self.dma_last.get(k)
            if prev is not None:
                op.selfwait = (slot, prev.dcnt * 16)
            self.dma_last[k] = op
        for b in reads:
            if b.space == "psum":
                for bank in range(b.lo // 2048, (b.hi - 1) // 2048 + 1):
                    rd = self.psum_rd.setdefault(bank, {})
                    for oe, oop in rd.items():
                        if oe != eng:
                            self._dep(op, oop)
                    rd[eng] = op
            recs = self.spaces.setdefault(b.space, [])
            for r in recs:
                if r[0] < b.hi and b.lo < r[1]:
                    self._dep(op, r[2])
                    if dma:
                        r[4].append(op)
                    else:
                        r[3][eng] = op
        for b in writes:
            recs = self.spaces.setdefault(b.space, [])
            keep = []
            for r in recs:
                if r[0] < b.hi and b.lo < r[1]:
                    self._dep(op, r[2])
                    for rd in r[3].values():
                        self._dep(op, rd)
                    for rd in r[4]:
                        self._dep(op, rd)
                    if r[0] < b.lo:
                        keep.append([r[0], b.lo, r[2], dict(r[3]), list(r[4])])
                    if b.hi < r[1]:
                        keep.append([b.hi, r[1], r[2], dict(r[3]), list(r[4])])
                else:
                    keep.append(r)
            keep.append([b.lo, b.hi, op, {}, []])
            self.spaces[b.space] = keep
        return op

    def emit(self, final_waits):
        nc = self.nc
        for e in ENGS:
            c = 0
            for op in self.ops[e]:
                if op.inc and not op.dma:
                    c += 1
                op.cnt = c
        with contextlib.ExitStack() as st:
            csem = {e: st.enter_context(nc.semaphore("c_" + e)) for e in ENGS}
            dsem = {}
            for e in ENGS:
                for s in range(min(DMA_POOL, self.dma_rr[e])):
                    dsem[(e, s)] = st.enter_context(nc.semaphore("d_%s_%d" % (e, s)))
            block = st.enter_context(nc.Block())

            def run(e, engine):
                waited = {}
                for op in self.ops[e]:
                    for key, val in op.waits.items():
                        if key[0] == "d":
                            sem = dsem[(key[1], key[2])]
                            v = val
                        else:
                            sem = csem[key[1]]
                            v = val.cnt
                        if waited.get(key, 0) >= v:
                            continue
                        waited[key] = v
                        engine.wait_ge(sem, v)
                    if op.dma:
                        if op.selfwait is not None:
                            key = ("d", e, op.selfwait[0])
                            if waited.get(key, 0) < op.selfwait[1]:
                                waited[key] = op.selfwait[1]
                                engine.wait_ge(dsem[(e, op.selfwait[0])], op.selfwait[1])
                        op.fn(engine).then_inc(dsem[(e, op.dsem)], 16)
                    else:
                        ins = op.fn(engine)
                        if op.inc:
                            ins.then_inc(csem[e], 1)
                if e in final_waits:
                    for (qe, slot), op in self.dma_last.items():
                        if qe in final_waits[e]:
                            engine.wait_ge(dsem[(qe, slot)], op.dcnt * 16)

            @block.tensor
            def _(t):
                run("pe", t)

            @block.scalar
            def _(a):
                run("act", a)

            @block.vector
            def _(v):
                run("dve", v)

            @block.gpsimd
            def _(g):
                run("pool", g)

            @block.sync
            def _(s):
                run("sp", s)


class Arena:
    def __init__(self, nc, base, limit):
        self.nc = nc
        self.off = base
        self.limit = limit
        self.n = 0

    def alloc(self, shape, dtype, at=None):
        size = mybir.dt.size(dtype)
        for s in shape[1:]:
            size *= s
        if at is None:
            at = (self.off + 63) // 64 * 64
            self.off = at + size
        assert at + size <= self.limit, ("SBUF overflow", at + size, self.limit)
        self.n += 1
        t = self.nc.alloc_sbuf_tensor_at("sb%d" % self.n, list(shape), dtype, offset=at)
        return Buf(t, "sbuf", at, at + size)


R_C = 0
R_BADA = 8
R_G1 = 56
R_G2 = 64
R_CMW = 72
R_CMB = 84
R_CNG = 88
R_CFW = 92
R_CFB = 224
R_TOT = 268
C_RNG = 0
C_FG = 512
C_BG1 = 1536
C_BG2 = 2560
C_TOT = 3584


def _host_consts():
    h = np.arange(HEADS, dtype=np.float64)
    gam = 1.0 - 2.0 ** (-5.0 - h)
    lg = np.log(gam)
    i = np.arange(128)
    ci, cj = i[:, None] // 64, i[None, :] // 64
    dist = i[:, None] - i[None, :]
    W = np.zeros((HEADS, 128, 128))
    for hh in range(HEADS):
        same = np.exp(np.abs(dist) * lg[hh])
        causal = np.exp(dist * lg[hh])
        W[hh] = np.where(ci == cj, same, np.where(ci > cj, causal, 0.0))
    qw = np.exp((i[None, :] + 1) * lg[:, None])
    maskT = np.transpose(W / qw[:, :, None], (2, 0, 1))
    qwT = np.broadcast_to((qw * 128 ** -0.5)[None], (128, HEADS, 128))
    kw = np.exp((127 - i)[:, None] * lg[None, :])
    dec = np.exp(128 * lg)
    invf = (np.float32(10000.0) ** (-(np.arange(0, 128, 2, dtype=np.float32)) / np.float32(128))).astype(np.float32)
    cst = np.zeros((128, 128 + 512 + 512 + 4 + 64), np.float32)
    cst[:, 0:128] = np.eye(128)
    cst[:, 128:640] = maskT.reshape(128, 512)
    cst[:, 640:1152] = qwT.reshape(128, 512)
    cst[:, 1152:1156] = kw
    cst[:, 1156:1220] = invf[None, :]
    return cst, [float(v) for v in dec]


N_CST = 1220


class Ring:
    def __init__(self, slots, total, issue_fn):
        self.slots = slots
        self.total = total
        self.issue_fn = issue_fn
        self.issued = 0
        self.consumed = 0

    def prefetch(self):
        lim = min(self.total, self.consumed + len(self.slots))
        while self.issued < lim:
            self.issue_fn(self.issued, self.slots[self.issued % len(self.slots)])
            self.issued += 1

    def get(self):
        self.prefetch()
        slot = self.slots[self.consumed % len(self.slots)]
        self.consumed += 1
        return slot


class _Stop(Exception):
    pass


def build_nc(dec128, debug=False, kstop=99, ntiles=NT):
    def chk(n):
        if kstop <= n:
            raise _Stop()

    nc = bass.Bass("TRN2", target_bir_lowering=False)

    def din(name, shape, dt=F32):
        return nc.dram_tensor(name, list(shape), dt, kind="ExternalInput").ap()

    x_d = din("x", [SEQ, D])
    pos_d = din("pos", [32, 128], I32)
    pv_d = din("pv", [R_TOT, 128])
    bro_d = din("bro", [128, C_TOT])
    cst_d = din("cst", [128, N_CST])
    wada_d = din("w_ada", [12, 128, 8, 512])
    winb_d = din("w_in_b", [4, 128, 8, 512])
    wina_d = din("w_in_a", [12, 128, 8, 128])
    wout_d = din("w_out", [2, 128, 8, 512])
    wup_d = din("w_up", [NFF, 128, 8, 2, 128])
    wdn_d = din("w_down", [2, 11, 128, 2, 512])
    y_d = nc.dram_tensor("y", [SEQ, D], F32, kind="ExternalOutput").ap()

    S = Sched(nc)
    A = Arena(nc, 16640, 229376)
    add = S.add

    tra = Buf(nc.alloc_psum_tensor("tra", [128, 1024], BF16), "psum", 0, 2048)
    trb = Buf(nc.alloc_psum_tensor("trb", [128, 1024], BF16), "psum", 2048, 4096)
    pb = {}
    for i in range(2, 8):
        pb[i] = Buf(nc.alloc_psum_tensor("pb%d" % i, [128, 512], F32), "psum", i * 2048, (i + 1) * 2048)

    cst = A.alloc([128, N_CST], F32)
    ident_f = cst.t[:, 0:128]
    maskT = cst.t[:, 128:640].rearrange("p (h i) -> p h i", h=4)
    qwT = cst.t[:, 640:1152].rearrange("p (h i) -> p h i", h=4)
    kwtab = cst.t[:, 1152:1156]
    invf = cst.t[:, 1156:1220]
    ident_b = A.alloc([128, 128], BF16)
    ones_b = A.alloc([128, 128], BF16)
    epsc = A.alloc([128, 4], F32)
    pvT = A.alloc([128, R_TOT + 4], F32)
    modT = A.alloc([128, 48], F32)
    ab = A.alloc([128, 32], F32)
    bro = A.alloc([128, C_TOT], F32)
    posf = A.alloc([128, 32], F32)
    x_res = A.alloc([128, 4, D], F32)
    hT = A.alloc([128, 8, T], BF16)
    w_inb = A.alloc([128, 4, 8, 512], BF16)
    w_outs = A.alloc([128, 2, 8, 512], BF16)
    NRA, NRU, NRD = 3, 4, 6
    ring_a = [A.alloc([128, 8, 128], BF16) for _ in range(NRA)]
    ring_u = [A.alloc([128, 8, 2, 128], BF16) for _ in range(NRU)]
    ring_d = [A.alloc([128, 2, 512], BF16) for _ in range(NRD)]
    state_f = A.alloc([128, 4, 128], F32)
    state_b = [A.alloc([128, 4, 128], BF16) for _ in range(2)]
    pbuf = A.alloc([128, 4, 514], F32)
    uhalo = A.alloc([128, NFF, 2, 2], F32)
    stat = A.alloc([128, 64], F32)
    junk = A.alloc([128, D], BF16)

    region0 = A.off
    cs_t = A.alloc([128, 4, 64], F32)
    sn_t = A.alloc([128, 4, 64], F32)
    rtmp = A.alloc([128, 4, 64], F32)
    rki = A.alloc([128, 4, 64], I32)
    xn_at = (A.off + 63) // 64 * 64
    qf = [A.alloc([128, 512], F32) for _ in range(2)]
    kf = [A.alloc([128, 512], F32) for _ in range(2)]
    xn = A.alloc([128, 4, D], BF16, at=xn_at)
    ta = [A.alloc([128, 512], F32) for _ in range(2)]
    tb = [A.alloc([128, 512], F32) for _ in range(2)]
    q_rot = [A.alloc([128, 512], BF16) for _ in range(2)]
    k_rot = [A.alloc([128, 512], BF16) for _ in range(2)]
    v_bf = [A.alloc([128, 512], BF16) for _ in range(2)]
    v_kw = [A.alloc([128, 512], BF16) for _ in range(2)]
    sg = [A.alloc([128, 512], F32) for _ in range(2)]
    qTs = A.alloc([128, 4, 128], BF16)
    kTs = A.alloc([128, 4, 128], BF16)
    smT = A.alloc([128, 4, 128], BF16)
    o_n = A.alloc([128, 512], F32)
    ret_o = A.alloc([128, 512], BF16)
    bnst = A.alloc([128, 4, 6], F32)
    bnag = A.alloc([128, 4, 2], F32)
    catT = A.alloc([128, 8, T], BF16)
    gc_sb = A.alloc([128, 512], F32)
    cy = A.alloc([128, 4, 512], F32)
    sq = A.alloc([128, 512], BF16)
    rstd_bc = A.alloc([128, 512], F32)
    gtmp = A.alloc([128, 512], F32)
    region_m_end = A.off
    A.off = region0
    u_sb = [A.alloc([128, 2, 514], F32) for _ in range(3)]
    yab = [A.alloc([128, 2, 512], F32) for _ in range(3)]
    gT = A.alloc([128, NFF, T], BF16)
    ftmp = A.alloc([128, 512], F32)
    region_f_end = A.off
    A.off = region0
    pstage = A.alloc([128, 3, 128], F32)
    posi = A.alloc([32, 128], I32)
    posff = A.alloc([32, 128], F32)
    s_col = A.alloc([128, 8], F32)
    s_bc = A.alloc([128, 8, 128], F32)
    wada = [A.alloc([128, 8, 512], F32) for _ in range(2)]
    A.off = max(region_m_end, region_f_end, A.off)
    assert A.off <= A.limit, A.off

    def dram(name, lo=0, hi=1):
        return Buf(None, "dram_" + name, lo, hi)

    add("sp", lambda e: e.dma_start(out=cst[:, :], in_=cst_d[:, :]), writes=[cst], dma=True)
    add("sp", lambda e: e.dma_start(out=bro[:, :], in_=bro_d[:, :]), writes=[bro], dma=True)
    add("dve", lambda e: e.memset(pstage[:, :, :], 0.0), writes=[pstage])
    for g in range(3):
        r0, r1 = g * 128, min(R_TOT, (g + 1) * 128)
        add("sp", lambda e, g=g, r0=r0, r1=r1: e.dma_start(out=pstage[0:r1 - r0, g, :], in_=pv_d[r0:r1, :]),
            writes=[pstage.sub(g * 512, (g + 1) * 512)], dma=True)
    add("sp", lambda e: e.dma_start(out=posi[:, :], in_=pos_d[:, :]), writes=[posi], dma=True)
    add("dve", lambda e: e.tensor_copy(ident_b[:, :], ident_f), reads=[cst], writes=[ident_b])
    add("dve", lambda e: e.memset(ones_b[:, :], 1.0), writes=[ones_b])
    add("dve", lambda e: e.memset(epsc[:, 0:1], 1024 * EPS), writes=[epsc.sub(0, 4)])
    add("dve", lambda e: e.memset(epsc[:, 1:2], EPS), writes=[epsc.sub(4, 8)])
    add("dve", lambda e: e.memset(epsc[:, 2:3], float(np.pi / 2)), writes=[epsc.sub(8, 12)])
    add("dve", lambda e: e.memset(epsc[:, 3:4], 512 * EPS), writes=[epsc.sub(12, 16)])
    add("dve", lambda e: e.memset(state_f[:, :, :], 0.0), writes=[state_f])
    add("dve", lambda e: e.memset(state_b[0][:, :, :], 0.0), writes=[state_b[0]])
    add("dve", lambda e: e.memset(pbuf[:, :, :], 0.0), writes=[pbuf])
    add("dve", lambda e: e.memset(uhalo[:, :, :, :], 0.0), writes=[uhalo])
    for g in range(3):
        n = min(R_TOT, (g + 1) * 128) - g * 128
        add("pe", lambda e, g=g: e.matmul(pb[2].t[:, g * 128:(g + 1) * 128], pstage[:, g, :], ident_f,
                                            start=True, stop=True),
            reads=[pstage, cst], writes=[pb[2]])
    add("act", lambda e: e.copy(pvT[:, 0:R_TOT], pb[2].t[:, 0:R_TOT]), reads=[pb[2]], writes=[pvT])
    add("dve", lambda e: e.tensor_copy(posff[:, :], posi[:, :]), reads=[posi], writes=[posff])
    add("pe", lambda e: e.matmul(pb[3].t[:, 0:32], posff[:, :], ident_f[0:32, 0:32], start=True, stop=True),
        reads=[posff, cst], writes=[pb[3]])
    add("act", lambda e: e.copy(posf[:, :], pb[3].t[:, 0:32]), reads=[pb[3]], writes=[posf])
    add("act", lambda e: e.activation(out=s_col[:, :], in_=pvT[:, R_C:R_C + 8], func=AF.Silu),
        reads=[pvT], writes=[s_col])
    add("dve", lambda e: e.tensor_copy(s_bc[:, :, :], s_col[:, :].unsqueeze(2).to_broadcast([128, 8, 128])),
        reads=[s_col], writes=[s_bc])
    for g in range(12):
        wb = wada[g % 2]
        add("sp", lambda e, g=g, wb=wb: e.dma_start(out=wb[:, :, :], in_=wada_d[g]), writes=[wb], dma=True)
        if g in (4, 5, 10, 11):
            bank = pb[4 + (g % 2)]
            for k in range(8):
                add("pe", lambda e, k=k, wb=wb, bank=bank: e.matmul(bank.t[:, :], s_bc[:, k, :], wb[:, k, :],
                                                                    start=(k == 0), stop=(k == 7)),
                    reads=[s_bc, wb], writes=[bank])
            c0 = (C_BG1 if g < 6 else C_BG2) + (g % 2) * 512
            add("dve", lambda e, bank=bank, c0=c0: e.tensor_tensor(bro[:, c0:c0 + 512], bank.t[:, :],
                                                                    bro[:, c0:c0 + 512], ALU.add),
                reads=[bank, bro.sub(c0 * 4, (c0 + 512) * 4)], writes=[bro.sub(c0 * 4, (c0 + 512) * 4)])
        else:
            for m in range(4):
                col = g * 4 + m
                for k in range(8):
                    add("pe", lambda e, k=k, m=m, wb=wb, col=col: e.matmul(
                        pb[6].t[:, col:col + 1], wb[:, k, m * 128:(m + 1) * 128], s_col[:, k:k + 1],
                        start=(k == 0), stop=(k == 7)),
                        reads=[wb, s_col], writes=[pb[6].sub(col * 4, col * 4 + 4)])
    add("dve", lambda e: e.tensor_tensor(modT[:, :], pb[6].t[:, 0:48], pvT[:, R_BADA:R_BADA + 48], ALU.add),
        reads=[pb[6], pvT], writes=[modT])
    add("dve", lambda e: e.scalar_tensor_tensor(out=ab[:, 0:8], in0=modT[:, 8:16], scalar=1.0, in1=pvT[:, R_G1:R_G1 + 8],
                                                op0=ALU.add, op1=ALU.mult), reads=[modT, pvT], writes=[ab.sub(0, 32)])
    add("dve", lambda e: e.tensor_copy(ab[:, 8:16], modT[:, 0:8]), reads=[modT], writes=[ab.sub(32, 64)])
    add("dve", lambda e: e.scalar_tensor_tensor(out=ab[:, 16:24], in0=modT[:, 32:40], scalar=1.0,
                                                in1=pvT[:, R_G2:R_G2 + 8], op0=ALU.add, op1=ALU.mult),
        reads=[modT, pvT], writes=[ab.sub(64, 96)])
    add("dve", lambda e: e.tensor_copy(ab[:, 24:32], modT[:, 24:32]), reads=[modT], writes=[ab.sub(96, 128)])
    add("dve", lambda e: e.tensor_scalar(bro[:, C_FG:C_FG + 1024], bro[:, C_FG:C_FG + 1024], 32.0, None, ALU.mult),
        reads=[bro.sub(C_FG * 4, (C_FG + 1024) * 4)], writes=[bro.sub(C_FG * 4, (C_FG + 1024) * 4)])

    rng_bc = bro.t[:, C_RNG:C_RNG + 512]
    kst0 = kstop <= 0
    fg_bc = bro.t[:, C_FG:C_FG + 1024]
    gt1_bc = bro.t[:, C_BG1:C_BG1 + 1024]
    gt2_bc = bro.t[:, C_BG2:C_BG2 + 1024]

    sc_inb = nc.dram_tensor("sc_inb", [4, 128, 8, 512], BF16).ap()
    sc_ina = nc.dram_tensor("sc_ina", [12, 128, 8, 128], BF16).ap()
    sc_out = nc.dram_tensor("sc_out", [2, 128, 8, 512], BF16).ap()
    sc_up = nc.dram_tensor("sc_up", [NFF, 128, 8, 2, 128], BF16).ap()
    sc_dn = nc.dram_tensor("sc_dn", [2, 11, 128, 2, 512], BF16).ap()

    def stream(first, name, idx, dst_fn, src_f32, src_bf, wbuf):
        scr = Buf(None, "dram_" + name, idx, idx + 1)
        if first:
            add("pool", lambda e: e.dma_start(out=dst_fn(), in_=src_f32), writes=[wbuf], dma=True)
            add("sp", lambda e: e.dma_start(out=src_bf, in_=dst_fn()), reads=[wbuf], writes=[scr], dma=True)
        else:
            add("sp", lambda e: e.dma_start(out=dst_fn(), in_=src_bf), reads=[scr], writes=[wbuf], dma=True)

    def _issue_a(g, slot):
        i = g % 12
        cc, part = i // 3, i % 3
        k = part * 4 + cc
        stream(g < 12, "ina", k, lambda: slot[:, :, :], wina_d[k], sc_ina[k], slot)

    def _issue_u(g, slot):
        c = g % NFF
        stream(g < NFF, "up", c, lambda: slot[:, :, :, :], wup_d[c], sc_up[c], slot)

    def _issue_d(g, slot):
        i = g % 22
        hf, cg = i // 11, i % 11
        stream(g < 22, "dn", i, lambda: slot[:, :, :], wdn_d[hf, cg], sc_dn[hf, cg], slot)

    rg_a = Ring(ring_a, 12 * ntiles, _issue_a)
    rg_u = Ring(ring_u, NFF * ntiles, _issue_u)
    rg_d = Ring(ring_d, 22 * ntiles, _issue_d)

    def load_mixer_weights(first):
        for g in range(4):
            stream(first, "inb", g, lambda g=g: w_inb[:, g, :, :], winb_d[g], sc_inb[g], w_inb.sub(g * 8192, (g + 1) * 8192))
        for hf in range(2):
            stream(first, "out", hf, lambda hf=hf: w_outs[:, hf, :, :], wout_d[hf], sc_out[hf],
                   w_outs.sub(hf * 8192, (hf + 1) * 8192))

    def rms_rstd32(src_buf, src_ap, col):
        add("act", lambda e: e.activation(out=junk[:, :], in_=src_ap, func=AF.Square, accum_out=stat[:, col:col + 1]),
            reads=[src_buf], writes=[junk, stat.sub(col * 4, col * 4 + 4)])
        add("act", lambda e: e.activation(out=stat[:, col:col + 1], in_=stat[:, col:col + 1], func=AF.Sqrt,
                                          bias=epsc[:, 0:1], scale=1.0),
            reads=[stat.sub(col * 4, col * 4 + 4), epsc], writes=[stat.sub(col * 4, col * 4 + 4)])
        add("dve", lambda e: e.reciprocal(stat[:, col:col + 1], stat[:, col:col + 1]),
            reads=[stat.sub(col * 4, col * 4 + 4)], writes=[stat.sub(col * 4, col * 4 + 4)])

    def norm_to_hT(aoff):
        for j in range(4):
            rms_rstd32(x_res.sub(j * 4096, (j + 1) * 4096), x_res[:, j, :], j)
            add("dve", lambda e, j=j: e.tensor_scalar(xn[:, j, :], x_res[:, j, :], stat[:, j:j + 1], 32.0,
                                                       ALU.mult, ALU.mult),
                reads=[x_res.sub(j * 4096, (j + 1) * 4096), stat.sub(j * 4, j * 4 + 4)],
                writes=[xn.sub(j * 2048, (j + 1) * 2048)])
        chk(0.6)
        for c in range(8):
            if c == 1:
                chk(0.7)
            if c == 2:
                chk(0.8)
            tr = tra if c % 2 == 0 else trb
            for j in range(4):
                add("pe", lambda e, c=c, j=j, tr=tr: e.transpose(
                    tr.t[:, j * 128:(j + 1) * 128], xn[:, j, c * 128:(c + 1) * 128], ident_b[:, :]),
                    reads=[xn.sub(j * 2048, (j + 1) * 2048), ident_b], writes=[tr])
            add("act", lambda e, c=c, tr=tr: e.activation(
                out=hT[:, c, :], in_=tr.t[:, 0:512], func=AF.Identity,
                bias=ab[:, aoff + 8 + c:aoff + 9 + c], scale=ab[:, aoff + c:aoff + c + 1]),
                reads=[tr, ab], writes=[hT.sub(c * 1024, (c + 1) * 1024)])

    def tile(tau):
        t0 = tau * T
        for j in range(4):
            add("sp", lambda e, j=j: e.dma_start(out=x_res[:, j, :], in_=x_d[t0 + j * 128:t0 + (j + 1) * 128, :]),
                writes=[x_res.sub(j * 4096, (j + 1) * 4096)], dma=True)
        if tau == 0:
            load_mixer_weights(True)
        rg_a.prefetch()
        rg_u.prefetch()
        rg_d.prefetch()
        chk(0.2)
        pj = posf.t[:, tau * 4:tau * 4 + 4].unsqueeze(2).to_broadcast([128, 4, 64])
        iv = invf.unsqueeze(1).to_broadcast([128, 4, 64])
        add("dve", lambda e: e.tensor_tensor(cs_t[:, :, :], pj, iv, ALU.mult), reads=[posf, cst], writes=[cs_t])
        add("dve", lambda e: e.tensor_scalar(rtmp[:, :, :], cs_t[:, :, :], float(1.0 / (2 * np.pi)), None, ALU.mult),
            reads=[cs_t], writes=[rtmp])
        add("dve", lambda e: e.tensor_copy(rki[:, :, :], rtmp[:, :, :]), reads=[rtmp], writes=[rki])
        add("dve", lambda e: e.tensor_copy(rtmp[:, :, :], rki[:, :, :]), reads=[rki], writes=[rtmp])
        add("dve", lambda e: e.scalar_tensor_tensor(out=cs_t[:, :, :], in0=rtmp[:, :, :], scalar=-6.28125,
                                                    in1=cs_t[:, :, :], op0=ALU.mult, op1=ALU.add),
            reads=[rtmp, cs_t], writes=[cs_t])
        add("dve", lambda e: e.scalar_tensor_tensor(out=cs_t[:, :, :], in0=rtmp[:, :, :],
                                                    scalar=-0.0019353071795864769, in1=cs_t[:, :, :],
                                                    op0=ALU.mult, op1=ALU.add), reads=[rtmp, cs_t], writes=[cs_t])
        add("dve", lambda e: e.tensor_scalar(cs_t[:, :, :], cs_t[:, :, :], -3.141592, 3.141592, ALU.max, ALU.min),
            reads=[cs_t], writes=[cs_t])
        add("act", lambda e: e.activation(out=sn_t[:, :, :], in_=cs_t[:, :, :], func=AF.Sin), reads=[cs_t], writes=[sn_t])
        add("act", lambda e: e.activation(out=rtmp[:, :, :], in_=cs_t[:, :, :], func=AF.Abs), reads=[cs_t], writes=[rtmp])
        add("act", lambda e: e.activation(out=cs_t[:, :, :], in_=rtmp[:, :, :], func=AF.Sin, scale=-1.0,
                                          bias=epsc[:, 2:3]), reads=[rtmp, epsc], writes=[cs_t])

        chk(0.5)
        norm_to_hT(0)
        chk(1)

        def stage_d(j):
            p = j % 2
            banks = [pb[2], pb[3], pb[4], pb[5]]
            for g in range(4):
                for k in range(8):
                    add("pe", lambda e, g=g, k=k: e.matmul(banks[g].t[:, :], hT[:, k, j * 128:(j + 1) * 128],
                                                            w_inb[:, g, k, :], start=(k == 0), stop=(k == 7)),
                        reads=[hT, w_inb.sub(g * 8192, (g + 1) * 8192)], writes=[banks[g]])
            chk(1.2)
            add("act", lambda e: e.copy(qf[p][:, :], pb[2].t[:, :]), reads=[pb[2]], writes=[qf[p]])
            add("act", lambda e: e.copy(kf[p][:, :], pb[3].t[:, :]), reads=[pb[3]], writes=[kf[p]])
            add("act", lambda e: e.copy(v_bf[p][:, :], pb[4].t[:, :]), reads=[pb[4]], writes=[v_bf[p]])
            chk(1.4)
            for h in range(4):
                add("dve", lambda e, h=h: e.tensor_scalar(v_kw[p][:, h * 128:(h + 1) * 128], pb[4].t[:, h * 128:(h + 1) * 128],
                                                           kwtab[:, h:h + 1], None, ALU.mult),
                    reads=[pb[4], cst], writes=[v_kw[p]])
            chk(1.5)
            add("act", lambda e: e.activation(out=sg[p][:, :], in_=pb[5].t[:, :], func=AF.Silu),
                reads=[pb[5]], writes=[sg[p]])
            chk(1.6)
            cosb = cs_t.t[:, j, :].unsqueeze(1).to_broadcast([128, 4, 64])
            sinb = sn_t.t[:, j, :].unsqueeze(1).to_broadcast([128, 4, 64])
            for src, dst, en, tt in ((qf[p], q_rot[p], "dve", 0), (kf[p], k_rot[p], _pe("rot"), 1)):
                s4 = src.t[:, :].rearrange("p (h s d) -> p h s d", h=4, s=2)
                a4 = ta[tt].t[:, :].rearrange("p (h s d) -> p h s d", h=4, s=2)
                b4 = tb[tt].t[:, :].rearrange("p (h s d) -> p h s d", h=4, s=2)
                d4 = dst.t[:, :].rearrange("p (h s d) -> p h s d", h=4, s=2)
                for sidx in range(2):
                    add(en, lambda e, s4=s4, a4=a4, sidx=sidx: e.tensor_tensor(a4[:, :, sidx, :], s4[:, :, sidx, :],
                                                                               cosb, ALU.mult),
                        reads=[src, cs_t], writes=[ta[tt]])
                add(en, lambda e, s4=s4, b4=b4: e.tensor_tensor(b4[:, :, 0, :], s4[:, :, 1, :], sinb, ALU.mult),
                    reads=[src, sn_t], writes=[tb[tt]])
                add(en, lambda e, s4=s4, b4=b4: e.tensor_tensor(b4[:, :, 1, :], s4[:, :, 0, :], sinb, ALU.mult),
                    reads=[src, sn_t], writes=[tb[tt]])
                add(en, lambda e, a4=a4, b4=b4, d4=d4: e.tensor_tensor(d4[:, :, 0, :], a4[:, :, 0, :], b4[:, :, 0, :],
                                                                       ALU.subtract),
                    reads=[ta[tt], tb[tt]], writes=[dst])
                add(en, lambda e, a4=a4, b4=b4, d4=d4: e.tensor_tensor(d4[:, :, 1, :], a4[:, :, 1, :], b4[:, :, 1, :],
                                                                       ALU.add),
                    reads=[ta[tt], tb[tt]], writes=[dst])

        def stage_e(j, blk):
            p = j % 2
            sb_cur = state_b[blk % 2]
            sb_nxt = state_b[(blk + 1) % 2]
            for h in range(4):
                add("pe", lambda e, h=h: e.transpose(trb.t[:, h * 128:(h + 1) * 128], q_rot[p][:, h * 128:(h + 1) * 128],
                                                      ident_b[:, :]), reads=[q_rot[p], ident_b], writes=[trb])
            for h in range(4):
                add("pe", lambda e, h=h: e.transpose(tra.t[:, h * 128:(h + 1) * 128],
                                                      k_rot[p][:, h * 128:(h + 1) * 128], ident_b[:, :]),
                    reads=[k_rot[p], ident_b], writes=[tra])
            add("dve", lambda e: e.tensor_tensor(qTs[:, :, :], trb.t[:, 0:512].rearrange("p (h i) -> p h i", h=4),
                                                 qwT, ALU.mult), reads=[trb, cst], writes=[qTs])
            add("act", lambda e: e.copy(kTs[:, :, :], tra.t[:, 0:512].rearrange("p (h i) -> p h i", h=4)),
                reads=[tra], writes=[kTs])
            for h in range(4):
                add("pe", lambda e, h=h: e.matmul(pb[6].t[:, h * 128:(h + 1) * 128], kTs[:, h, :], qTs[:, h, :],
                                                   start=True, stop=True), reads=[kTs, qTs], writes=[pb[6]])
            add("dve", lambda e: e.tensor_tensor(smT[:, :, :], pb[6].t[:, :].rearrange("p (h i) -> p h i", h=4),
                                                 maskT, ALU.mult), reads=[pb[6], cst], writes=[smT])
            for h in range(4):
                add("pe", lambda e, h=h: e.matmul(pb[7].t[:, h * 128:(h + 1) * 128], smT[:, h, :],
                                                   v_bf[p][:, h * 128:(h + 1) * 128], start=True, stop=False),
                    reads=[smT, v_bf[p]], writes=[pb[7]])
                add("pe", lambda e, h=h: e.matmul(pb[7].t[:, h * 128:(h + 1) * 128], qTs[:, h, :], sb_cur[:, h, :],
                                                   start=False, stop=True), reads=[qTs, sb_cur], writes=[pb[7]])
            for h in range(4):
                add("pe", lambda e, h=h: e.matmul(pb[6].t[:, h * 128:(h + 1) * 128], k_rot[p][:, h * 128:(h + 1) * 128],
                                                   v_kw[p][:, h * 128:(h + 1) * 128], start=True, stop=True),
                    reads=[k_rot[p], v_kw[p]], writes=[pb[6]])
            for h in range(4):
                add("dve", lambda e, h=h: e.scalar_tensor_tensor(
                    out=state_f[:, h, :], in0=state_f[:, h, :], scalar=dec128[h], in1=pb[6].t[:, h * 128:(h + 1) * 128],
                    op0=ALU.mult, op1=ALU.add), reads=[state_f, pb[6]], writes=[state_f])
            add("act", lambda e: e.copy(sb_nxt[:, :, :], state_f[:, :, :]), reads=[state_f], writes=[sb_nxt])
            for h in range(4):
                add("dve", lambda e, h=h: e.bn_stats(bnst[:, h, :], pb[7].t[:, h * 128:(h + 1) * 128]),
                    reads=[pb[7]], writes=[bnst])
            for h in range(4):
                add("dve", lambda e, h=h: e.bn_aggr(bnag[:, h, :], bnst[:, h, :]), reads=[bnst], writes=[bnag])
            add("act", lambda e: e.activation(out=stat[:, 8:12], in_=bnag[:, :, 1], func=AF.Sqrt, bias=epsc[:, 1:2],
                                              scale=1.0), reads=[bnag, epsc], writes=[stat.sub(32, 48)])
            add("dve", lambda e: e.reciprocal(stat[:, 8:12], stat[:, 8:12]), reads=[stat.sub(32, 48)],
                writes=[stat.sub(32, 48)])
            for h in range(4):
                add("dve", lambda e, h=h: e.tensor_scalar(o_n[:, h * 128:(h + 1) * 128], pb[7].t[:, h * 128:(h + 1) * 128],
                                                           bnag[:, h, 0:1], stat[:, 8 + h:9 + h], ALU.subtract, ALU.mult),
                    reads=[pb[7], bnag, stat.sub(32, 48)], writes=[o_n])
            add(_pe("sg"), lambda e: e.tensor_tensor(sg[p][:, :], sg[p][:, :], rng_bc, ALU.mult),
                reads=[sg[p], bro.sub(0, 2048)], writes=[sg[p]])
            add(_pe("sg"), lambda e: e.tensor_tensor(ret_o[:, :], o_n[:, :], sg[p][:, :], ALU.mult),
                reads=[o_n, sg[p]], writes=[ret_o])
            for h in range(4):
                add("pe", lambda e, h=h: e.transpose(tra.t[:, h * 128:(h + 1) * 128], ret_o[:, h * 128:(h + 1) * 128],
                                                      ident_b[:, :]), reads=[ret_o, ident_b], writes=[tra])
            add("act", lambda e: e.copy(catT[:, 0:4, j * 128:(j + 1) * 128],
                                        tra.t[:, 0:512].rearrange("p (h i) -> p h i", h=4)),
                reads=[tra], writes=[catT.sub(0, 4096)])

        stage_d(0)
        chk(2)
        stage_d(1)
        stage_e(0, tau * 4 + 0)
        stage_d(2)
        stage_e(1, tau * 4 + 1)
        stage_d(3)
        stage_e(2, tau * 4 + 2)
        stage_e(3, tau * 4 + 3)
        chk(3)

        def conv_chunk(cc):
            rot = [pb[2], pb[3], pb[4], pb[6], pb[7]]
            banks = [rot[(cc * 3 + part) % 5] for part in range(3)]
            b_hc, b_gb, b_gc = banks
            for part in range(3):
                slot = rg_a.get()
                for k in range(8):
                    add("pe", lambda e, part=part, k=k, slot=slot: e.matmul(banks[part].t[:, :], slot[:, k, :], hT[:, k, :],
                                                                             start=(k == 0), stop=(k == 7)),
                        reads=[slot, hT], writes=[banks[part]])
            pc = pbuf.sub(cc * 2056, (cc + 1) * 2056)
            add("act", lambda e: e.copy(gc_sb[:, :], b_gc.t[:, :]), reads=[b_gc], writes=[gc_sb])
            add("dve", lambda e, cc=cc: e.tensor_copy(pbuf[:, cc, 0:2], pbuf[:, cc, 512:514]), reads=[pc], writes=[pc])
            add("dve", lambda e, cc=cc: e.tensor_tensor(pbuf[:, cc, 2:514], b_hc.t[:, :], gc_sb[:, :], ALU.mult),
                reads=[b_hc, gc_sb, pc], writes=[pc])
            cyc = cy.sub(cc * 2048, (cc + 1) * 2048)
            add("act", lambda e, cc=cc: e.activation(out=cy[:, cc, :], in_=pbuf[:, cc, 2:514], func=AF.Identity,
                                                     bias=pvT[:, R_CMB + cc:R_CMB + cc + 1],
                                                     scale=pvT[:, R_CMW + 8 + cc:R_CMW + 9 + cc]),
                reads=[pc, pvT], writes=[cyc])
            add("dve", lambda e, cc=cc: e.scalar_tensor_tensor(out=cy[:, cc, :], in0=pbuf[:, cc, 1:513],
                                                               scalar=pvT[:, R_CMW + 4 + cc:R_CMW + 5 + cc],
                                                               in1=cy[:, cc, :], op0=ALU.mult, op1=ALU.add),
                reads=[pc, pvT, cyc], writes=[cyc])
            add("dve", lambda e, cc=cc: e.scalar_tensor_tensor(out=cy[:, cc, :], in0=pbuf[:, cc, 0:512],
                                                               scalar=pvT[:, R_CMW + cc:R_CMW + cc + 1],
                                                               in1=cy[:, cc, :], op0=ALU.mult, op1=ALU.add),
                reads=[pc, pvT, cyc], writes=[cyc])
            add("dve", lambda e, cc=cc: e.tensor_tensor(cy[:, cc, :], b_gb.t[:, :], cy[:, cc, :], ALU.mult),
                reads=[b_gb, cyc], writes=[cyc])
            add("act", lambda e, cc=cc: e.activation(out=sq[:, :], in_=cy[:, cc, :], func=AF.Square),
                reads=[cyc], writes=[sq])
            add("pe", lambda e, cc=cc: e.matmul(pb[5].t[:, :], ones_b[:, :], sq[:, :], start=(cc == 0), stop=(cc == 3)),
                reads=[ones_b, sq], writes=[pb[5]])
        for cc in range(4):
            conv_chunk(cc)
        add("act", lambda e: e.activation(out=rstd_bc[:, :], in_=pb[5].t[:, :], func=AF.Sqrt, bias=epsc[:, 3:4], scale=1.0),
            reads=[pb[5], epsc], writes=[rstd_bc])
        add("dve", lambda e: e.reciprocal(rstd_bc[:, :], rstd_bc[:, :]), reads=[rstd_bc], writes=[rstd_bc])
        add("dve", lambda e: e.tensor_scalar(rstd_bc[:, :], rstd_bc[:, :], float(np.sqrt(512.0)), None, ALU.mult),
            reads=[rstd_bc], writes=[rstd_bc])
        for cc in range(4):
            add("dve", lambda e, cc=cc: e.scalar_tensor_tensor(out=catT[:, 4 + cc, :], in0=cy[:, cc, :],
                                                               scalar=pvT[:, R_CNG + cc:R_CNG + cc + 1],
                                                               in1=rstd_bc[:, :], op0=ALU.mult, op1=ALU.mult),
                reads=[cy.sub(cc * 2048, (cc + 1) * 2048), pvT, rstd_bc], writes=[catT.sub((4 + cc) * 1024, (5 + cc) * 1024)])

        def wout_block(j):
            for hf in range(2):
                bank = [pb[6], pb[7], pb[2], pb[3]][(j * 2 + hf) % 4]
                for k in range(8):
                    add("pe", lambda e, k=k, hf=hf, bank=bank: e.matmul(bank.t[:, :], catT[:, k, j * 128:(j + 1) * 128],
                                                                        w_outs[:, hf, k, :], start=(k == 0), stop=(k == 7)),
                        reads=[catT, w_outs.sub(hf * 8192, (hf + 1) * 8192)], writes=[bank])
                xr = x_res.sub(j * 4096 + hf * 2048, j * 4096 + (hf + 1) * 2048)
                add("dve", lambda e, hf=hf, bank=bank: e.tensor_tensor(gtmp[:, :], bank.t[:, :],
                                                                       gt1_bc[:, hf * 512:(hf + 1) * 512], ALU.mult),
                    reads=[bank, bro], writes=[gtmp])
                add(_pe("res"), lambda e, hf=hf: e.tensor_tensor(x_res[:, j, hf * 512:(hf + 1) * 512],
                                                             x_res[:, j, hf * 512:(hf + 1) * 512], gtmp[:, :], ALU.add),
                    reads=[xr, gtmp], writes=[xr])

        chk(4)
        for j in range(4):
            wout_block(j)
        chk(5)

        norm_to_hT(16)
        chk(6)

        def up_chunk(c):
            slot = rg_u.get()
            pr = c % 3
            banks = [pb[2 + 2 * pr], pb[3 + 2 * pr]]
            for a in range(2):
                for k in range(8):
                    add("pe", lambda e, a=a, k=k, slot=slot: e.matmul(banks[a].t[:, :], slot[:, k, a, :], hT[:, k, :],
                                                                      start=(k == 0), stop=(k == 7)),
                        reads=[slot, hT], writes=[banks[a]])
            us = u_sb[c % 3]
            ys = yab[c % 3]
            uh = uhalo.sub(c * 16, (c + 1) * 16)
            add(_pe("halo"), lambda e, c=c, us=us: e.tensor_copy(us[:, :, 0:2], uhalo[:, c, :, :]), reads=[uh], writes=[us])
            usa = [us.sub(a * 2056, (a + 1) * 2056) for a in range(2)]
            ysa = [ys.sub(a * 2048, (a + 1) * 2048) for a in range(2)]
            for a in range(2):
                add("act", lambda e, a=a, us=us: e.copy(us[:, a, 2:514], banks[a].t[:, :]), reads=[banks[a]], writes=[usa[a]])
            add(_pe("halo"), lambda e, c=c, us=us: e.tensor_copy(uhalo[:, c, :, :], us[:, :, 512:514]), reads=[us], writes=[uh])
            for a in range(2):
                ch = a * NFF + c
                add("act", lambda e, a=a, ch=ch, us=us, ys=ys: e.activation(
                    out=ys[:, a, :], in_=us[:, a, 2:514], func=AF.Identity,
                    bias=pvT[:, R_CFB + ch:R_CFB + ch + 1], scale=pvT[:, R_CFW + 88 + ch:R_CFW + 89 + ch]),
                    reads=[usa[a], pvT], writes=[ysa[a]])
            for a in range(2):
                ch = a * NFF + c
                add("dve", lambda e, a=a, ch=ch, us=us, ys=ys: e.scalar_tensor_tensor(
                    out=ys[:, a, :], in0=us[:, a, 1:513], scalar=pvT[:, R_CFW + 44 + ch:R_CFW + 45 + ch],
                    in1=ys[:, a, :], op0=ALU.mult, op1=ALU.add), reads=[usa[a], pvT, ysa[a]], writes=[ysa[a]])
                add("dve", lambda e, a=a, ch=ch, us=us, ys=ys: e.scalar_tensor_tensor(
                    out=ys[:, a, :], in0=us[:, a, 0:512], scalar=pvT[:, R_CFW + ch:R_CFW + ch + 1],
                    in1=ys[:, a, :], op0=ALU.mult, op1=ALU.add), reads=[usa[a], pvT, ysa[a]], writes=[ysa[a]])
            return ys

        def up_tail(c, ys):
            add("act", lambda e: e.activation(out=ys[:, 0, :], in_=ys[:, 0, :], func=AF.Silu),
                reads=[ys.sub(0, 2048)], writes=[ys.sub(0, 2048)])
            add(_pe("gmul"), lambda e: e.tensor_tensor(gT[:, c, :], ys[:, 0, :], ys[:, 1, :], ALU.mult),
                reads=[ys], writes=[gT.sub(c * 1024, (c + 1) * 1024)])

        if tau + 1 < ntiles:
            load_mixer_weights(False)
        prev = None
        for c in range(NFF):
            ys_c = up_chunk(c)
            if prev is not None:
                up_tail(*prev)
            prev = (c, ys_c)
        up_tail(*prev)
        chk(7)

        acc_banks = {0: [pb[2], pb[3], pb[4], pb[5]], 1: [pb[6], pb[7], pb[2], pb[3]]}
        for hf in range(2):
            for cg in range(11):
                slot = rg_d.get()
                for j in range(4):
                    bank = acc_banks[hf][j]
                    for cl in range(2):
                        c = cg * 2 + cl
                        add("pe", lambda e, j=j, cl=cl, c=c, slot=slot, bank=bank: e.matmul(
                            bank.t[:, :], gT[:, c, j * 128:(j + 1) * 128], slot[:, cl, :],
                            start=(c == 0), stop=(c == NFF - 1)), reads=[gT, slot], writes=[bank])
            for j in range(4):
                bank = acc_banks[hf][j]
                xr = x_res.sub(j * 4096 + hf * 2048, j * 4096 + (hf + 1) * 2048)
                add("dve", lambda e, hf=hf, bank=bank: e.tensor_tensor(ftmp[:, :], bank.t[:, :],
                                                                       gt2_bc[:, hf * 512:(hf + 1) * 512], ALU.mult),
                    reads=[bank, bro], writes=[ftmp])
                add(_pe("res"), lambda e, j=j, hf=hf: e.tensor_tensor(x_res[:, j, hf * 512:(hf + 1) * 512],
                                                                  x_res[:, j, hf * 512:(hf + 1) * 512], ftmp[:, :], ALU.add),
                    reads=[xr, ftmp], writes=[xr])

        chk(8)
        def final_block(j):
            xr = x_res.sub(j * 4096, (j + 1) * 4096)
            rms_rstd32(xr, x_res[:, j, :], 16 + j)
            add("dve", lambda e, j=j: e.scalar_tensor_tensor(out=x_res[:, j, :], in0=x_res[:, j, :],
                                                             scalar=stat[:, 16 + j:17 + j], in1=fg_bc,
                                                             op0=ALU.mult, op1=ALU.mult),
                reads=[xr, stat.sub((16 + j) * 4, (17 + j) * 4), bro], writes=[xr])
            add("sp", lambda e, j=j: e.dma_start(out=y_d[t0 + j * 128:t0 + (j + 1) * 128, :], in_=x_res[:, j, :]),
                reads=[xr], writes=[dram("y", t0 + j * 128, t0 + (j + 1) * 128)], dma=True)

        for j in range(4):
            final_block(j)

    try:
        if kst0:
            raise _Stop()
        for tau in range(ntiles):
            tile(tau)
    except _Stop:
        pass

    S.emit({"sp": ["sp"]})
    return nc


_CACHE = {}


def kernel(x, c, positions, w_ada, b_ada, norm1_g, w_in, conv_mix_w, conv_mix_b, ret_norm_g, conv_norm_g,
           w_out, norm2_g, w_up, conv_ffn_w, conv_ffn_b, w_down, final_g):
    f = lambda a: np.ascontiguousarray(np.asarray(a), dtype=np.float32)
    x, c, w_ada, b_ada, w_in, w_out, w_up, w_down = map(f, (x, c, w_ada, b_ada, w_in, w_out, w_up, w_down))
    positions = np.ascontiguousarray(np.asarray(positions), dtype=np.int32)
    cst, dec128 = _host_consts()
    if "nc" not in _CACHE:
        _CACHE["nc"] = build_nc(dec128)
    nc = _CACHE["nc"]

    def kmaj(w):
        K, N = w.shape
        return np.ascontiguousarray(w.reshape(K // 128, 128, N).transpose(1, 0, 2))

    wada_l = np.ascontiguousarray(kmaj(w_ada).reshape(128, 8, 12, 512).transpose(2, 0, 1, 3))
    win = kmaj(w_in)
    winb_l = np.ascontiguousarray(win[:, :, 0:2048].reshape(128, 8, 4, 512).transpose(2, 0, 1, 3))
    wina_l = np.ascontiguousarray(win[:, :, 2048:3584].reshape(128, 8, 12, 128).transpose(2, 0, 1, 3))
    wout_l = np.ascontiguousarray(kmaj(w_out).reshape(128, 8, 2, 512).transpose(2, 0, 1, 3))
    wup = kmaj(w_up).reshape(128, 8, 2, NFF, 128)
    wup_l = np.ascontiguousarray(wup.transpose(3, 0, 1, 2, 4))
    wdn = w_down.reshape(11, 2, 128, 2, 512)
    wdn_l = np.ascontiguousarray(wdn.transpose(3, 0, 2, 1, 4))

    pv = np.zeros((R_TOT, 128), np.float32)
    pv[R_BADA:R_BADA + 48] = b_ada.reshape(48, 128)
    pv[R_G1:R_G1 + 8] = f(norm1_g).reshape(8, 128)
    pv[R_G2:R_G2 + 8] = f(norm2_g).reshape(8, 128)
    pv[R_CMW:R_CMW + 12] = f(conv_mix_w).reshape(12, 128)
    pv[R_CMB:R_CMB + 4] = f(conv_mix_b).reshape(4, 128)
    pv[R_CNG:R_CNG + 4] = f(conv_norm_g).reshape(4, 128)
    pv[R_CFW:R_CFW + 132] = f(conv_ffn_w).reshape(132, 128)
    pv[R_CFB:R_CFB + 44] = f(conv_ffn_b).reshape(44, 128)
    brow = np.concatenate([f(ret_norm_g), f(final_g), b_ada[2048:3072], b_ada[5120:6144]])
    bro = np.ascontiguousarray(np.broadcast_to(brow[None, :], (128, C_TOT)))

    in_maps = []
    for b in range(NB):
        pvb = pv.copy()
        pvb[R_C:R_C + 8] = c[b].reshape(8, 128)
        in_maps.append({
            "x": x[b], "pos": positions[b].reshape(32, 128), "pv": pvb, "bro": bro, "cst": cst,
            "w_ada": wada_l, "w_in_b": winb_l, "w_in_a": wina_l, "w_out": wout_l, "w_up": wup_l, "w_down": wdn_l,
        })
    res = run_bass_kernel_spmd(nc, in_maps, core_ids=list(range(NB)))
    return np.stack([np.asarray(r["y"], dtype=np.float32) for r in res.results], axis=0)
```

```python
import contextlib
import numpy as np
import concourse.bass as bass
import concourse.mybir as mybir
from concourse.bass_utils import run_bass_kernel_spmd

F32 = mybir.dt.float32
BF16 = mybir.dt.bfloat16
I32 = mybir.dt.int32
AF = mybir.ActivationFunctionType
ALU = mybir.AluOpType

ENGS = ("pe", "act", "dve", "pool", "sp")
import os as _os
_KP = _os.environ.get("KPOOL", "rot,halo,gmul,res,sg").split(",")


def _pe(group):
    return "pool" if group in _KP else "dve"
DMA_POOL = 16

D = 1024
SEQ = 4096
NB = 8
T = 512
NT = SEQ // T
DFF = 2816
NFF = DFF // 128
EPS = 1e-6
HEADS = 4


class Buf:
    def __init__(self, t, space, lo, hi):
        self.t = t
        self.space = space
        self.lo = lo
        self.hi = hi

    def __getitem__(self, k):
        return self.t[k]

    def sub(self, lo, hi):
        return Buf(self.t, self.space, self.lo + lo, self.lo + hi)


class _Op:
    __slots__ = ("eng", "fn", "idx", "waits", "inc", "dma", "dsem", "dcnt", "cnt", "selfwait")

    def __init__(self, eng, fn, dma):
        self.eng = eng
        self.fn = fn
        self.dma = dma
        self.waits = {}
        self.inc = False
        self.dsem = None
        self.dcnt = 0
        self.cnt = 0
        self.selfwait = None


class Sched:
    def __init__(self, nc):
        self.nc = nc
        self.ops = {e: [] for e in ENGS}
        self.spaces = {}
        self.dma_rr = {e: 0 for e in ENGS}
        self.dma_cnt = {}
        self.dma_last = {}
        self.psum_rd = {}

    def _dep(self, op, prod):
        if prod is None or prod is op:
            return
        if prod.dma:
            key = ("d", prod.eng, prod.dsem)
            val = prod.dcnt * 16
            cur = op.waits.get(key)
            op.waits[key] = val if cur is None else max(cur, val)
            return
        if prod.eng == "pe" and op.eng == "pe":
            return
        prod.inc = True
        key = ("c", prod.eng)
        cur = op.waits.get(key)
        if cur is None or cur.idx < prod.idx:
            op.waits[key] = prod

    def add(self, eng, fn, reads=(), writes=(), dma=False):
        op = _Op(eng, fn, dma)
        op.idx = len(self.ops[eng])
        self.ops[eng].append(op)
        if dma:
            slot = self.dma_rr[eng] % DMA_POOL
            self.dma_rr[eng] += 1
            k = (eng, slot)
            self.dma_cnt[k] = self.dma_cnt.get(k, 0) + 1
            op.dsem = slot
            op.dcnt = self.dma_cnt[k]
            prev = self.dma_last.get(k)
            if prev is not None:
                op.selfwait = (slot, prev.dcnt * 16)
            self.dma_last[k] = op
        for b in reads:
            if b.space == "psum":
                for bank in range(b.lo // 2048, (b.hi - 1) // 2048 + 1):
                    rd = self.psum_rd.setdefault(bank, {})
                    for oe, oop in rd.items():
                        if oe != eng:
                            self._dep(op, oop)
                    rd[eng] = op
            recs = self.spaces.setdefault(b.space, [])
            for r in recs:
                if r[0] < b.hi and b.lo < r[1]:
                    self._dep(op, r[2])
                    if dma:
                        r[4].append(op)
                    else:
                        r[3][eng] = op
        for b in writes:
            recs = self.spaces.setdefault(b.space, [])
            keep = []
            for r in recs:
                if r[0] < b.hi and b.lo < r[1]:
                    self._dep(op, r[2])
                    for rd in r[3].values():
                        self._dep(op, rd)
                    for rd in r[4]:
                        self._dep(op, rd)
                    if r[0] < b.lo:
                        keep.append([r[0], b.lo, r[2], dict(r[3]), list(r[4])])
                    if b.hi < r[1]:
                        keep.append([b.hi, r[1], r[2], dict(r[3]), list(r[4])])
                else:
                    keep.append(r)
            keep.append([b.lo, b.hi, op, {}, []])
            self.spaces[b.space] = keep
        return op

    def emit(self, final_waits):
        nc = self.nc
        for e in ENGS:
            c = 0
            for op in self.ops[e]:
                if op.inc and not op.dma:
                    c += 1
                op.cnt = c
        with contextlib.ExitStack() as st:
            csem = {e: st.enter_context(nc.semaphore("c_" + e)) for e in ENGS}
            dsem = {}
            for e in ENGS:
                for s in range(min(DMA_POOL, self.dma_rr[e])):
                    dsem[(e, s)] = st.enter_context(nc.semaphore("d_%s_%d" % (e, s)))
            block = st.enter_context(nc.Block())

            def run(e, engine):
                waited = {}
                for op in self.ops[e]:
                    for key, val in op.waits.items():
                        if key[0] == "d":
                            sem = dsem[(key[1], key[2])]
                            v = val
                        else:
                            sem = csem[key[1]]
                            v = val.cnt
                        if waited.get(key, 0) >= v:
                            continue
                        waited[key] = v
                        engine.wait_ge(sem, v)
                    if op.dma:
                        if op.selfwait is not None:
                            key = ("d", e, op.selfwait[0])
                            if waited.get(key, 0) < op.selfwait[1]:
                                waited[key] = op.selfwait[1]
                                engine.wait_ge(dsem[(e, op.selfwait[0])], op.selfwait[1])
                        op.fn(engine).then_inc(dsem[(e, op.dsem)], 16)
                    else:
                        ins = op.fn(engine)
                        if op.inc:
                            ins.then_inc(csem[e], 1)
                if e in final_waits:
                    for (qe, slot), op in self.dma_last.items():
                        if qe in final_waits[e]:
                            engine.wait_ge(dsem[(qe, slot)], op.dcnt * 16)

            @block.tensor
            def _(t):
                run("pe", t)

            @block.scalar
            def _(a):
                run("act", a)

            @block.vector
            def _(v):
                run("dve", v)

            @block.gpsimd
            def _(g):
                run("pool", g)

            @block.sync
            def _(s):
                run("sp", s)


class Arena:
    def __init__(self, nc, base, limit):
        self.nc = nc
        self.off = base
        self.limit = limit
        self.n = 0

    def alloc(self, shape, dtype, at=None):
        size = mybir.dt.size(dtype)
        for s in shape[1:]:
            size *= s
        if at is None:
            at = (self.off + 63) // 64 * 64
            self.off = at + size
        assert at + size <= self.limit, ("SBUF overflow", at + size, self.limit)
        self.n += 1
        t = self.nc.alloc_sbuf_tensor_at("sb%d" % self.n, list(shape), dtype, offset=at)
        return Buf(t, "sbuf", at, at + size)


R_C = 0
R_BADA = 8
R_G1 = 56
R_G2 = 64
R_CMW = 72
R_CMB = 84
R_CNG = 88
R_CFW = 92
R_CFB = 224
R_TOT = 268
C_RNG = 0
C_FG = 512
C_BG1 = 1536
C_BG2 = 2560
C_TOT = 3584


def _host_consts():
    h = np.arange(HEADS, dtype=np.float64)
    gam = 1.0 - 2.0 ** (-5.0 - h)
    lg = np.log(gam)
    i = np.arange(128)
    ci, cj = i[:, None] // 64, i[None, :] // 64
    dist = i[:, None] - i[None, :]
    W = np.zeros((HEADS, 128, 128))
    for hh in range(HEADS):
        same = np.exp(np.abs(dist) * lg[hh])
        causal = np.exp(dist * lg[hh])
        W[hh] = np.where(ci == cj, same, np.where(ci > cj, causal, 0.0))
    qw = np.exp((i[None, :] + 1) * lg[:, None])
    maskT = np.transpose(W / qw[:, :, None], (2, 0, 1))
    qwT = np.broadcast_to((qw * 128 ** -0.5)[None], (128, HEADS, 128))
    kw = np.exp((127 - i)[:, None] * lg[None, :])
    dec = np.exp(128 * lg)
    invf = (np.float32(10000.0) ** (-(np.arange(0, 128, 2, dtype=np.float32)) / np.float32(128))).astype(np.float32)
    cst = np.zeros((128, 128 + 512 + 512 + 4 + 64), np.float32)
    cst[:, 0:128] = np.eye(128)
    cst[:, 128:640] = maskT.reshape(128, 512)
    cst[:, 640:1152] = qwT.reshape(128, 512)
    cst[:, 1152:1156] = kw
    cst[:, 1156:1220] = invf[None, :]
    return cst, [float(v) for v in dec]


N_CST = 1220


class Ring:
    def __init__(self, slots, total, issue_fn):
        self.slots = slots
        self.total = total
        self.issue_fn = issue_fn
        self.issued = 0
        self.consumed = 0

    def prefetch(self):
        lim = min(self.total, self.consumed + len(self.slots))
        while self.issued < lim:
            self.issue_fn(self.issued, self.slots[self.issued % len(self.slots)])
            self.issued += 1

    def get(self):
        self.prefetch()
        slot = self.slots[self.consumed % len(self.slots)]
        self.consumed += 1
        return slot


class _Stop(Exception):
    pass


def build_nc(dec128, debug=False, kstop=99, ntiles=NT):
    def chk(n):
        if kstop <= n:
            raise _Stop()

    nc = bass.Bass("TRN2", target_bir_lowering=False)

    def din(name, shape, dt=F32):
        return nc.dram_tensor(name, list(shape), dt, kind="ExternalInput").ap()

    x_d = din("x", [SEQ, D])
    pos_d = din("pos", [32, 128], I32)
    pv_d = din("pv", [R_TOT, 128])
    bro_d = din("bro", [128, C_TOT])
    cst_d = din("cst", [128, N_CST])
    wada_d = din("w_ada", [12, 128, 8, 512])
    winb_d = din("w_in_b", [4, 128, 8, 512])
    wina_d = din("w_in_a", [12, 128, 8, 128])
    wout_d = din("w_out", [2, 128, 8, 512])
    wup_d = din("w_up", [NFF, 128, 8, 2, 128])
    wdn_d = din("w_down", [2, 11, 128, 2, 512])
    y_d = nc.dram_tensor("y", [SEQ, D], F32, kind="ExternalOutput").ap()

    S = Sched(nc)
    A = Arena(nc, 16640, 229376)
    add = S.add

    tra = Buf(nc.alloc_psum_tensor("tra", [128, 1024], BF16), "psum", 0, 2048)
    trb = Buf(nc.alloc_psum_tensor("trb", [128, 1024], BF16), "psum", 2048, 4096)
    pb = {}
    for i in range(2, 8):
        pb[i] = Buf(nc.alloc_psum_tensor("pb%d" % i, [128, 512], F32), "psum", i * 2048, (i + 1) * 2048)

    cst = A.alloc([128, N_CST], F32)
    ident_f = cst.t[:, 0:128]
    maskT = cst.t[:, 128:640].rearrange("p (h i) -> p h i", h=4)
    qwT = cst.t[:, 640:1152].rearrange("p (h i) -> p h i", h=4)
    kwtab = cst.t[:, 1152:1156]
    invf = cst.t[:, 1156:1220]
    ident_b = A.alloc([128, 128], BF16)
    ones_b = A.alloc([128, 128], BF16)
    epsc = A.alloc([128, 4], F32)
    pvT = A.alloc([128, R_TOT + 4], F32)
    modT = A.alloc([128, 48], F32)
    ab = A.alloc([128, 32], F32)
    bro = A.alloc([128, C_TOT], F32)
    posf = A.alloc([128, 32], F32)
    x_res = A.alloc([128, 4, D], F32)
    hT = A.alloc([128, 8, T], BF16)
    w_inb = A.alloc([128, 4, 8, 512], BF16)
    w_outs = A.alloc([128, 2, 8, 512], BF16)
    NRA, NRU, NRD = 3, 4, 6
    ring_a = [A.alloc([128, 8, 128], BF16) for _ in range(NRA)]
    ring_u = [A.alloc([128, 8, 2, 128], BF16) for _ in range(NRU)]
    ring_d = [A.alloc([128, 2, 512], BF16) for _ in range(NRD)]
    state_f = A.alloc([128, 4, 128], F32)
    state_b = [A.alloc([128, 4, 128], BF16) for _ in range(2)]
    pbuf = A.alloc([128, 4, 514], F32)
    uhalo = A.alloc([128, NFF, 2, 2], F32)
    stat = A.alloc([128, 64], F32)
    junk = A.alloc([128, D], BF16)

    region0 = A.off
    cs_t = A.alloc([128, 4, 64], F32)
    sn_t = A.alloc([128, 4, 64], F32)
    rtmp = A.alloc([128, 4, 64], F32)
    rki = A.alloc([128, 4, 64], I32)
    xn_at = (A.off + 63) // 64 * 64
    qf = [A.alloc([128, 512], F32) for _ in range(2)]
    kf = [A.alloc([128, 512], F32) for _ in range(2)]
    xn = A.alloc([128, 4, D], BF16, at=xn_at)
    ta = [A.alloc([128, 512], F32) for _ in range(2)]
    tb = [A.alloc([128, 512], F32) for _ in range(2)]
    q_rot = [A.alloc([128, 512], BF16) for _ in range(2)]
    k_rot = [A.alloc([128, 512], BF16) for _ in range(2)]
    v_bf = [A.alloc([128, 512], BF16) for _ in range(2)]
    v_kw = [A.alloc([128, 512], BF16) for _ in range(2)]
    sg = [A.alloc([128, 512], F32) for _ in range(2)]
    qTs = A.alloc([128, 4, 128], BF16)
    kTs = A.alloc([128, 4, 128], BF16)
    smT = A.alloc([128, 4, 128], BF16)
    o_n = A.alloc([128, 512], F32)
    ret_o = A.alloc([128, 512], BF16)
    bnst = A.alloc([128, 4, 6], F32)
    bnag = A.alloc([128, 4, 2], F32)
    catT = A.alloc([128, 8, T], BF16)
    gc_sb = A.alloc([128, 512], F32)
    cy = A.alloc([128, 4, 512], F32)
    sq = A.alloc([128, 512], BF16)
    rstd_bc = A.alloc([128, 512], F32)
    gtmp = A.alloc([128, 512], F32)
    region_m_end = A.off
    A.off = region0
    u_sb = [A.alloc([128, 2, 514], F32) for _ in range(3)]
    yab = [A.alloc([128, 2, 512], F32) for _ in range(3)]
    gT = A.alloc([128, NFF, T], BF16)
    ftmp = A.alloc([128, 512], F32)
    region_f_end = A.off
    A.off = region0
    pstage = A.alloc([128, 3, 128], F32)
    posi = A.alloc([32, 128], I32)
    posff = A.alloc([32, 128], F32)
    s_col = A.alloc([128, 8], F32)
    s_bc = A.alloc([128, 8, 128], F32)
    wada = [A.alloc([128, 8, 512], F32) for _ in range(2)]
    A.off = max(region_m_end, region_f_end, A.off)
    assert A.off <= A.limit, A.off

    def dram(name, lo=0, hi=1):
        return Buf(None, "dram_" + name, lo, hi)

    add("sp", lambda e: e.dma_start(out=cst[:, :], in_=cst_d[:, :]), writes=[cst], dma=True)
    add("sp", lambda e: e.dma_start(out=bro[:, :], in_=bro_d[:, :]), writes=[bro], dma=True)
    add("dve", lambda e: e.memset(pstage[:, :, :], 0.0), writes=[pstage])
    for g in range(3):
        r0, r1 = g * 128, min(R_TOT, (g + 1) * 128)
        add("sp", lambda e, g=g, r0=r0, r1=r1: e.dma_start(out=pstage[0:r1 - r0, g, :], in_=pv_d[r0:r1, :]),
            writes=[pstage.sub(g * 512, (g + 1) * 512)], dma=True)
    add("sp", lambda e: e.dma_start(out=posi[:, :], in_=pos_d[:, :]), writes=[posi], dma=True)
    add("dve", lambda e: e.tensor_copy(ident_b[:, :], ident_f), reads=[cst], writes=[ident_b])
    add("dve", lambda e: e.memset(ones_b[:, :], 1.0), writes=[ones_b])
    add("dve", lambda e: e.memset(epsc[:, 0:1], 1024 * EPS), writes=[epsc.sub(0, 4)])
    add("dve", lambda e: e.memset(epsc[:, 1:2], EPS), writes=[epsc.sub(4, 8)])
    add("dve", lambda e: e.memset(epsc[:, 2:3], float(np.pi / 2)), writes=[epsc.sub(8, 12)])
    add("dve", lambda e: e.memset(epsc[:, 3:4], 512 * EPS), writes=[epsc.sub(12, 16)])
    add("dve", lambda e: e.memset(state_f[:, :, :], 0.0), writes=[state_f])
    add("dve", lambda e: e.memset(state_b[0][:, :, :], 0.0), writes=[state_b[0]])
    add("dve", lambda e: e.memset(pbuf[:, :, :], 0.0), writes=[pbuf])
    add("dve", lambda e: e.memset(uhalo[:, :, :, :], 0.0), writes=[uhalo])
    for g in range(3):
        n = min(R_TOT, (g + 1) * 128) - g * 128
        add("pe", lambda e, g=g: e.matmul(pb[2].t[:, g * 128:(g + 1) * 128], pstage[:, g, :], ident_f,
                                            start=True, stop=True),
            reads=[pstage, cst], writes=[pb[2]])
    add("act", lambda e: e.copy(pvT[:, 0:R_TOT], pb[2].t[:, 0:R_TOT]), reads=[pb[2]], writes=[pvT])
    add("dve", lambda e: e.tensor_copy(posff[:, :], posi[:, :]), reads=[posi], writes=[posff])
    add("pe", lambda e: e.matmul(pb[3].t[:, 0:32], posff[:, :], ident_f[0:32, 0:32], start=True, stop=True),
        reads=[posff, cst], writes=[pb[3]])
    add("act", lambda e: e.copy(posf[:, :], pb[3].t[:, 0:32]), reads=[pb[3]], writes=[posf])
    add("act", lambda e: e.activation(out=s_col[:, :], in_=pvT[:, R_C:R_C + 8], func=AF.Silu),
        reads=[pvT], writes=[s_col])
    add("dve", lambda e: e.tensor_copy(s_bc[:, :, :], s_col[:, :].unsqueeze(2).to_broadcast([128, 8, 128])),
        reads=[s_col], writes=[s_bc])
    for g in range(12):
        wb = wada[g % 2]
        add("sp", lambda e, g=g, wb=wb: e.dma_start(out=wb[:, :, :], in_=wada_d[g]), writes=[wb], dma=True)
        if g in (4, 5, 10, 11):
            bank = pb[4 + (g % 2)]
            for k in range(8):
                add("pe", lambda e, k=k, wb=wb, bank=bank: e.matmul(bank.t[:, :], s_bc[:, k, :], wb[:, k, :],
                                                                    start=(k == 0), stop=(k == 7)),
                    reads=[s_bc, wb], writes=[bank])
            c0 = (C_BG1 if g < 6 else C_BG2) + (g % 2) * 512
            add("dve", lambda e, bank=bank, c0=c0: e.tensor_tensor(bro[:, c0:c0 + 512], bank.t[:, :],
                                                                    bro[:, c0:c0 + 512], ALU.add),
                reads=[bank, bro.sub(c0 * 4, (c0 + 512) * 4)], writes=[bro.sub(c0 * 4, (c0 + 512) * 4)])
        else:
            for m in range(4):
                col = g * 4 + m
                for k in range(8):
                    add("pe", lambda e, k=k, m=m, wb=wb, col=col: e.matmul(
                        pb[6].t[:, col:col + 1], wb[:, k, m * 128:(m + 1) * 128], s_col[:, k:k + 1],
                        start=(k == 0), stop=(k == 7)),
                        reads=[wb, s_col], writes=[pb[6].sub(col * 4, col * 4 + 4)])
    add("dve", lambda e: e.tensor_tensor(modT[:, :], pb[6].t[:, 0:48], pvT[:, R_BADA:R_BADA + 48], ALU.add),
        reads=[pb[6], pvT], writes=[modT])
    add("dve", lambda e: e.scalar_tensor_tensor(out=ab[:, 0:8], in0=modT[:, 8:16], scalar=1.0, in1=pvT[:, R_G1:R_G1 + 8],
                                                op0=ALU.add, op1=ALU.mult), reads=[modT, pvT], writes=[ab.sub(0, 32)])
    add("dve", lambda e: e.tensor_copy(ab[:, 8:16], modT[:, 0:8]), reads=[modT], writes=[ab.sub(32, 64)])
    add("dve", lambda e: e.scalar_tensor_tensor(out=ab[:, 16:24], in0=modT[:, 32:40], scalar=1.0,
                                                in1=pvT[:, R_G2:R_G2 + 8], op0=ALU.add, op1=ALU.mult),
        reads=[modT, pvT], writes=[ab.sub(64, 96)])
    add("dve", lambda e: e.tensor_copy(ab[:, 24:32], modT[:, 24:32]), reads=[modT], writes=[ab.sub(96, 128)])
    add("dve", lambda e: e.tensor_scalar(bro[:, C_FG:C_FG + 1024], bro[:, C_FG:C_FG + 1024], 32.0, None, ALU.mult),
        reads=[bro.sub(C_FG * 4, (C_FG + 1024) * 4)], writes=[bro.sub(C_FG * 4, (C_FG + 1024) * 4)])

    rng_bc = bro.t[:, C_RNG:C_RNG + 512]
    kst0 = kstop <= 0
    fg_bc = bro.t[:, C_FG:C_FG + 1024]
    gt1_bc = bro.t[:, C_BG1:C_BG1 + 1024]
    gt2_bc = bro.t[:, C_BG2:C_BG2 + 1024]

    sc_inb = nc.dram_tensor("sc_inb", [4, 128, 8, 512], BF16).ap()
    sc_ina = nc.dram_tensor("sc_ina", [12, 128, 8, 128], BF16).ap()
    sc_out = nc.dram_tensor("sc_out", [2, 128, 8, 512], BF16).ap()
    sc_up = nc.dram_tensor("sc_up", [NFF, 128, 8, 2, 128], BF16).ap()
    sc_dn = nc.dram_tensor("sc_dn", [2, 11, 128, 2, 512], BF16).ap()

    def stream(first, name, idx, dst_fn, src_f32, src_bf, wbuf):
        scr = Buf(None, "dram_" + name, idx, idx + 1)
        if first:
            add("pool", lambda e: e.dma_start(out=dst_fn(), in_=src_f32), writes=[wbuf], dma=True)
            add("sp", lambda e: e.dma_start(out=src_bf, in_=dst_fn()), reads=[wbuf], writes=[scr], dma=True)
        else:
            add("sp", lambda e: e.dma_start(out=dst_fn(), in_=src_bf), reads=[scr], writes=[wbuf], dma=True)

    def _issue_a(g, slot):
        i = g % 12
        cc, part = i // 3, i % 3
        k = part * 4 + cc
        stream(g < 12, "ina", k, lambda: slot[:, :, :], wina_d[k], sc_ina[k], slot)

    def _issue_u(g, slot):
        c = g % NFF
        stream(g < NFF, "up", c, lambda: slot[:, :, :, :], wup_d[c], sc_up[c], slot)

    def _issue_d(g, slot):
        i = g % 22
        hf, cg = i // 11, i % 11
        stream(g < 22, "dn", i, lambda: slot[:, :, :], wdn_d[hf, cg], sc_dn[hf, cg], slot)

    rg_a = Ring(ring_a, 12 * ntiles, _issue_a)
    rg_u = Ring(ring_u, NFF * ntiles, _issue_u)
    rg_d = Ring(ring_d, 22 * ntiles, _issue_d)

    def load_mixer_weights(first):
        for g in range(4):
            stream(first, "inb", g, lambda g=g: w_inb[:, g, :, :], winb_d[g], sc_inb[g], w_inb.sub(g * 8192, (g + 1) * 8192))
        for hf in range(2):
            stream(first, "out", hf, lambda hf=hf: w_outs[:, hf, :, :], wout_d[hf], sc_out[hf],
                   w_outs.sub(hf * 8192, (hf + 1) * 8192))

    def rms_rstd32(src_buf, src_ap, col):
        add("act", lambda e: e.activation(out=junk[:, :], in_=src_ap, func=AF.Square, accum_out=stat[:, col:col + 1]),
            reads=[src_buf], writes=[junk, stat.sub(col * 4, col * 4 + 4)])
        add("act", lambda e: e.activation(out=stat[:, col:col + 1], in_=stat[:, col:col + 1], func=AF.Sqrt,
                                          bias=epsc[:, 0:1], scale=1.0),
            reads=[stat.sub(col * 4, col * 4 + 4), epsc], writes=[stat.sub(col * 4, col * 4 + 4)])
        add("dve", lambda e: e.reciprocal(stat[:, col:col + 1], stat[:, col:col + 1]),
            reads=[stat.sub(col * 4, col * 4 + 4)], writes=[stat.sub(col * 4, col * 4 + 4)])

    def norm_to_hT(aoff):
        for j in range(4):
            rms_rstd32(x_res.sub(j * 4096, (j + 1) * 4096), x_res[:, j, :], j)
            add("dve", lambda e, j=j: e.tensor_scalar(xn[:, j, :], x_res[:, j, :], stat[:, j:j + 1], 32.0,
                                                       ALU.mult, ALU.mult),
                reads=[x_res.sub(j * 4096, (j + 1) * 4096), stat.sub(j * 4, j * 4 + 4)],
                writes=[xn.sub(j * 2048, (j + 1) * 2048)])
        chk(0.6)
        for c in range(8):
            if c == 1:
                chk(0.7)
            if c == 2:
                chk(0.8)
            tr = tra if c % 2 == 0 else trb
            for j in range(4):
                add("pe", lambda e, c=c, j=j, tr=tr: e.transpose(
                    tr.t[:, j * 128:(j + 1) * 128], xn[:, j, c * 128:(c + 1) * 128], ident_b[:, :]),
                    reads=[xn.sub(j * 2048, (j + 1) * 2048), ident_b], writes=[tr])
            add("act", lambda e, c=c, tr=tr: e.activation(
                out=hT[:, c, :], in_=tr.t[:, 0:512], func=AF.Identity,
                bias=ab[:, aoff + 8 + c:aoff + 9 + c], scale=ab[:, aoff + c:aoff + c + 1]),
                reads=[tr, ab], writes=[hT.sub(c * 1024, (c + 1) * 1024)])

    def tile(tau):
        t0 = tau * T
        for j in range(4):
            add("sp", lambda e, j=j: e.dma_start(out=x_res[:, j, :], in_=x_d[t0 + j * 128:t0 + (j + 1) * 128, :]),
                writes=[x_res.sub(j * 4096, (j + 1) * 4096)], dma=True)
        if tau == 0:
            load_mixer_weights(True)
        rg_a.prefetch()
        rg_u.prefetch()
        rg_d.prefetch()
        chk(0.2)
        pj = posf.t[:, tau * 4:tau * 4 + 4].unsqueeze(2).to_broadcast([128, 4, 64])
        iv = invf.unsqueeze(1).to_broadcast([128, 4, 64])
        add("dve", lambda e: e.tensor_tensor(cs_t[:, :, :], pj, iv, ALU.mult), reads=[posf, cst], writes=[cs_t])
        add("dve", lambda e: e.tensor_scalar(rtmp[:, :, :], cs_t[:, :, :], float(1.0 / (2 * np.pi)), None, ALU.mult),
            reads=[cs_t], writes=[rtmp])
        add("dve", lambda e: e.tensor_copy(rki[:, :, :], rtmp[:, :, :]), reads=[rtmp], writes=[rki])
        add("dve", lambda e: e.tensor_copy(rtmp[:, :, :], rki[:, :, :]), reads=[rki], writes=[rtmp])
        add("dve", lambda e: e.scalar_tensor_tensor(out=cs_t[:, :, :], in0=rtmp[:, :, :], scalar=-6.28125,
                                                    in1=cs_t[:, :, :], op0=ALU.mult, op1=ALU.add),
            reads=[rtmp, cs_t], writes=[cs_t])
        add("dve", lambda e: e.scalar_tensor_tensor(out=cs_t[:, :, :], in0=rtmp[:, :, :],
                                                    scalar=-0.0019353071795864769, in1=cs_t[:, :, :],
                                                    op0=ALU.mult, op1=ALU.add), reads=[rtmp, cs_t], writes=[cs_t])
        add("dve", lambda e: e.tensor_scalar(cs_t[:, :, :], cs_t[:, :, :], -3.141592, 3.141592, ALU.max, ALU.min),
            reads=[cs_t], writes=[cs_t])
        add("act", lambda e: e.activation(out=sn_t[:, :, :], in_=cs_t[:, :, :], func=AF.Sin), reads=[cs_t], writes=[sn_t])
        add("act", lambda e: e.activation(out=rtmp[:, :, :], in_=cs_t[:, :, :], func=AF.Abs), reads=[cs_t], writes=[rtmp])
        add("act", lambda e: e.activation(out=cs_t[:, :, :], in_=rtmp[:, :, :], func=AF.Sin, scale=-1.0,
                                          bias=epsc[:, 2:3]), reads=[rtmp, epsc], writes=[cs_t])

        chk(0.5)
        norm_to_hT(0)
        chk(1)

        def stage_d(j):
            p = j % 2
            banks = [pb[2], pb[3], pb[4], pb[5]]
            for g in range(4):
                for k in range(8):
                    add("pe", lambda e, g=g, k=k: e.matmul(banks[g].t[:, :], hT[:, k, j * 128:(j + 1) * 128],
                                                            w_inb[:, g, k, :], start=(k == 0), stop=(k == 7)),
                        reads=[hT, w_inb.sub(g * 8192, (g + 1) * 8192)], writes=[banks[g]])
            chk(1.2)
            add("act", lambda e: e.copy(qf[p][:, :], pb[2].t[:, :]), reads=[pb[2]], writes=[qf[p]])
            add("act", lambda e: e.copy(kf[p][:, :], pb[3].t[:, :]), reads=[pb[3]], writes=[kf[p]])
            add("act", lambda e: e.copy(v_bf[p][:, :], pb[4].t[:, :]), reads=[pb[4]], writes=[v_bf[p]])
            chk(1.4)
            for h in range(4):
                add("dve", lambda e, h=h: e.tensor_scalar(v_kw[p][:, h * 128:(h + 1) * 128], pb[4].t[:, h * 128:(h + 1) * 128],
                                                           kwtab[:, h:h + 1], None, ALU.mult),
                    reads=[pb[4], cst], writes=[v_kw[p]])
            chk(1.5)
            add("act", lambda e: e.activation(out=sg[p][:, :], in_=pb[5].t[:, :], func=AF.Silu),
                reads=[pb[5]], writes=[sg[p]])
            chk(1.6)
            cosb = cs_t.t[:, j, :].unsqueeze(1).to_broadcast([128, 4, 64])
            sinb = sn_t.t[:, j, :].unsqueeze(1).to_broadcast([128, 4, 64])
            for src, dst, en, tt in ((qf[p], q_rot[p], "dve", 0), (kf[p], k_rot[p], _pe("rot"), 1)):
                s4 = src.t[:, :].rearrange("p (h s d) -> p h s d", h=4, s=2)
                a4 = ta[tt].t[:, :].rearrange("p (h s d) -> p h s d", h=4, s=2)
                b4 = tb[tt].t[:, :].rearrange("p (h s d) -> p h s d", h=4, s=2)
                d4 = dst.t[:, :].rearrange("p (h s d) -> p h s d", h=4, s=2)
                for sidx in range(2):
                    add(en, lambda e, s4=s4, a4=a4, sidx=sidx: e.tensor_tensor(a4[:, :, sidx, :], s4[:, :, sidx, :],
                                                                               cosb, ALU.mult),
                        reads=[src, cs_t], writes=[ta[tt]])
                add(en, lambda e, s4=s4, b4=b4: e.tensor_tensor(b4[:, :, 0, :], s4[:, :, 1, :], sinb, ALU.mult),
                    reads=[src, sn_t], writes=[tb[tt]])
                add(en, lambda e, s4=s4, b4=b4: e.tensor_tensor(b4[:, :, 1, :], s4[:, :, 0, :], sinb, ALU.mult),
                    reads=[src, sn_t], writes=[tb[tt]])
                add(en, lambda e, a4=a4, b4=b4, d4=d4: e.tensor_tensor(d4[:, :, 0, :], a4[:, :, 0, :], b4[:, :, 0, :],
                                                                       ALU.subtract),
                    reads=[ta[tt], tb[tt]], writes=[dst])
                add(en, lambda e, a4=a4, b4=b4, d4=d4: e.tensor_tensor(d4[:, :, 1, :], a4[:, :, 1, :], b4[:, :, 1, :],
                                                                       ALU.add),
                    reads=[ta[tt], tb[tt]], writes=[dst])

        def stage_e(j, blk):
            p = j % 2
            sb_cur = state_b[blk % 2]
            sb_nxt = state_b[(blk + 1) % 2]
            for h in range(4):
                add("pe", lambda e, h=h: e.transpose(trb.t[:, h * 128:(h + 1) * 128], q_rot[p][:, h * 128:(h + 1) * 128],
                                                      ident_b[:, :]), reads=[q_rot[p], ident_b], writes=[trb])
            for h in range(4):
                add("pe", lambda e, h=h: e.transpose(tra.t[:, h * 128:(h + 1) * 128],
                                                      k_rot[p][:, h * 128:(h + 1) * 128], ident_b[:, :]),
                    reads=[k_rot[p], ident_b], writes=[tra])
            add("dve", lambda e: e.tensor_tensor(qTs[:, :, :], trb.t[:, 0:512].rearrange("p (h i) -> p h i", h=4),
                                                 qwT, ALU.mult), reads=[trb, cst], writes=[qTs])
            add("act", lambda e: e.copy(kTs[:, :, :], tra.t[:, 0:512].rearrange("p (h i) -> p h i", h=4)),
                reads=[tra], writes=[kTs])
            for h in range(4):
                add("pe", lambda e, h=h: e.matmul(pb[6].t[:, h * 128:(h + 1) * 128], kTs[:, h, :], qTs[:, h, :],
                                                   start=True, stop=True), reads=[kTs, qTs], writes=[pb[6]])
            add("dve", lambda e: e.tensor_tensor(smT[:, :, :], pb[6].t[:, :].rearrange("p (h i) -> p h i", h=4),
                                                 maskT, ALU.mult), reads=[pb[6], cst], writes=[smT])
            for h in range(4):
                add("pe", lambda e, h=h: e.matmul(pb[7].t[:, h * 128:(h + 1) * 128], smT[:, h, :],
                                                   v_bf[p][:, h * 128:(h + 1) * 128], start=True, stop=False),
                    reads=[smT, v_bf[p]], writes=[pb[7]])
                add("pe", lambda e, h=h: e.matmul(pb[7].t[:, h * 128:(h + 1) * 128], qTs[:, h, :], sb_cur[:, h, :],
                                                   start=False, stop=True), reads=[qTs, sb_cur], writes=[pb[7]])
            for h in range(4):
                add("pe", lambda e, h=h: e.matmul(pb[6].t[:, h * 128:(h + 1) * 128], k_rot[p][:, h * 128:(h + 1) * 128],
                                                   v_kw[p][:, h * 128:(h + 1) * 128], start=True, stop=True),
                    reads=[k_rot[p], v_kw[p]], writes=[pb[6]])
            for h in range(4):
                add("dve", lambda e, h=h: e.scalar_tensor_tensor(
                    out=state_f[:, h, :], in0=state_f[:, h, :], scalar=dec128[h], in1=pb[6].t[:, h * 128:(h + 1) * 128],
                    op0=ALU.mult, op1=ALU.add), reads=[state_f, pb[6]], writes=[state_f])
            add("act", lambda e: e.copy(sb_nxt[:, :, :], state_f[:, :, :]), reads=[state_f], writes=[sb_nxt])
            for h in range(4):
                add("dve", lambda e, h=h: e.bn_stats(bnst[:, h, :], pb[7].t[:, h * 128:(h + 1) * 128]),
                    reads=[pb[7]], writes=[bnst])
            for h in range(4):
                add("dve", lambda e, h=h: e.bn_aggr(bnag[:, h, :], bnst[:, h, :]), reads=[bnst], writes=[bnag])
            add("act", lambda e: e.activation(out=stat[:, 8:12], in_=bnag[:, :, 1], func=AF.Sqrt, bias=epsc[:, 1:2],
                                              scale=1.0), reads=[bnag, epsc], writes=[stat.sub(32, 48)])
            add("dve", lambda e: e.reciprocal(stat[:, 8:12], stat[:, 8:12]), reads=[stat.sub(32, 48)],
                writes=[stat.sub(32, 48)])
            for h in range(4):
                add("dve", lambda e, h=h: e.tensor_scalar(o_n[:, h * 128:(h + 1) * 128], pb[7].t[:, h * 128:(h + 1) * 128],
                                                           bnag[:, h, 0:1], stat[:, 8 + h:9 + h], ALU.subtract, ALU.mult),
                    reads=[pb[7], bnag, stat.sub(32, 48)], writes=[o_n])
            add(_pe("sg"), lambda e: e.tensor_tensor(sg[p][:, :], sg[p][:, :], rng_bc, ALU.mult),
                reads=[sg[p], bro.sub(0, 2048)], writes=[sg[p]])
            add(_pe("sg"), lambda e: e.tensor_tensor(ret_o[:, :], o_n[:, :], sg[p][:, :], ALU.mult),
                reads=[o_n, sg[p]], writes=[ret_o])
            for h in range(4):
                add("pe", lambda e, h=h: e.transpose(tra.t[:, h * 128:(h + 1) * 128], ret_o[:, h * 128:(h + 1) * 128],
                                                      ident_b[:, :]), reads=[ret_o, ident_b], writes=[tra])
            add("act", lambda e: e.copy(catT[:, 0:4, j * 128:(j + 1) * 128],
                                        tra.t[:, 0:512].rearrange("p (h i) -> p h i", h=4)),
                reads=[tra], writes=[catT.sub(0, 4096)])

        stage_d(0)
        chk(2)
        stage_d(1)
        stage_e(0, tau * 4 + 0)
        stage_d(2)
        stage_e(1, tau * 4 + 1)
        stage_d(3)
        stage_e(2, tau * 4 + 2)
        stage_e(3, tau * 4 + 3)
        chk(3)

        def conv_chunk(cc):
            rot = [pb[2], pb[3], pb[4], pb[6], pb[7]]
            banks = [rot[(cc * 3 + part) % 5] for part in range(3)]
            b_hc, b_gb, b_gc = banks
            for part in range(3):
                slot = rg_a.get()
                for k in range(8):
                    add("pe", lambda e, part=part, k=k, slot=slot: e.matmul(banks[part].t[:, :], slot[:, k, :], hT[:, k, :],
                                                                             start=(k == 0), stop=(k == 7)),
                        reads=[slot, hT], writes=[banks[part]])
            pc = pbuf.sub(cc * 2056, (cc + 1) * 2056)
            add("act", lambda e: e.copy(gc_sb[:, :], b_gc.t[:, :]), reads=[b_gc], writes=[gc_sb])
            add("dve", lambda e, cc=cc: e.tensor_copy(pbuf[:, cc, 0:2], pbuf[:, cc, 512:514]), reads=[pc], writes=[pc])
            add("dve", lambda e, cc=cc: e.tensor_tensor(pbuf[:, cc, 2:514], b_hc.t[:, :], gc_sb[:, :], ALU.mult),
                reads=[b_hc, gc_sb, pc], writes=[pc])
            cyc = cy.sub(cc * 2048, (cc + 1) * 2048)
            add("act", lambda e, cc=cc: e.activation(out=cy[:, cc, :], in_=pbuf[:, cc, 2:514], func=AF.Identity,
                                                     bias=pvT[:, R_CMB + cc:R_CMB + cc + 1],
                                                     scale=pvT[:, R_CMW + 8 + cc:R_CMW + 9 + cc]),
                reads=[pc, pvT], writes=[cyc])
            add("dve", lambda e, cc=cc: e.scalar_tensor_tensor(out=cy[:, cc, :], in0=pbuf[:, cc, 1:513],
                                                               scalar=pvT[:, R_CMW + 4 + cc:R_CMW + 5 + cc],
                                                               in1=cy[:, cc, :], op0=ALU.mult, op1=ALU.add),
                reads=[pc, pvT, cyc], writes=[cyc])
            add("dve", lambda e, cc=cc: e.scalar_tensor_tensor(out=cy[:, cc, :], in0=pbuf[:, cc, 0:512],
                                                               scalar=pvT[:, R_CMW + cc:R_CMW + cc + 1],
                                                               in1=cy[:, cc, :], op0=ALU.mult, op1=ALU.add),
                reads=[pc, pvT, cyc], writes=[cyc])
            add("dve", lambda e, cc=cc: e.tensor_tensor(cy[:, cc, :], b_gb.t[:, :], cy[:, cc, :], ALU.mult),
                reads=[b_gb, cyc], writes=[cyc])
            add("act", lambda e, cc=cc: e.activation(out=sq[:, :], in_=cy[:, cc, :], func=AF.Square),
                reads=[cyc], writes=[sq])
            add("pe", lambda e, cc=cc: e.matmul(pb[5].t[:, :], ones_b[:, :], sq[:, :], start=(cc == 0), stop=(cc == 3)),
                reads=[ones_b, sq], writes=[pb[5]])
        for cc in range(4):
            conv_chunk(cc)
        add("act", lambda e: e.activation(out=rstd_bc[:, :], in_=pb[5].t[:, :], func=AF.Sqrt, bias=epsc[:, 3:4], scale=1.0),
            reads=[pb[5], epsc], writes=[rstd_bc])
        add("dve", lambda e: e.reciprocal(rstd_bc[:, :], rstd_bc[:, :]), reads=[rstd_bc], writes=[rstd_bc])
        add("dve", lambda e: e.tensor_scalar(rstd_bc[:, :], rstd_bc[:, :], float(np.sqrt(512.0)), None, ALU.mult),
            reads=[rstd_bc], writes=[rstd_bc])
        for cc in range(4):
            add("dve", lambda e, cc=cc: e.scalar_tensor_tensor(out=catT[:, 4 + cc, :], in0=cy[:, cc, :],
                                                               scalar=pvT[:, R_CNG + cc:R_CNG + cc + 1],
                                                               in1=rstd_bc[:, :], op0=ALU.mult, op1=ALU.mult),
                reads=[cy.sub(cc * 2048, (cc + 1) * 2048), pvT, rstd_bc], writes=[catT.sub((4 + cc) * 1024, (5 + cc) * 1024)])

        def wout_block(j):
            for hf in range(2):
                bank = [pb[6], pb[7], pb[2], pb[3]][(j * 2 + hf) % 4]
                for k in range(8):
                    add("pe", lambda e, k=k, hf=hf, bank=bank: e.matmul(bank.t[:, :], catT[:, k, j * 128:(j + 1) * 128],
                                                                        w_outs[:, hf, k, :], start=(k == 0), stop=(k == 7)),
                        reads=[catT, w_outs.sub(hf * 8192, (hf + 1) * 8192)], writes=[bank])
                xr = x_res.sub(j * 4096 + hf * 2048, j * 4096 + (hf + 1) * 2048)
                add("dve", lambda e, hf=hf, bank=bank: e.tensor_tensor(gtmp[:, :], bank.t[:, :],
                                                                       gt1_bc[:, hf * 512:(hf + 1) * 512], ALU.mult),
                    reads=[bank, bro], writes=[gtmp])
                add(_pe("res"), lambda e, hf=hf: e.tensor_tensor(x_res[:, j, hf * 512:(hf + 1) * 512],
                                                             x_res[:, j, hf * 512:(hf + 1) * 512], gtmp[:, :], ALU.add),
                    reads=[xr, gtmp], writes=[xr])

        chk(4)
        for j in range(4):
            wout_block(j)
        chk(5)

        norm_to_hT(16)
        chk(6)

        def up_chunk(c):
            slot = rg_u.get()
            pr = c % 3
            banks = [pb[2 + 2 * pr], pb[3 + 2 * pr]]
            for a in range(2):
                for k in range(8):
                    add("pe", lambda e, a=a, k=k, slot=slot: e.matmul(banks[a].t[:, :], slot[:, k, a, :], hT[:, k, :],
                                                                      start=(k == 0), stop=(k == 7)),
                        reads=[slot, hT], writes=[banks[a]])
            us = u_sb[c % 3]
            ys = yab[c % 3]
            uh = uhalo.sub(c * 16, (c + 1) * 16)
            if c == 0:
                up_halo_in(0)
            usa = [us.sub(a * 2056, (a + 1) * 2056) for a in range(2)]
            ysa = [ys.sub(a * 2048, (a + 1) * 2048) for a in range(2)]
            for a in range(2):
                add("act", lambda e, a=a, us=us: e.copy(us[:, a, 2:514], banks[a].t[:, :]), reads=[banks[a]], writes=[usa[a]])
            add(_pe("halo"), lambda e, c=c, us=us: e.tensor_copy(uhalo[:, c, :, :], us[:, :, 512:514]), reads=[us], writes=[uh])
            for a in range(2):
                ch = a * NFF + c
                add("act", lambda e, a=a, ch=ch, us=us, ys=ys: e.activation(
                    out=ys[:, a, :], in_=us[:, a, 2:514], func=AF.Identity,
                    bias=pvT[:, R_CFB + ch:R_CFB + ch + 1], scale=pvT[:, R_CFW + 88 + ch:R_CFW + 89 + ch]),
                    reads=[usa[a], pvT], writes=[ysa[a]])
            for a in range(2):
                ch = a * NFF + c
                add("dve", lambda e, a=a, ch=ch, us=us, ys=ys: e.scalar_tensor_tensor(
                    out=ys[:, a, :], in0=us[:, a, 1:513], scalar=pvT[:, R_CFW + 44 + ch:R_CFW + 45 + ch],
                    in1=ys[:, a, :], op0=ALU.mult, op1=ALU.add), reads=[usa[a], pvT, ysa[a]], writes=[ysa[a]])
                add("dve", lambda e, a=a, ch=ch, us=us, ys=ys: e.scalar_tensor_tensor(
                    out=ys[:, a, :], in0=us[:, a, 0:512], scalar=pvT[:, R_CFW + ch:R_CFW + ch + 1],
                    in1=ys[:, a, :], op0=ALU.mult, op1=ALU.add), reads=[usa[a], pvT, ysa[a]], writes=[ysa[a]])
            if c + 1 < NFF:
                up_halo_in(c + 1)
            return ys

        def up_halo_in(c):
            us = u_sb[c % 3]
            add(_pe("halo"), lambda e: e.tensor_copy(us[:, :, 0:2], uhalo[:, c, :, :]),
                reads=[uhalo.sub(c * 16, (c + 1) * 16)], writes=[us])

        def up_tail(c, ys):
            add("act", lambda e: e.activation(out=ys[:, 0, :], in_=ys[:, 0, :], func=AF.Silu),
                reads=[ys.sub(0, 2048)], writes=[ys.sub(0, 2048)])
            add(_pe("gmul"), lambda e: e.tensor_tensor(gT[:, c, :], ys[:, 0, :], ys[:, 1, :], ALU.mult),
                reads=[ys], writes=[gT.sub(c * 1024, (c + 1) * 1024)])

        if tau + 1 < ntiles:
            load_mixer_weights(False)
        prev = None
        for c in range(NFF):
            ys_c = up_chunk(c)
            if prev is not None:
                up_tail(*prev)
            prev = (c, ys_c)
        up_tail(*prev)
        chk(7)

        acc_banks = {0: [pb[2], pb[3], pb[4], pb[5]], 1: [pb[6], pb[7], pb[2], pb[3]]}
        for hf in range(2):
            for cg in range(11):
                slot = rg_d.get()
                for j in range(4):
                    bank = acc_banks[hf][j]
                    for cl in range(2):
                        c = cg * 2 + cl
                        add("pe", lambda e, j=j, cl=cl, c=c, slot=slot, bank=bank: e.matmul(
                            bank.t[:, :], gT[:, c, j * 128:(j + 1) * 128], slot[:, cl, :],
                            start=(c == 0), stop=(c == NFF - 1)), reads=[gT, slot], writes=[bank])
            for j in range(4):
                bank = acc_banks[hf][j]
                xr = x_res.sub(j * 4096 + hf * 2048, j * 4096 + (hf + 1) * 2048)
                add("dve", lambda e, hf=hf, bank=bank: e.tensor_tensor(ftmp[:, :], bank.t[:, :],
                                                                       gt2_bc[:, hf * 512:(hf + 1) * 512], ALU.mult),
                    reads=[bank, bro], writes=[ftmp])
                add(_pe("res"), lambda e, j=j, hf=hf: e.tensor_tensor(x_res[:, j, hf * 512:(hf + 1) * 512],
                                                                  x_res[:, j, hf * 512:(hf + 1) * 512], ftmp[:, :], ALU.add),
                    reads=[xr, ftmp], writes=[xr])

        chk(8)
        def final_block(j):
            xr = x_res.sub(j * 4096, (j + 1) * 4096)
            rms_rstd32(xr, x_res[:, j, :], 16 + j)
            add("dve", lambda e, j=j: e.scalar_tensor_tensor(out=x_res[:, j, :], in0=x_res[:, j, :],
                                                             scalar=stat[:, 16 + j:17 + j], in1=fg_bc,
                                                             op0=ALU.mult, op1=ALU.mult),
                reads=[xr, stat.sub((16 + j) * 4, (17 + j) * 4), bro], writes=[xr])
            add("sp", lambda e, j=j: e.dma_start(out=y_d[t0 + j * 128:t0 + (j + 1) * 128, :], in_=x_res[:, j, :]),
                reads=[xr], writes=[dram("y", t0 + j * 128, t0 + (j + 1) * 128)], dma=True)

        for j in range(4):
            final_block(j)

    try:
        if kst0:
            raise _Stop()
        for tau in range(ntiles):
            tile(tau)
    except _Stop:
        pass

    S.emit({"sp": ["sp"]})
    return nc


_CACHE = {}


def kernel(x, c, positions, w_ada, b_ada, norm1_g, w_in, conv_mix_w, conv_mix_b, ret_norm_g, conv_norm_g,
           w_out, norm2_g, w_up, conv_ffn_w, conv_ffn_b, w_down, final_g):
    f = lambda a: np.ascontiguousarray(np.asarray(a), dtype=np.float32)
    x, c, w_ada, b_ada, w_in, w_out, w_up, w_down = map(f, (x, c, w_ada, b_ada, w_in, w_out, w_up, w_down))
    positions = np.ascontiguousarray(np.asarray(positions), dtype=np.int32)
    cst, dec128 = _host_consts()
    if "nc" not in _CACHE:
        _CACHE["nc"] = build_nc(dec128)
    nc = _CACHE["nc"]

    def kmaj(w):
        K, N = w.shape
        return np.ascontiguousarray(w.reshape(K // 128, 128, N).transpose(1, 0, 2))

    wada_l = np.ascontiguousarray(kmaj(w_ada).reshape(128, 8, 12, 512).transpose(2, 0, 1, 3))
    win = kmaj(w_in)
    winb_l = np.ascontiguousarray(win[:, :, 0:2048].reshape(128, 8, 4, 512).transpose(2, 0, 1, 3))
    wina_l = np.ascontiguousarray(win[:, :, 2048:3584].reshape(128, 8, 12, 128).transpose(2, 0, 1, 3))
    wout_l = np.ascontiguousarray(kmaj(w_out).reshape(128, 8, 2, 512).transpose(2, 0, 1, 3))
    wup = kmaj(w_up).reshape(128, 8, 2, NFF, 128)
    wup_l = np.ascontiguousarray(wup.transpose(3, 0, 1, 2, 4))
    wdn = w_down.reshape(11, 2, 128, 2, 512)
    wdn_l = np.ascontiguousarray(wdn.transpose(3, 0, 2, 1, 4))

    pv = np.zeros((R_TOT, 128), np.float32)
    pv[R_BADA:R_BADA + 48] = b_ada.reshape(48, 128)
    pv[R_G1:R_G1 + 8] = f(norm1_g).reshape(8, 128)
    pv[R_G2:R_G2 + 8] = f(norm2_g).reshape(8, 128)
    pv[R_CMW:R_CMW + 12] = f(conv_mix_w).reshape(12, 128)
    pv[R_CMB:R_CMB + 4] = f(conv_mix_b).reshape(4, 128)
    pv[R_CNG:R_CNG + 4] = f(conv_norm_g).reshape(4, 128)
    pv[R_CFW:R_CFW + 132] = f(conv_ffn_w).reshape(132, 128)
    pv[R_CFB:R_CFB + 44] = f(conv_ffn_b).reshape(44, 128)
    brow = np.concatenate([f(ret_norm_g), f(final_g), b_ada[2048:3072], b_ada[5120:6144]])
    bro = np.ascontiguousarray(np.broadcast_to(brow[None, :], (128, C_TOT)))

    in_maps = []
    for b in range(NB):
        pvb = pv.copy()
        pvb[R_C:R_C + 8] = c[b].reshape(8, 128)
        in_maps.append({
            "x": x[b], "pos": positions[b].reshape(32, 128), "pv": pvb, "bro": bro, "cst": cst,
            "w_ada": wada_l, "w_in_b": winb_l, "w_in_a": wina_l, "w_out": wout_l, "w_up": wup_l, "w_down": wdn_l,
        })
    res = run_bass_kernel_spmd(nc, in_maps, core_ids=list(range(NB)))
    return np.stack([np.asarray(r["y"], dtype=np.float32) for r in res.results], axis=0)
```

```python
import contextlib
import numpy as np
import concourse.bass as bass
import concourse.mybir as mybir
from concourse.bass_utils import run_bass_kernel_spmd

F32 = mybir.dt.float32
BF16 = mybir.dt.bfloat16
I32 = mybir.dt.int32
AF = mybir.ActivationFunctionType
ALU = mybir.AluOpType

ENGS = ("pe", "act", "dve", "pool", "sp")
import os as _os
_KP = _os.environ.get("KPOOL", "rot,halo,gmul,res,sg").split(",")


def _pe(group):
    return "pool" if group in _KP else "dve"
DMA_POOL = 16

D = 1024
SEQ = 4096
NB = 8
T = 512
NT = SEQ // T
DFF = 2816
NFF = DFF // 128
EPS = 1e-6
HEADS = 4


class Buf:
    def __init__(self, t, space, lo, hi):
        self.t = t
        self.space = space
        self.lo = lo
        self.hi = hi

    def __getitem__(self, k):
        return self.t[k]

    def sub(self, lo, hi):
        return Buf(self.t, self.space, self.lo + lo, self.lo + hi)


class _Op:
    __slots__ = ("eng", "fn", "idx", "waits", "inc", "dma", "dsem", "dcnt", "cnt", "selfwait")

    def __init__(self, eng, fn, dma):
        self.eng = eng
        self.fn = fn
        self.dma = dma
        self.waits = {}
        self.inc = False
        self.dsem = None
        self.dcnt = 0
        self.cnt = 0
        self.selfwait = None


class Sched:
    def __init__(self, nc):
        self.nc = nc
        self.ops = {e: [] for e in ENGS}
        self.spaces = {}
        self.dma_rr = {e: 0 for e in ENGS}
        self.dma_cnt = {}
        self.dma_last = {}
        self.psum_rd = {}

    def _dep(self, op, prod):
        if prod is None or prod is op:
            return
        if prod.dma:
            key = ("d", prod.eng, prod.dsem)
            val = prod.dcnt * 16
            cur = op.waits.get(key)
            op.waits[key] = val if cur is None else max(cur, val)
            return
        if prod.eng == "pe" and op.eng == "pe":
            return
        prod.inc = True
        key = ("c", prod.eng)
        cur = op.waits.get(key)
        if cur is None or cur.idx < prod.idx:
            op.waits[key] = prod

    def add(self, eng, fn, reads=(), writes=(), dma=False):
        op = _Op(eng, fn, dma)
        op.idx = len(self.ops[eng])
        self.ops[eng].append(op)
        if dma:
            slot = self.dma_rr[eng] % DMA_POOL
            self.dma_rr[eng] += 1
            k = (eng, slot)
            self.dma_cnt[k] = self.dma_cnt.get(k, 0) + 1
            op.dsem = slot
            op.dcnt = self.dma_cnt[k]
            prev = self.dma_last.get(k)
            if prev is not None:
                op.selfwait = (slot, prev.dcnt * 16)
            self.dma_last[k] = op
        for b in reads:
            if b.space == "psum":
                for bank in range(b.lo // 2048, (b.hi - 1) // 2048 + 1):
                    rd = self.psum_rd.setdefault(bank, {})
                    for oe, oop in rd.items():
                        if oe != eng:
                            self._dep(op, oop)
                    rd[eng] = op
            recs = self.spaces.setdefault(b.space, [])
            for r in recs:
                if r[0] < b.hi and b.lo < r[1]:
                    self._dep(op, r[2])
                    if dma:
                        r[4].append(op)
                    else:
                        r[3][eng] = op
        for b in writes:
            recs = self.spaces.setdefault(b.space, [])
            keep = []
            for r in recs:
                if r[0] < b.hi and b.lo < r[1]:
                    self._dep(op, r[2])
                    for rd in r[3].values():
                        self._dep(op, rd)
                    for rd in r[4]:
                        self._dep(op, rd)
                    if r[0] < b.lo:
                        keep.append([r[0], b.lo, r[2], dict(r[3]), list(r[4])])
                    if b.hi < r[1]:
                        keep.append([b.hi, r[1], r[2], dict(r[3]), list(r[4])])
                else:
                    keep.append(r)
            keep.append([b.lo, b.hi, op, {}, []])
            self.spaces[b.space] = keep
        return op

    def emit(self, final_waits):
        nc = self.nc
        for e in ENGS:
            c = 0
            for op in self.ops[e]:
                if op.inc and not op.dma:
                    c += 1
                op.cnt = c
        with contextlib.ExitStack() as st:
            csem = {e: st.enter_context(nc.semaphore("c_" + e)) for e in ENGS}
            dsem = {}
            for e in ENGS:
                for s in range(min(DMA_POOL, self.dma_rr[e])):
                    dsem[(e, s)] = st.enter_context(nc.semaphore("d_%s_%d" % (e, s)))
            block = st.enter_context(nc.Block())

            def run(e, engine):
                waited = {}
                for op in self.ops[e]:
                    for key, val in op.waits.items():
                        if key[0] == "d":
                            sem = dsem[(key[1], key[2])]
                            v = val
                        else:
                            sem = csem[key[1]]
                            v = val.cnt
                        if waited.get(key, 0) >= v:
                            continue
                        waited[key] = v
                        engine.wait_ge(sem, v)
                    if op.dma:
                        if op.selfwait is not None:
                            key = ("d", e, op.selfwait[0])
                            if waited.get(key, 0) < op.selfwait[1]:
                                waited[key] = op.selfwait[1]
                                engine.wait_ge(dsem[(e, op.selfwait[0])], op.selfwait[1])
                        op.fn(engine).then_inc(dsem[(e, op.dsem)], 16)
                    else:
                        ins = op.fn(engine)
                        if op.inc:
                            ins.then_inc(csem[e], 1)
                if e in final_waits:
                    for (qe, slot), op in self.dma_last.items():
                        if qe in final_waits[e]:
                            engine.wait_ge(dsem[(qe, slot)], op.dcnt * 16)

            @block.tensor
            def _(t):
                run("pe", t)

            @block.scalar
            def _(a):
                run("act", a)

            @block.vector
            def _(v):
                run("dve", v)

            @block.gpsimd
            def _(g):
                run("pool", g)

            @block.sync
            def _(s):
                run("sp", s)


class Arena:
    def __init__(self, nc, base, limit):
        self.nc = nc
        self.off = base
        self.limit = limit
        self.n = 0

    def alloc(self, shape, dtype, at=None):
        size = mybir.dt.size(dtype)
        for s in shape[1:]:
            size *= s
        if at is None:
            at = (self.off + 63) // 64 * 64
            self.off = at + size
        assert at + size <= self.limit, ("SBUF overflow", at + size, self.limit)
        self.n += 1
        t = self.nc.alloc_sbuf_tensor_at("sb%d" % self.n, list(shape), dtype, offset=at)
        return Buf(t, "sbuf", at, at + size)


R_C = 0
R_BADA = 8
R_G1 = 56
R_G2 = 64
R_CMW = 72
R_CMB = 84
R_CNG = 88
R_CFW = 92
R_CFB = 224
R_TOT = 268
C_RNG = 0
C_FG = 512
C_BG1 = 1536
C_BG2 = 2560
C_TOT = 3584


def _host_consts():
    h = np.arange(HEADS, dtype=np.float64)
    gam = 1.0 - 2.0 ** (-5.0 - h)
    lg = np.log(gam)
    i = np.arange(128)
    ci, cj = i[:, None] // 64, i[None, :] // 64
    dist = i[:, None] - i[None, :]
    W = np.zeros((HEADS, 128, 128))
    for hh in range(HEADS):
        same = np.exp(np.abs(dist) * lg[hh])
        causal = np.exp(dist * lg[hh])
        W[hh] = np.where(ci == cj, same, np.where(ci > cj, causal, 0.0))
    qw = np.exp((i[None, :] + 1) * lg[:, None])
    maskT = np.transpose(W / qw[:, :, None], (2, 0, 1))
    qwT = np.broadcast_to((qw * 128 ** -0.5)[None], (128, HEADS, 128))
    kw = np.exp((127 - i)[:, None] * lg[None, :])
    dec = np.exp(128 * lg)
    invf = (np.float32(10000.0) ** (-(np.arange(0, 128, 2, dtype=np.float32)) / np.float32(128))).astype(np.float32)
    cst = np.zeros((128, 128 + 512 + 512 + 4 + 64), np.float32)
    cst[:, 0:128] = np.eye(128)
    cst[:, 128:640] = maskT.reshape(128, 512)
    cst[:, 640:1152] = qwT.reshape(128, 512)
    cst[:, 1152:1156] = kw
    cst[:, 1156:1220] = invf[None, :]
    return cst, [float(v) for v in dec]


N_CST = 1220


class Ring:
    def __init__(self, slots, total, issue_fn):
        self.slots = slots
        self.total = total
        self.issue_fn = issue_fn
        self.issued = 0
        self.consumed = 0

    def prefetch(self):
        lim = min(self.total, self.consumed + len(self.slots))
        while self.issued < lim:
            self.issue_fn(self.issued, self.slots[self.issued % len(self.slots)])
            self.issued += 1

    def get(self):
        self.prefetch()
        slot = self.slots[self.consumed % len(self.slots)]
        self.consumed += 1
        return slot


class _Stop(Exception):
    pass


def build_nc(dec128, debug=False, kstop=99, ntiles=NT):
    def chk(n):
        if kstop <= n:
            raise _Stop()

    nc = bass.Bass("TRN2", target_bir_lowering=False)

    def din(name, shape, dt=F32):
        return nc.dram_tensor(name, list(shape), dt, kind="ExternalInput").ap()

    x_d = din("x", [SEQ, D])
    pos_d = din("pos", [32, 128], I32)
    pv_d = din("pv", [R_TOT, 128])
    bro_d = din("bro", [128, C_TOT])
    cst_d = din("cst", [128, N_CST])
    wada_d = din("w_ada", [12, 128, 8, 512])
    winb_d = din("w_in_b", [4, 128, 8, 512])
    wina_d = din("w_in_a", [12, 128, 8, 128])
    wout_d = din("w_out", [2, 128, 8, 512])
    wup_d = din("w_up", [NFF, 128, 8, 2, 128])
    wdn_d = din("w_down", [2, 11, 128, 2, 512])
    y_d = nc.dram_tensor("y", [SEQ, D], F32, kind="ExternalOutput").ap()

    S = Sched(nc)
    A = Arena(nc, 16640, 229376)
    add = S.add

    tra = Buf(nc.alloc_psum_tensor("tra", [128, 1024], BF16), "psum", 0, 2048)
    trb = Buf(nc.alloc_psum_tensor("trb", [128, 1024], BF16), "psum", 2048, 4096)
    pb = {}
    for i in range(2, 8):
        pb[i] = Buf(nc.alloc_psum_tensor("pb%d" % i, [128, 512], F32), "psum", i * 2048, (i + 1) * 2048)

    cst = A.alloc([128, N_CST], F32)
    ident_f = cst.t[:, 0:128]
    maskT = cst.t[:, 128:640].rearrange("p (h i) -> p h i", h=4)
    qwT = cst.t[:, 640:1152].rearrange("p (h i) -> p h i", h=4)
    kwtab = cst.t[:, 1152:1156]
    invf = cst.t[:, 1156:1220]
    ident_b = A.alloc([128, 128], BF16)
    ones_b = A.alloc([128, 128], BF16)
    epsc = A.alloc([128, 4], F32)
    pvT = A.alloc([128, R_TOT + 4], F32)
    modT = A.alloc([128, 48], F32)
    ab = A.alloc([128, 32], F32)
    bro = A.alloc([128, C_TOT], F32)
    posf = A.alloc([128, 32], F32)
    x_res = A.alloc([128, 4, D], F32)
    hT = A.alloc([128, 8, T], BF16)
    w_inb = A.alloc([128, 4, 8, 512], BF16)
    w_outs = A.alloc([128, 2, 8, 512], BF16)
    NRA, NRU, NRD = 3, 4, 6
    ring_a = [A.alloc([128, 8, 128], BF16) for _ in range(NRA)]
    ring_u = [A.alloc([128, 8, 2, 128], BF16) for _ in range(NRU)]
    ring_d = [A.alloc([128, 2, 512], BF16) for _ in range(NRD)]
    state_f = A.alloc([128, 4, 128], F32)
    state_b = [A.alloc([128, 4, 128], BF16) for _ in range(2)]
    pbuf = A.alloc([128, 4, 514], F32)
    uhalo = A.alloc([128, NFF, 2, 2], F32)
    stat = A.alloc([128, 64], F32)
    junk = A.alloc([128, D], BF16)

    region0 = A.off
    cs_t = A.alloc([128, 4, 64], F32)
    sn_t = A.alloc([128, 4, 64], F32)
    rtmp = A.alloc([128, 4, 64], F32)
    rki = A.alloc([128, 4, 64], I32)
    xn_at = (A.off + 63) // 64 * 64
    qf = [A.alloc([128, 512], F32) for _ in range(2)]
    kf = [A.alloc([128, 512], F32) for _ in range(2)]
    xn = A.alloc([128, 4, D], BF16, at=xn_at)
    ta = [A.alloc([128, 512], F32) for _ in range(2)]
    tb = [A.alloc([128, 512], F32) for _ in range(2)]
    q_rot = [A.alloc([128, 512], BF16) for _ in range(2)]
    k_rot = [A.alloc([128, 512], BF16) for _ in range(2)]
    v_bf = [A.alloc([128, 512], BF16) for _ in range(2)]
    v_kw = [A.alloc([128, 512], BF16) for _ in range(2)]
    sg = [A.alloc([128, 512], F32) for _ in range(2)]
    qTs = A.alloc([128, 4, 128], BF16)
    kTs = A.alloc([128, 4, 128], BF16)
    smT = A.alloc([128, 4, 128], BF16)
    o_n = A.alloc([128, 512], F32)
    ret_o = A.alloc([128, 512], BF16)
    bnst = A.alloc([128, 4, 6], F32)
    bnag = A.alloc([128, 4, 2], F32)
    catT = A.alloc([128, 8, T], BF16)
    gc_sb = A.alloc([128, 512], F32)
    cy = A.alloc([128, 4, 512], F32)
    sq = A.alloc([128, 512], BF16)
    rstd_bc = A.alloc([128, 512], F32)
    gtmp2 = [A.alloc([128, 512], F32) for _ in range(2)]
    region_m_end = A.off
    A.off = region0
    u_sb = [A.alloc([128, 2, 514], F32) for _ in range(3)]
    yab = [A.alloc([128, 2, 512], F32) for _ in range(3)]
    gT = A.alloc([128, NFF, T], BF16)
    ftmp2 = [A.alloc([128, 512], F32) for _ in range(2)]
    region_f_end = A.off
    A.off = region0
    pstage = A.alloc([128, 3, 128], F32)
    posi = A.alloc([32, 128], I32)
    posff = A.alloc([32, 128], F32)
    s_col = A.alloc([128, 8], F32)
    s_bc = A.alloc([128, 8, 128], F32)
    wada = [A.alloc([128, 8, 512], F32) for _ in range(2)]
    A.off = max(region_m_end, region_f_end, A.off)
    assert A.off <= A.limit, A.off

    def dram(name, lo=0, hi=1):
        return Buf(None, "dram_" + name, lo, hi)

    add("sp", lambda e: e.dma_start(out=cst[:, :], in_=cst_d[:, :]), writes=[cst], dma=True)
    add("sp", lambda e: e.dma_start(out=bro[:, :], in_=bro_d[:, :]), writes=[bro], dma=True)
    add("dve", lambda e: e.memset(pstage[:, :, :], 0.0), writes=[pstage])
    for g in range(3):
        r0, r1 = g * 128, min(R_TOT, (g + 1) * 128)
        add("sp", lambda e, g=g, r0=r0, r1=r1: e.dma_start(out=pstage[0:r1 - r0, g, :], in_=pv_d[r0:r1, :]),
            writes=[pstage.sub(g * 512, (g + 1) * 512)], dma=True)
    add("sp", lambda e: e.dma_start(out=posi[:, :], in_=pos_d[:, :]), writes=[posi], dma=True)
    add("dve", lambda e: e.tensor_copy(ident_b[:, :], ident_f), reads=[cst], writes=[ident_b])
    add("dve", lambda e: e.memset(ones_b[:, :], 1.0), writes=[ones_b])
    add("dve", lambda e: e.memset(epsc[:, 0:1], 1024 * EPS), writes=[epsc.sub(0, 4)])
    add("dve", lambda e: e.memset(epsc[:, 1:2], EPS), writes=[epsc.sub(4, 8)])
    add("dve", lambda e: e.memset(epsc[:, 2:3], float(np.pi / 2)), writes=[epsc.sub(8, 12)])
    add("dve", lambda e: e.memset(epsc[:, 3:4], 512 * EPS), writes=[epsc.sub(12, 16)])
    add("dve", lambda e: e.memset(state_f[:, :, :], 0.0), writes=[state_f])
    add("dve", lambda e: e.memset(state_b[0][:, :, :], 0.0), writes=[state_b[0]])
    add("dve", lambda e: e.memset(pbuf[:, :, :], 0.0), writes=[pbuf])
    add("dve", lambda e: e.memset(uhalo[:, :, :, :], 0.0), writes=[uhalo])
    for g in range(3):
        n = min(R_TOT, (g + 1) * 128) - g * 128
        add("pe", lambda e, g=g: e.matmul(pb[2].t[:, g * 128:(g + 1) * 128], pstage[:, g, :], ident_f,
                                            start=True, stop=True),
            reads=[pstage, cst], writes=[pb[2]])
    add("act", lambda e: e.copy(pvT[:, 0:R_TOT], pb[2].t[:, 0:R_TOT]), reads=[pb[2]], writes=[pvT])
    add("dve", lambda e: e.tensor_copy(posff[:, :], posi[:, :]), reads=[posi], writes=[posff])
    add("pe", lambda e: e.matmul(pb[3].t[:, 0:32], posff[:, :], ident_f[0:32, 0:32], start=True, stop=True),
        reads=[posff, cst], writes=[pb[3]])
    add("act", lambda e: e.copy(posf[:, :], pb[3].t[:, 0:32]), reads=[pb[3]], writes=[posf])
    add("act", lambda e: e.activation(out=s_col[:, :], in_=pvT[:, R_C:R_C + 8], func=AF.Silu),
        reads=[pvT], writes=[s_col])
    add("dve", lambda e: e.tensor_copy(s_bc[:, :, :], s_col[:, :].unsqueeze(2).to_broadcast([128, 8, 128])),
        reads=[s_col], writes=[s_bc])
    for g in range(12):
        wb = wada[g % 2]
        add("sp", lambda e, g=g, wb=wb: e.dma_start(out=wb[:, :, :], in_=wada_d[g]), writes=[wb], dma=True)
        if g in (4, 5, 10, 11):
            bank = pb[4 + (g % 2)]
            for k in range(8):
                add("pe", lambda e, k=k, wb=wb, bank=bank: e.matmul(bank.t[:, :], s_bc[:, k, :], wb[:, k, :],
                                                                    start=(k == 0), stop=(k == 7)),
                    reads=[s_bc, wb], writes=[bank])
            c0 = (C_BG1 if g < 6 else C_BG2) + (g % 2) * 512
            add("dve", lambda e, bank=bank, c0=c0: e.tensor_tensor(bro[:, c0:c0 + 512], bank.t[:, :],
                                                                    bro[:, c0:c0 + 512], ALU.add),
                reads=[bank, bro.sub(c0 * 4, (c0 + 512) * 4)], writes=[bro.sub(c0 * 4, (c0 + 512) * 4)])
        else:
            for m in range(4):
                col = g * 4 + m
                for k in range(8):
                    add("pe", lambda e, k=k, m=m, wb=wb, col=col: e.matmul(
                        pb[6].t[:, col:col + 1], wb[:, k, m * 128:(m + 1) * 128], s_col[:, k:k + 1],
                        start=(k == 0), stop=(k == 7)),
                        reads=[wb, s_col], writes=[pb[6].sub(col * 4, col * 4 + 4)])
    add("dve", lambda e: e.tensor_tensor(modT[:, :], pb[6].t[:, 0:48], pvT[:, R_BADA:R_BADA + 48], ALU.add),
        reads=[pb[6], pvT], writes=[modT])
    add("dve", lambda e: e.scalar_tensor_tensor(out=ab[:, 0:8], in0=modT[:, 8:16], scalar=1.0, in1=pvT[:, R_G1:R_G1 + 8],
                                                op0=ALU.add, op1=ALU.mult), reads=[modT, pvT], writes=[ab.sub(0, 32)])
    add("dve", lambda e: e.tensor_copy(ab[:, 8:16], modT[:, 0:8]), reads=[modT], writes=[ab.sub(32, 64)])
    add("dve", lambda e: e.scalar_tensor_tensor(out=ab[:, 16:24], in0=modT[:, 32:40], scalar=1.0,
                                                in1=pvT[:, R_G2:R_G2 + 8], op0=ALU.add, op1=ALU.mult),
        reads=[modT, pvT], writes=[ab.sub(64, 96)])
    add("dve", lambda e: e.tensor_copy(ab[:, 24:32], modT[:, 24:32]), reads=[modT], writes=[ab.sub(96, 128)])
    add("dve", lambda e: e.tensor_scalar(bro[:, C_FG:C_FG + 1024], bro[:, C_FG:C_FG + 1024], 32.0, None, ALU.mult),
        reads=[bro.sub(C_FG * 4, (C_FG + 1024) * 4)], writes=[bro.sub(C_FG * 4, (C_FG + 1024) * 4)])

    rng_bc = bro.t[:, C_RNG:C_RNG + 512]
    kst0 = kstop <= 0
    fg_bc = bro.t[:, C_FG:C_FG + 1024]
    gt1_bc = bro.t[:, C_BG1:C_BG1 + 1024]
    gt2_bc = bro.t[:, C_BG2:C_BG2 + 1024]

    sc_inb = nc.dram_tensor("sc_inb", [4, 128, 8, 512], BF16).ap()
    sc_ina = nc.dram_tensor("sc_ina", [12, 128, 8, 128], BF16).ap()
    sc_out = nc.dram_tensor("sc_out", [2, 128, 8, 512], BF16).ap()
    sc_up = nc.dram_tensor("sc_up", [NFF, 128, 8, 2, 128], BF16).ap()
    sc_dn = nc.dram_tensor("sc_dn", [2, 11, 128, 2, 512], BF16).ap()

    def stream(first, name, idx, dst_fn, src_f32, src_bf, wbuf):
        scr = Buf(None, "dram_" + name, idx, idx + 1)
        if first:
            add("pool", lambda e: e.dma_start(out=dst_fn(), in_=src_f32), writes=[wbuf], dma=True)
            add("sp", lambda e: e.dma_start(out=src_bf, in_=dst_fn()), reads=[wbuf], writes=[scr], dma=True)
        else:
            add("sp", lambda e: e.dma_start(out=dst_fn(), in_=src_bf), reads=[scr], writes=[wbuf], dma=True)

    def _issue_a(g, slot):
        i = g % 12
        cc, part = i // 3, i % 3
        k = part * 4 + cc
        stream(g < 12, "ina", k, lambda: slot[:, :, :], wina_d[k], sc_ina[k], slot)

    def _issue_u(g, slot):
        c = g % NFF
        stream(g < NFF, "up", c, lambda: slot[:, :, :, :], wup_d[c], sc_up[c], slot)

    def _issue_d(g, slot):
        i = g % 22
        hf, cg = i // 11, i % 11
        stream(g < 22, "dn", i, lambda: slot[:, :, :], wdn_d[hf, cg], sc_dn[hf, cg], slot)

    rg_a = Ring(ring_a, 12 * ntiles, _issue_a)
    rg_u = Ring(ring_u, NFF * ntiles, _issue_u)
    rg_d = Ring(ring_d, 22 * ntiles, _issue_d)

    def load_mixer_weights(first):
        for g in range(4):
            stream(first, "inb", g, lambda g=g: w_inb[:, g, :, :], winb_d[g], sc_inb[g], w_inb.sub(g * 8192, (g + 1) * 8192))
        for hf in range(2):
            stream(first, "out", hf, lambda hf=hf: w_outs[:, hf, :, :], wout_d[hf], sc_out[hf],
                   w_outs.sub(hf * 8192, (hf + 1) * 8192))

    def rms_rstd32(src_buf, src_ap, col):
        add("act", lambda e: e.activation(out=junk[:, :], in_=src_ap, func=AF.Square, accum_out=stat[:, col:col + 1]),
            reads=[src_buf], writes=[junk, stat.sub(col * 4, col * 4 + 4)])
        add("act", lambda e: e.activation(out=stat[:, col:col + 1], in_=stat[:, col:col + 1], func=AF.Sqrt,
                                          bias=epsc[:, 0:1], scale=1.0),
            reads=[stat.sub(col * 4, col * 4 + 4), epsc], writes=[stat.sub(col * 4, col * 4 + 4)])
        add("dve", lambda e: e.reciprocal(stat[:, col:col + 1], stat[:, col:col + 1]),
            reads=[stat.sub(col * 4, col * 4 + 4)], writes=[stat.sub(col * 4, col * 4 + 4)])

    def norm_to_hT(aoff):
        for j in range(4):
            rms_rstd32(x_res.sub(j * 4096, (j + 1) * 4096), x_res[:, j, :], j)
            add("dve", lambda e, j=j: e.tensor_scalar(xn[:, j, :], x_res[:, j, :], stat[:, j:j + 1], 32.0,
                                                       ALU.mult, ALU.mult),
                reads=[x_res.sub(j * 4096, (j + 1) * 4096), stat.sub(j * 4, j * 4 + 4)],
                writes=[xn.sub(j * 2048, (j + 1) * 2048)])
        chk(0.6)
        for c in range(8):
            if c == 1:
                chk(0.7)
            if c == 2:
                chk(0.8)
            tr = tra if c % 2 == 0 else trb
            for j in range(4):
                add("pe", lambda e, c=c, j=j, tr=tr: e.transpose(
                    tr.t[:, j * 128:(j + 1) * 128], xn[:, j, c * 128:(c + 1) * 128], ident_b[:, :]),
                    reads=[xn.sub(j * 2048, (j + 1) * 2048), ident_b], writes=[tr])
            add("act", lambda e, c=c, tr=tr: e.activation(
                out=hT[:, c, :], in_=tr.t[:, 0:512], func=AF.Identity,
                bias=ab[:, aoff + 8 + c:aoff + 9 + c], scale=ab[:, aoff + c:aoff + c + 1]),
                reads=[tr, ab], writes=[hT.sub(c * 1024, (c + 1) * 1024)])

    def tile(tau):
        t0 = tau * T
        for j in range(4):
            add("sp", lambda e, j=j: e.dma_start(out=x_res[:, j, :], in_=x_d[t0 + j * 128:t0 + (j + 1) * 128, :]),
                writes=[x_res.sub(j * 4096, (j + 1) * 4096)], dma=True)
        if tau == 0:
            load_mixer_weights(True)
        rg_a.prefetch()
        rg_u.prefetch()
        rg_d.prefetch()
        chk(0.2)
        pj = posf.t[:, tau * 4:tau * 4 + 4].unsqueeze(2).to_broadcast([128, 4, 64])
        iv = invf.unsqueeze(1).to_broadcast([128, 4, 64])
        add("dve", lambda e: e.tensor_tensor(cs_t[:, :, :], pj, iv, ALU.mult), reads=[posf, cst], writes=[cs_t])
        add("dve", lambda e: e.tensor_scalar(rtmp[:, :, :], cs_t[:, :, :], float(1.0 / (2 * np.pi)), None, ALU.mult),
            reads=[cs_t], writes=[rtmp])
        add("dve", lambda e: e.tensor_copy(rki[:, :, :], rtmp[:, :, :]), reads=[rtmp], writes=[rki])
        add("dve", lambda e: e.tensor_copy(rtmp[:, :, :], rki[:, :, :]), reads=[rki], writes=[rtmp])
        add("dve", lambda e: e.scalar_tensor_tensor(out=cs_t[:, :, :], in0=rtmp[:, :, :], scalar=-6.28125,
                                                    in1=cs_t[:, :, :], op0=ALU.mult, op1=ALU.add),
            reads=[rtmp, cs_t], writes=[cs_t])
        add("dve", lambda e: e.scalar_tensor_tensor(out=cs_t[:, :, :], in0=rtmp[:, :, :],
                                                    scalar=-0.0019353071795864769, in1=cs_t[:, :, :],
                                                    op0=ALU.mult, op1=ALU.add), reads=[rtmp, cs_t], writes=[cs_t])
        add("dve", lambda e: e.tensor_scalar(cs_t[:, :, :], cs_t[:, :, :], -3.141592, 3.141592, ALU.max, ALU.min),
            reads=[cs_t], writes=[cs_t])
        add("act", lambda e: e.activation(out=sn_t[:, :, :], in_=cs_t[:, :, :], func=AF.Sin), reads=[cs_t], writes=[sn_t])
        add("act", lambda e: e.activation(out=rtmp[:, :, :], in_=cs_t[:, :, :], func=AF.Abs), reads=[cs_t], writes=[rtmp])
        add("act", lambda e: e.activation(out=cs_t[:, :, :], in_=rtmp[:, :, :], func=AF.Sin, scale=-1.0,
                                          bias=epsc[:, 2:3]), reads=[rtmp, epsc], writes=[cs_t])

        chk(0.5)
        norm_to_hT(0)
        chk(1)

        def stage_d(j):
            p = j % 2
            banks = [pb[2], pb[3], pb[4], pb[5]]
            for g in range(4):
                for k in range(8):
                    add("pe", lambda e, g=g, k=k: e.matmul(banks[g].t[:, :], hT[:, k, j * 128:(j + 1) * 128],
                                                            w_inb[:, g, k, :], start=(k == 0), stop=(k == 7)),
                        reads=[hT, w_inb.sub(g * 8192, (g + 1) * 8192)], writes=[banks[g]])
            chk(1.2)
            add("act", lambda e: e.copy(qf[p][:, :], pb[2].t[:, :]), reads=[pb[2]], writes=[qf[p]])
            add("act", lambda e: e.copy(kf[p][:, :], pb[3].t[:, :]), reads=[pb[3]], writes=[kf[p]])
            add("act", lambda e: e.copy(v_bf[p][:, :], pb[4].t[:, :]), reads=[pb[4]], writes=[v_bf[p]])
            chk(1.4)
            for h in range(4):
                add("dve", lambda e, h=h: e.tensor_scalar(v_kw[p][:, h * 128:(h + 1) * 128], pb[4].t[:, h * 128:(h + 1) * 128],
                                                           kwtab[:, h:h + 1], None, ALU.mult),
                    reads=[pb[4], cst], writes=[v_kw[p]])
            chk(1.5)
            add("act", lambda e: e.activation(out=sg[p][:, :], in_=pb[5].t[:, :], func=AF.Silu),
                reads=[pb[5]], writes=[sg[p]])
            chk(1.6)
            cosb = cs_t.t[:, j, :].unsqueeze(1).to_broadcast([128, 4, 64])
            sinb = sn_t.t[:, j, :].unsqueeze(1).to_broadcast([128, 4, 64])
            for src, dst, en, tt in ((qf[p], q_rot[p], "dve", 0), (kf[p], k_rot[p], _pe("rot"), 1)):
                s4 = src.t[:, :].rearrange("p (h s d) -> p h s d", h=4, s=2)
                a4 = ta[tt].t[:, :].rearrange("p (h s d) -> p h s d", h=4, s=2)
                b4 = tb[tt].t[:, :].rearrange("p (h s d) -> p h s d", h=4, s=2)
                d4 = dst.t[:, :].rearrange("p (h s d) -> p h s d", h=4, s=2)
                for sidx in range(2):
                    add(en, lambda e, s4=s4, a4=a4, sidx=sidx: e.tensor_tensor(a4[:, :, sidx, :], s4[:, :, sidx, :],
                                                                               cosb, ALU.mult),
                        reads=[src, cs_t], writes=[ta[tt]])
                add(en, lambda e, s4=s4, b4=b4: e.tensor_tensor(b4[:, :, 0, :], s4[:, :, 1, :], sinb, ALU.mult),
                    reads=[src, sn_t], writes=[tb[tt]])
                add(en, lambda e, s4=s4, b4=b4: e.tensor_tensor(b4[:, :, 1, :], s4[:, :, 0, :], sinb, ALU.mult),
                    reads=[src, sn_t], writes=[tb[tt]])
                add(en, lambda e, a4=a4, b4=b4, d4=d4: e.tensor_tensor(d4[:, :, 0, :], a4[:, :, 0, :], b4[:, :, 0, :],
                                                                       ALU.subtract),
                    reads=[ta[tt], tb[tt]], writes=[dst])
                add(en, lambda e, a4=a4, b4=b4, d4=d4: e.tensor_tensor(d4[:, :, 1, :], a4[:, :, 1, :], b4[:, :, 1, :],
                                                                       ALU.add),
                    reads=[ta[tt], tb[tt]], writes=[dst])

        def stage_e(j, blk):
            p = j % 2
            sb_cur = state_b[blk % 2]
            sb_nxt = state_b[(blk + 1) % 2]
            for h in range(4):
                add("pe", lambda e, h=h: e.transpose(trb.t[:, h * 128:(h + 1) * 128], q_rot[p][:, h * 128:(h + 1) * 128],
                                                      ident_b[:, :]), reads=[q_rot[p], ident_b], writes=[trb])
            for h in range(4):
                add("pe", lambda e, h=h: e.transpose(tra.t[:, h * 128:(h + 1) * 128],
                                                      k_rot[p][:, h * 128:(h + 1) * 128], ident_b[:, :]),
                    reads=[k_rot[p], ident_b], writes=[tra])
            add("dve", lambda e: e.tensor_tensor(qTs[:, :, :], trb.t[:, 0:512].rearrange("p (h i) -> p h i", h=4),
                                                 qwT, ALU.mult), reads=[trb, cst], writes=[qTs])
            add("act", lambda e: e.copy(kTs[:, :, :], tra.t[:, 0:512].rearrange("p (h i) -> p h i", h=4)),
                reads=[tra], writes=[kTs])
            for h in range(4):
                add("pe", lambda e, h=h: e.matmul(pb[6].t[:, h * 128:(h + 1) * 128], kTs[:, h, :], qTs[:, h, :],
                                                   start=True, stop=True), reads=[kTs, qTs], writes=[pb[6]])
            add("dve", lambda e: e.tensor_tensor(smT[:, :, :], pb[6].t[:, :].rearrange("p (h i) -> p h i", h=4),
                                                 maskT, ALU.mult), reads=[pb[6], cst], writes=[smT])
            for h in range(4):
                add("pe", lambda e, h=h: e.matmul(pb[7].t[:, h * 128:(h + 1) * 128], smT[:, h, :],
                                                   v_bf[p][:, h * 128:(h + 1) * 128], start=True, stop=False),
                    reads=[smT, v_bf[p]], writes=[pb[7]])
                add("pe", lambda e, h=h: e.matmul(pb[7].t[:, h * 128:(h + 1) * 128], qTs[:, h, :], sb_cur[:, h, :],
                                                   start=False, stop=True), reads=[qTs, sb_cur], writes=[pb[7]])
            for h in range(4):
                add("pe", lambda e, h=h: e.matmul(pb[6].t[:, h * 128:(h + 1) * 128], k_rot[p][:, h * 128:(h + 1) * 128],
                                                   v_kw[p][:, h * 128:(h + 1) * 128], start=True, stop=True),
                    reads=[k_rot[p], v_kw[p]], writes=[pb[6]])
            for h in range(4):
                add("dve", lambda e, h=h: e.scalar_tensor_tensor(
                    out=state_f[:, h, :], in0=state_f[:, h, :], scalar=dec128[h], in1=pb[6].t[:, h * 128:(h + 1) * 128],
                    op0=ALU.mult, op1=ALU.add), reads=[state_f, pb[6]], writes=[state_f])
            add("act", lambda e: e.copy(sb_nxt[:, :, :], state_f[:, :, :]), reads=[state_f], writes=[sb_nxt])
            for h in range(4):
                add("dve", lambda e, h=h: e.bn_stats(bnst[:, h, :], pb[7].t[:, h * 128:(h + 1) * 128]),
                    reads=[pb[7]], writes=[bnst])
            for h in range(4):
                add("dve", lambda e, h=h: e.bn_aggr(bnag[:, h, :], bnst[:, h, :]), reads=[bnst], writes=[bnag])
            add("act", lambda e: e.activation(out=stat[:, 8:12], in_=bnag[:, :, 1], func=AF.Sqrt, bias=epsc[:, 1:2],
                                              scale=1.0), reads=[bnag, epsc], writes=[stat.sub(32, 48)])
            add("dve", lambda e: e.reciprocal(stat[:, 8:12], stat[:, 8:12]), reads=[stat.sub(32, 48)],
                writes=[stat.sub(32, 48)])
            for h in range(4):
                add("dve", lambda e, h=h: e.tensor_scalar(o_n[:, h * 128:(h + 1) * 128], pb[7].t[:, h * 128:(h + 1) * 128],
                                                           bnag[:, h, 0:1], stat[:, 8 + h:9 + h], ALU.subtract, ALU.mult),
                    reads=[pb[7], bnag, stat.sub(32, 48)], writes=[o_n])
            add(_pe("sg"), lambda e: e.tensor_tensor(sg[p][:, :], sg[p][:, :], rng_bc, ALU.mult),
                reads=[sg[p], bro.sub(0, 2048)], writes=[sg[p]])
            add(_pe("sg"), lambda e: e.tensor_tensor(ret_o[:, :], o_n[:, :], sg[p][:, :], ALU.mult),
                reads=[o_n, sg[p]], writes=[ret_o])
            for h in range(4):
                add("pe", lambda e, h=h: e.transpose(tra.t[:, h * 128:(h + 1) * 128], ret_o[:, h * 128:(h + 1) * 128],
                                                      ident_b[:, :]), reads=[ret_o, ident_b], writes=[tra])
            add("act", lambda e: e.copy(catT[:, 0:4, j * 128:(j + 1) * 128],
                                        tra.t[:, 0:512].rearrange("p (h i) -> p h i", h=4)),
                reads=[tra], writes=[catT.sub(0, 4096)])

        stage_d(0)
        chk(2)
        stage_d(1)
        stage_e(0, tau * 4 + 0)
        stage_d(2)
        stage_e(1, tau * 4 + 1)
        stage_d(3)
        stage_e(2, tau * 4 + 2)
        stage_e(3, tau * 4 + 3)
        chk(3)

        def conv_chunk(cc):
            rot = [pb[2], pb[3], pb[4], pb[6], pb[7]]
            banks = [rot[(cc * 3 + part) % 5] for part in range(3)]
            b_hc, b_gb, b_gc = banks
            for part in range(3):
                slot = rg_a.get()
                for k in range(8):
                    add("pe", lambda e, part=part, k=k, slot=slot: e.matmul(banks[part].t[:, :], slot[:, k, :], hT[:, k, :],
                                                                             start=(k == 0), stop=(k == 7)),
                        reads=[slot, hT], writes=[banks[part]])
            pc = pbuf.sub(cc * 2056, (cc + 1) * 2056)
            add("act", lambda e: e.copy(gc_sb[:, :], b_gc.t[:, :]), reads=[b_gc], writes=[gc_sb])
            add("dve", lambda e, cc=cc: e.tensor_copy(pbuf[:, cc, 0:2], pbuf[:, cc, 512:514]), reads=[pc], writes=[pc])
            add("dve", lambda e, cc=cc: e.tensor_tensor(pbuf[:, cc, 2:514], b_hc.t[:, :], gc_sb[:, :], ALU.mult),
                reads=[b_hc, gc_sb, pc], writes=[pc])
            cyc = cy.sub(cc * 2048, (cc + 1) * 2048)
            add("act", lambda e, cc=cc: e.activation(out=cy[:, cc, :], in_=pbuf[:, cc, 2:514], func=AF.Identity,
                                                     bias=pvT[:, R_CMB + cc:R_CMB + cc + 1],
                                                     scale=pvT[:, R_CMW + 8 + cc:R_CMW + 9 + cc]),
                reads=[pc, pvT], writes=[cyc])
            add("dve", lambda e, cc=cc: e.scalar_tensor_tensor(out=cy[:, cc, :], in0=pbuf[:, cc, 1:513],
                                                               scalar=pvT[:, R_CMW + 4 + cc:R_CMW + 5 + cc],
                                                               in1=cy[:, cc, :], op0=ALU.mult, op1=ALU.add),
                reads=[pc, pvT, cyc], writes=[cyc])
            add("dve", lambda e, cc=cc: e.scalar_tensor_tensor(out=cy[:, cc, :], in0=pbuf[:, cc, 0:512],
                                                               scalar=pvT[:, R_CMW + cc:R_CMW + cc + 1],
                                                               in1=cy[:, cc, :], op0=ALU.mult, op1=ALU.add),
                reads=[pc, pvT, cyc], writes=[cyc])
            add("dve", lambda e, cc=cc: e.tensor_tensor(cy[:, cc, :], b_gb.t[:, :], cy[:, cc, :], ALU.mult),
                reads=[b_gb, cyc], writes=[cyc])

        def conv_tail(cc):
            cyc = cy.sub(cc * 2048, (cc + 1) * 2048)
            add("act", lambda e: e.activation(out=sq[:, :], in_=cy[:, cc, :], func=AF.Square), reads=[cyc], writes=[sq])
            add("pe", lambda e: e.matmul(pb[5].t[:, :], ones_b[:, :], sq[:, :], start=(cc == 0), stop=(cc == 3)),
                reads=[ones_b, sq], writes=[pb[5]])

        for cc in range(4):
            conv_chunk(cc)
            if cc > 0:
                conv_tail(cc - 1)
        conv_tail(3)
        add("act", lambda e: e.activation(out=rstd_bc[:, :], in_=pb[5].t[:, :], func=AF.Sqrt, bias=epsc[:, 3:4], scale=1.0),
            reads=[pb[5], epsc], writes=[rstd_bc])
        add("dve", lambda e: e.reciprocal(rstd_bc[:, :], rstd_bc[:, :]), reads=[rstd_bc], writes=[rstd_bc])
        add("dve", lambda e: e.tensor_scalar(rstd_bc[:, :], rstd_bc[:, :], float(np.sqrt(512.0)), None, ALU.mult),
            reads=[rstd_bc], writes=[rstd_bc])
        for cc in range(4):
            add("dve", lambda e, cc=cc: e.scalar_tensor_tensor(out=catT[:, 4 + cc, :], in0=cy[:, cc, :],
                                                               scalar=pvT[:, R_CNG + cc:R_CNG + cc + 1],
                                                               in1=rstd_bc[:, :], op0=ALU.mult, op1=ALU.mult),
                reads=[cy.sub(cc * 2048, (cc + 1) * 2048), pvT, rstd_bc], writes=[catT.sub((4 + cc) * 1024, (5 + cc) * 1024)])

        def wout_block(j):
            for hf in range(2):
                bank = [pb[6], pb[7], pb[2], pb[3]][(j * 2 + hf) % 4]
                for k in range(8):
                    add("pe", lambda e, k=k, hf=hf, bank=bank: e.matmul(bank.t[:, :], catT[:, k, j * 128:(j + 1) * 128],
                                                                        w_outs[:, hf, k, :], start=(k == 0), stop=(k == 7)),
                        reads=[catT, w_outs.sub(hf * 8192, (hf + 1) * 8192)], writes=[bank])
                xr = x_res.sub(j * 4096 + hf * 2048, j * 4096 + (hf + 1) * 2048)
                gtmp = gtmp2[hf]
                add("dve", lambda e, hf=hf, bank=bank, gtmp=gtmp: e.tensor_tensor(gtmp[:, :], bank.t[:, :],
                                                                                  gt1_bc[:, hf * 512:(hf + 1) * 512], ALU.mult),
                    reads=[bank, bro], writes=[gtmp])
                add(_pe("res"), lambda e, hf=hf, gtmp=gtmp: e.tensor_tensor(x_res[:, j, hf * 512:(hf + 1) * 512],
                                                                            x_res[:, j, hf * 512:(hf + 1) * 512], gtmp[:, :], ALU.add),
                    reads=[xr, gtmp], writes=[xr])

        chk(4)
        for j in range(4):
            wout_block(j)
        chk(5)

        norm_to_hT(16)
        chk(6)

        def up_chunk(c):
            slot = rg_u.get()
            pr = c % 3
            banks = [pb[2 + 2 * pr], pb[3 + 2 * pr]]
            for a in range(2):
                for k in range(8):
                    add("pe", lambda e, a=a, k=k, slot=slot: e.matmul(banks[a].t[:, :], slot[:, k, a, :], hT[:, k, :],
                                                                      start=(k == 0), stop=(k == 7)),
                        reads=[slot, hT], writes=[banks[a]])
            us = u_sb[c % 3]
            ys = yab[c % 3]
            uh = uhalo.sub(c * 16, (c + 1) * 16)
            if c == 0:
                up_halo_in(0)
            usa = [us.sub(a * 2056, (a + 1) * 2056) for a in range(2)]
            ysa = [ys.sub(a * 2048, (a + 1) * 2048) for a in range(2)]
            for a in range(2):
                add("act", lambda e, a=a, us=us: e.copy(us[:, a, 2:514], banks[a].t[:, :]), reads=[banks[a]], writes=[usa[a]])
            add(_pe("halo"), lambda e, c=c, us=us: e.tensor_copy(uhalo[:, c, :, :], us[:, :, 512:514]), reads=[us], writes=[uh])
            for a in range(2):
                ch = a * NFF + c
                add("act", lambda e, a=a, ch=ch, us=us, ys=ys: e.activation(
                    out=ys[:, a, :], in_=us[:, a, 2:514], func=AF.Identity,
                    bias=pvT[:, R_CFB + ch:R_CFB + ch + 1], scale=pvT[:, R_CFW + 88 + ch:R_CFW + 89 + ch]),
                    reads=[usa[a], pvT], writes=[ysa[a]])
            for a in range(2):
                ch = a * NFF + c
                add("dve", lambda e, a=a, ch=ch, us=us, ys=ys: e.scalar_tensor_tensor(
                    out=ys[:, a, :], in0=us[:, a, 1:513], scalar=pvT[:, R_CFW + 44 + ch:R_CFW + 45 + ch],
                    in1=ys[:, a, :], op0=ALU.mult, op1=ALU.add), reads=[usa[a], pvT, ysa[a]], writes=[ysa[a]])
                add("dve", lambda e, a=a, ch=ch, us=us, ys=ys: e.scalar_tensor_tensor(
                    out=ys[:, a, :], in0=us[:, a, 0:512], scalar=pvT[:, R_CFW + ch:R_CFW + ch + 1],
                    in1=ys[:, a, :], op0=ALU.mult, op1=ALU.add), reads=[usa[a], pvT, ysa[a]], writes=[ysa[a]])
            if c + 1 < NFF:
                up_halo_in(c + 1)
            return ys

        def up_halo_in(c):
            us = u_sb[c % 3]
            add(_pe("halo"), lambda e: e.tensor_copy(us[:, :, 0:2], uhalo[:, c, :, :]),
                reads=[uhalo.sub(c * 16, (c + 1) * 16)], writes=[us])

        def up_tail(c, ys):
            add("act", lambda e: e.activation(out=ys[:, 0, :], in_=ys[:, 0, :], func=AF.Silu),
                reads=[ys.sub(0, 2048)], writes=[ys.sub(0, 2048)])
            add(_pe("gmul"), lambda e: e.tensor_tensor(gT[:, c, :], ys[:, 0, :], ys[:, 1, :], ALU.mult),
                reads=[ys], writes=[gT.sub(c * 1024, (c + 1) * 1024)])

        if tau + 1 < ntiles:
            load_mixer_weights(False)
        prev = None
        for c in range(NFF):
            ys_c = up_chunk(c)
            if prev is not None:
                up_tail(*prev)
            prev = (c, ys_c)
        up_tail(*prev)
        chk(7)

        acc_banks = {0: [pb[2], pb[3], pb[4], pb[5]], 1: [pb[6], pb[7], pb[2], pb[3]]}
        for hf in range(2):
            for cg in range(11):
                slot = rg_d.get()
                for j in range(4):
                    bank = acc_banks[hf][j]
                    for cl in range(2):
                        c = cg * 2 + cl
                        add("pe", lambda e, j=j, cl=cl, c=c, slot=slot, bank=bank: e.matmul(
                            bank.t[:, :], gT[:, c, j * 128:(j + 1) * 128], slot[:, cl, :],
                            start=(c == 0), stop=(c == NFF - 1)), reads=[gT, slot], writes=[bank])
            for j in range(4):
                bank = acc_banks[hf][j]
                xr = x_res.sub(j * 4096 + hf * 2048, j * 4096 + (hf + 1) * 2048)
                ftmp = ftmp2[j % 2]
                add("dve", lambda e, hf=hf, bank=bank, ftmp=ftmp: e.tensor_tensor(ftmp[:, :], bank.t[:, :],
                                                                                  gt2_bc[:, hf * 512:(hf + 1) * 512], ALU.mult),
                    reads=[bank, bro], writes=[ftmp])
                add(_pe("res"), lambda e, j=j, hf=hf, ftmp=ftmp: e.tensor_tensor(x_res[:, j, hf * 512:(hf + 1) * 512],
                                                                             x_res[:, j, hf * 512:(hf + 1) * 512], ftmp[:, :], ALU.add),
                    reads=[xr, ftmp], writes=[xr])

        chk(8)
        def final_block(j):
            xr = x_res.sub(j * 4096, (j + 1) * 4096)
            rms_rstd32(xr, x_res[:, j, :], 16 + j)
            add("dve", lambda e, j=j: e.scalar_tensor_tensor(out=x_res[:, j, :], in0=x_res[:, j, :],
                                                             scalar=stat[:, 16 + j:17 + j], in1=fg_bc,
                                                             op0=ALU.mult, op1=ALU.mult),
                reads=[xr, stat.sub((16 + j) * 4, (17 + j) * 4), bro], writes=[xr])
            add("sp", lambda e, j=j: e.dma_start(out=y_d[t0 + j * 128:t0 + (j + 1) * 128, :], in_=x_res[:, j, :]),
                reads=[xr], writes=[dram("y", t0 + j * 128, t0 + (j + 1) * 128)], dma=True)

        for j in range(4):
            final_block(j)

    try:
        if kst0:
            raise _Stop()
        for tau in range(ntiles):
            tile(tau)
    except _Stop:
        pass

    S.emit({"sp": ["sp"]})
    return nc


_CACHE = {}


def kernel(x, c, positions, w_ada, b_ada, norm1_g, w_in, conv_mix_w, conv_mix_b, ret_norm_g, conv_norm_g,
           w_out, norm2_g, w_up, conv_ffn_w, conv_ffn_b, w_down, final_g):
    f = lambda a: np.ascontiguousarray(np.asarray(a), dtype=np.float32)
    x, c, w_ada, b_ada, w_in, w_out, w_up, w_down = map(f, (x, c, w_ada, b_ada, w_in, w_out, w_up, w_down))
    positions = np.ascontiguousarray(np.asarray(positions), dtype=np.int32)
    cst, dec128 = _host_consts()
    if "nc" not in _CACHE:
        _CACHE["nc"] = build_nc(dec128)
    nc = _CACHE["nc"]

    def kmaj(w):
        K, N = w.shape
        return np.ascontiguousarray(w.reshape(K // 128, 128, N).transpose(1, 0, 2))

    wada_l = np.ascontiguousarray(kmaj(w_ada).reshape(128, 8, 12, 512).transpose(2, 0, 1, 3))
    win = kmaj(w_in)
    winb_l = np.ascontiguousarray(win[:, :, 0:2048].reshape(128, 8, 4, 512).transpose(2, 0, 1, 3))
    wina_l = np.ascontiguousarray(win[:, :, 2048:3584].reshape(128, 8, 12, 128).transpose(2, 0, 1, 3))
    wout_l = np.ascontiguousarray(kmaj(w_out).reshape(128, 8, 2, 512).transpose(2, 0, 1, 3))
    wup = kmaj(w_up).reshape(128, 8, 2, NFF, 128)
    wup_l = np.ascontiguousarray(wup.transpose(3, 0, 1, 2, 4))
    wdn = w_down.reshape(11, 2, 128, 2, 512)
    wdn_l = np.ascontiguousarray(wdn.transpose(3, 0, 2, 1, 4))

    pv = np.zeros((R_TOT, 128), np.float32)
    pv[R_BADA:R_BADA + 48] = b_ada.reshape(48, 128)
    pv[R_G1:R_G1 + 8] = f(norm1_g).reshape(8, 128)
    pv[R_G2:R_G2 + 8] = f(norm2_g).reshape(8, 128)
    pv[R_CMW:R_CMW + 12] = f(conv_mix_w).reshape(12, 128)
    pv[R_CMB:R_CMB + 4] = f(conv_mix_b).reshape(4, 128)
    pv[R_CNG:R_CNG + 4] = f(conv_norm_g).reshape(4, 128)
    pv[R_CFW:R_CFW + 132] = f(conv_ffn_w).reshape(132, 128)
    pv[R_CFB:R_CFB + 44] = f(conv_ffn_b).reshape(44, 128)
    brow = np.concatenate([f(ret_norm_g), f(final_g), b_ada[2048:3072], b_ada[5120:6144]])
    bro = np.ascontiguousarray(np.broadcast_to(brow[None, :], (128, C_TOT)))

    in_maps = []
    for b in range(NB):
        pvb = pv.copy()
        pvb[R_C:R_C + 8] = c[b].reshape(8, 128)
        in_maps.append({
            "x": x[b], "pos": positions[b].reshape(32, 128), "pv": pvb, "bro": bro, "cst": cst,
            "w_ada": wada_l, "w_in_b": winb_l, "w_in_a": wina_l, "w_out": wout_l, "w_up": wup_l, "w_down": wdn_l,
        })
    res = run_bass_kernel_spmd(nc, in_maps, core_ids=list(range(NB)))
    return np.stack([np.asarray(r["y"], dtype=np.float32) for r in res.results], axis=0)
```

```python
import contextlib
import numpy as np
import concourse.bass as bass
import concourse.mybir as mybir
from concourse.bass_utils import run_bass_kernel_spmd

F32 = mybir.dt.float32
BF16 = mybir.dt.bfloat16
I32 = mybir.dt.int32
AF = mybir.ActivationFunctionType
ALU = mybir.AluOpType

ENGS = ("pe", "act", "dve", "pool", "sp")
import os as _os
_KP = _os.environ.get("KPOOL", "rot,halo,gmul,res,sg").split(",")


def _pe(group):
    return "pool" if group in _KP else "dve"
DMA_POOL = 16

D = 1024
SEQ = 4096
NB = 8
T = 512
NT = SEQ // T
DFF = 2816
NFF = DFF // 128
EPS = 1e-6
HEADS = 4


class Buf:
    def __init__(self, t, space, lo, hi):
        self.t = t
        self.space = space
        self.lo = lo
        self.hi = hi

    def __getitem__(self, k):
        return self.t[k]

    def sub(self, lo, hi):
        return Buf(self.t, self.space, self.lo + lo, self.lo + hi)


class _Op:
    __slots__ = ("eng", "fn", "idx", "waits", "inc", "dma", "dsem", "dcnt", "cnt", "selfwait")

    def __init__(self, eng, fn, dma):
        self.eng = eng
        self.fn = fn
        self.dma = dma
        self.waits = {}
        self.inc = False
        self.dsem = None
        self.dcnt = 0
        self.cnt = 0
        self.selfwait = None


class Sched:
    def __init__(self, nc):
        self.nc = nc
        self.ops = {e: [] for e in ENGS}
        self.spaces = {}
        self.dma_rr = {e: 0 for e in ENGS}
        self.dma_cnt = {}
        self.dma_last = {}
        self.psum_rd = {}

    def _dep(self, op, prod):
        if prod is None or prod is op:
            return
        if prod.dma:
            key = ("d", prod.eng, prod.dsem)
            val = prod.dcnt * 16
            cur = op.waits.get(key)
            op.waits[key] = val if cur is None else max(cur, val)
            return
        if prod.eng == "pe" and op.eng == "pe":
            return
        prod.inc = True
        key = ("c", prod.eng)
        cur = op.waits.get(key)
        if cur is None or cur.idx < prod.idx:
            op.waits[key] = prod

    def add(self, eng, fn, reads=(), writes=(), dma=False):
        op = _Op(eng, fn, dma)
        op.idx = len(self.ops[eng])
        self.ops[eng].append(op)
        if dma:
            slot = self.dma_rr[eng] % DMA_POOL
            self.dma_rr[eng] += 1
            k = (eng, slot)
            self.dma_cnt[k] = self.dma_cnt.get(k, 0) + 1
            op.dsem = slot
            op.dcnt = self.dma_cnt[k]
            prev = self.dma_last.get(k)
            if prev is not None:
                op.selfwait = (slot, prev.dcnt * 16)
            self.dma_last[k] = op
        for b in reads:
            if b.space == "psum":
                for bank in range(b.lo // 2048, (b.hi - 1) // 2048 + 1):
                    rd = self.psum_rd.setdefault(bank, {})
                    for oe, oop in rd.items():
                        if oe != eng:
                            self._dep(op, oop)
                    rd[eng] = op
            recs = self.spaces.setdefault(b.space, [])
            for r in recs:
                if r[0] < b.hi and b.lo < r[1]:
                    self._dep(op, r[2])
                    if dma:
                        r[4].append(op)
                    else:
                        r[3][eng] = op
        for b in writes:
            recs = self.spaces.setdefault(b.space, [])
            keep = []
            for r in recs:
                if r[0] < b.hi and b.lo < r[1]:
                    self._dep(op, r[2])
                    for rd in r[3].values():
                        self._dep(op, rd)
                    for rd in r[4]:
                        self._dep(op, rd)
                    if r[0] < b.lo:
                        keep.append([r[0], b.lo, r[2], dict(r[3]), list(r[4])])
                    if b.hi < r[1]:
                        keep.append([b.hi, r[1], r[2], dict(r[3]), list(r[4])])
                else:
                    keep.append(r)
            keep.append([b.lo, b.hi, op, {}, []])
            self.spaces[b.space] = keep
        return op

    def emit(self, final_waits):
        nc = self.nc
        for e in ENGS:
            c = 0
            for op in self.ops[e]:
                if op.inc and not op.dma:
                    c += 1
                op.cnt = c
        with contextlib.ExitStack() as st:
            csem = {e: st.enter_context(nc.semaphore("c_" + e)) for e in ENGS}
            dsem = {}
            for e in ENGS:
                for s in range(min(DMA_POOL, self.dma_rr[e])):
                    dsem[(e, s)] = st.enter_context(nc.semaphore("d_%s_%d" % (e, s)))
            block = st.enter_context(nc.Block())

            def run(e, engine):
                waited = {}
                for op in self.ops[e]:
                    for key, val in op.waits.items():
                        if key[0] == "d":
                            sem = dsem[(key[1], key[2])]
                            v = val
                        else:
                            sem = csem[key[1]]
                            v = val.cnt
                        if waited.get(key, 0) >= v:
                            continue
                        waited[key] = v
                        engine.wait_ge(sem, v)
                    if op.dma:
                        if op.selfwait is not None:
                            key = ("d", e, op.selfwait[0])
                            if waited.get(key, 0) < op.selfwait[1]:
                                waited[key] = op.selfwait[1]
                                engine.wait_ge(dsem[(e, op.selfwait[0])], op.selfwait[1])
                        op.fn(engine).then_inc(dsem[(e, op.dsem)], 16)
                    else:
                        ins = op.fn(engine)
                        if op.inc:
                            ins.then_inc(csem[e], 1)
                if e in final_waits:
                    for (qe, slot), op in self.dma_last.items():
                        if qe in final_waits[e]:
                            engine.wait_ge(dsem[(qe, slot)], op.dcnt * 16)

            @block.tensor
            def _(t):
                run("pe", t)

            @block.scalar
            def _(a):
                run("act", a)

            @block.vector
            def _(v):
                run("dve", v)

            @block.gpsimd
            def _(g):
                run("pool", g)

            @block.sync
            def _(s):
                run("sp", s)


class Arena:
    def __init__(self, nc, base, limit):
        self.nc = nc
        self.off = base
        self.limit = limit
        self.n = 0

    def alloc(self, shape, dtype, at=None):
        size = mybir.dt.size(dtype)
        for s in shape[1:]:
            size *= s
        if at is None:
            at = (self.off + 63) // 64 * 64
            self.off = at + size
        assert at + size <= self.limit, ("SBUF overflow", at + size, self.limit)
        self.n += 1
        t = self.nc.alloc_sbuf_tensor_at("sb%d" % self.n, list(shape), dtype, offset=at)
        return Buf(t, "sbuf", at, at + size)


R_C = 0
R_BADA = 8
R_G1 = 56
R_G2 = 64
R_CMW = 72
R_CMB = 84
R_CNG = 88
R_CFW = 92
R_CFB = 224
R_TOT = 268
C_RNG = 0
C_FG = 512
C_BG1 = 1536
C_BG2 = 2560
C_TOT = 3584


def _host_consts():
    h = np.arange(HEADS, dtype=np.float64)
    gam = 1.0 - 2.0 ** (-5.0 - h)
    lg = np.log(gam)
    i = np.arange(128)
    ci, cj = i[:, None] // 64, i[None, :] // 64
    dist = i[:, None] - i[None, :]
    W = np.zeros((HEADS, 128, 128))
    for hh in range(HEADS):
        same = np.exp(np.abs(dist) * lg[hh])
        causal = np.exp(dist * lg[hh])
        W[hh] = np.where(ci == cj, same, np.where(ci > cj, causal, 0.0))
    qw = np.exp((i[None, :] + 1) * lg[:, None])
    maskT = np.transpose(W / qw[:, :, None], (2, 0, 1))
    qwT = np.broadcast_to((qw * 128 ** -0.5)[None], (128, HEADS, 128))
    kw = np.exp((127 - i)[:, None] * lg[None, :])
    dec = np.exp(128 * lg)
    invf = (np.float32(10000.0) ** (-(np.arange(0, 128, 2, dtype=np.float32)) / np.float32(128))).astype(np.float32)
    cst = np.zeros((128, 128 + 512 + 512 + 4 + 64), np.float32)
    cst[:, 0:128] = np.eye(128)
    cst[:, 128:640] = maskT.reshape(128, 512)
    cst[:, 640:1152] = qwT.reshape(128, 512)
    cst[:, 1152:1156] = kw
    cst[:, 1156:1220] = invf[None, :]
    return cst, [float(v) for v in dec]


N_CST = 1220


class Ring:
    def __init__(self, slots, total, issue_fn):
        self.slots = slots
        self.total = total
        self.issue_fn = issue_fn
        self.issued = 0
        self.consumed = 0

    def prefetch(self):
        lim = min(self.total, self.consumed + len(self.slots))
        while self.issued < lim:
            self.issue_fn(self.issued, self.slots[self.issued % len(self.slots)])
            self.issued += 1

    def get(self):
        self.prefetch()
        slot = self.slots[self.consumed % len(self.slots)]
        self.consumed += 1
        return slot


class _Stop(Exception):
    pass


def build_nc(dec128, debug=False, kstop=99, ntiles=NT):
    def chk(n):
        if kstop <= n:
            raise _Stop()

    nc = bass.Bass("TRN2", target_bir_lowering=False)

    def din(name, shape, dt=F32):
        return nc.dram_tensor(name, list(shape), dt, kind="ExternalInput").ap()

    x_d = din("x", [SEQ, D])
    pos_d = din("pos", [32, 128], I32)
    pv_d = din("pv", [R_TOT, 128])
    bro_d = din("bro", [128, C_TOT])
    cst_d = din("cst", [128, N_CST])
    wada_d = din("w_ada", [12, 128, 8, 512])
    winb_d = din("w_in_b", [4, 128, 8, 512])
    wina_d = din("w_in_a", [12, 128, 8, 128])
    wout_d = din("w_out", [2, 128, 8, 512])
    wup_d = din("w_up", [NFF, 128, 8, 2, 128])
    wdn_d = din("w_down", [2, 11, 128, 2, 512])
    y_d = nc.dram_tensor("y", [SEQ, D], F32, kind="ExternalOutput").ap()

    S = Sched(nc)
    A = Arena(nc, 16640, 229376)
    add = S.add

    tra = Buf(nc.alloc_psum_tensor("tra", [128, 1024], BF16), "psum", 0, 2048)
    trb = Buf(nc.alloc_psum_tensor("trb", [128, 1024], BF16), "psum", 2048, 4096)
    pb = {}
    for i in range(2, 8):
        pb[i] = Buf(nc.alloc_psum_tensor("pb%d" % i, [128, 512], F32), "psum", i * 2048, (i + 1) * 2048)

    cst = A.alloc([128, N_CST], F32)
    ident_f = cst.t[:, 0:128]
    maskT = cst.t[:, 128:640].rearrange("p (h i) -> p h i", h=4)
    qwT = cst.t[:, 640:1152].rearrange("p (h i) -> p h i", h=4)
    kwtab = cst.t[:, 1152:1156]
    invf = cst.t[:, 1156:1220]
    ident_b = A.alloc([128, 128], BF16)
    ones_b = A.alloc([128, 128], BF16)
    epsc = A.alloc([128, 4], F32)
    pvT = A.alloc([128, R_TOT + 4], F32)
    modT = A.alloc([128, 48], F32)
    ab = A.alloc([128, 32], F32)
    bro = A.alloc([128, C_TOT], F32)
    posf = A.alloc([128, 32], F32)
    x_res = A.alloc([128, 4, D], F32)
    hT = A.alloc([128, 8, T], BF16)
    w_inb = A.alloc([128, 4, 8, 512], BF16)
    w_outs = A.alloc([128, 2, 8, 512], BF16)
    NRA, NRU, NRD = 3, 4, 6
    ring_a = [A.alloc([128, 8, 128], BF16) for _ in range(NRA)]
    ring_u = [A.alloc([128, 8, 2, 128], BF16) for _ in range(NRU)]
    ring_d = [A.alloc([128, 2, 512], BF16) for _ in range(NRD)]
    state_f = A.alloc([128, 4, 128], F32)
    state_b = [A.alloc([128, 4, 128], BF16) for _ in range(2)]
    pbuf = A.alloc([128, 4, 514], F32)
    uhalo = A.alloc([128, NFF, 2, 2], F32)
    stat = A.alloc([128, 64], F32)
    junk = A.alloc([128, D], BF16)

    region0 = A.off
    cs_t = A.alloc([128, 4, 64], F32)
    sn_t = A.alloc([128, 4, 64], F32)
    rtmp = A.alloc([128, 4, 64], F32)
    rki = A.alloc([128, 4, 64], I32)
    xn_at = (A.off + 63) // 64 * 64
    qf = [A.alloc([128, 512], F32) for _ in range(2)]
    kf = [A.alloc([128, 512], F32) for _ in range(2)]
    xn = A.alloc([128, 4, D], BF16, at=xn_at)
    ta = [A.alloc([128, 512], F32) for _ in range(2)]
    tb = [A.alloc([128, 512], F32) for _ in range(2)]
    q_rot = [A.alloc([128, 512], BF16) for _ in range(2)]
    k_rot = [A.alloc([128, 512], BF16) for _ in range(2)]
    v_bf = [A.alloc([128, 512], BF16) for _ in range(2)]
    v_kw = [A.alloc([128, 512], BF16) for _ in range(2)]
    sg = [A.alloc([128, 512], F32) for _ in range(2)]
    qTs = A.alloc([128, 4, 128], BF16)
    kTs = A.alloc([128, 4, 128], BF16)
    smT = A.alloc([128, 4, 128], BF16)
    o_n = A.alloc([128, 512], F32)
    ret_o = A.alloc([128, 512], BF16)
    bnst = A.alloc([128, 4, 6], F32)
    bnag = A.alloc([128, 4, 2], F32)
    catT = A.alloc([128, 8, T], BF16)
    gc_sb = A.alloc([128, 512], F32)
    cy = A.alloc([128, 4, 512], F32)
    sq = A.alloc([128, 512], BF16)
    rstd_bc = A.alloc([128, 512], F32)
    gtmp2 = [A.alloc([128, 512], F32) for _ in range(2)]
    region_m_end = A.off
    A.off = region0
    u_sb = [A.alloc([128, 2, 514], F32) for _ in range(3)]
    yab = [A.alloc([128, 2, 512], F32) for _ in range(3)]
    gT = A.alloc([128, NFF, T], BF16)
    ftmp2 = [A.alloc([128, 512], F32) for _ in range(2)]
    region_f_end = A.off
    A.off = region0
    pstage = A.alloc([128, 3, 128], F32)
    posi = A.alloc([32, 128], I32)
    posff = A.alloc([32, 128], F32)
    s_col = A.alloc([128, 8], F32)
    s_bc = A.alloc([128, 8, 128], F32)
    wada = [A.alloc([128, 8, 512], F32) for _ in range(2)]
    A.off = max(region_m_end, region_f_end, A.off)
    assert A.off <= A.limit, A.off

    def dram(name, lo=0, hi=1):
        return Buf(None, "dram_" + name, lo, hi)

    add("sp", lambda e: e.dma_start(out=cst[:, :], in_=cst_d[:, :]), writes=[cst], dma=True)
    add("sp", lambda e: e.dma_start(out=bro[:, :], in_=bro_d[:, :]), writes=[bro], dma=True)
    add("dve", lambda e: e.memset(pstage[:, :, :], 0.0), writes=[pstage])
    for g in range(3):
        r0, r1 = g * 128, min(R_TOT, (g + 1) * 128)
        add("sp", lambda e, g=g, r0=r0, r1=r1: e.dma_start(out=pstage[0:r1 - r0, g, :], in_=pv_d[r0:r1, :]),
            writes=[pstage.sub(g * 512, (g + 1) * 512)], dma=True)
    add("sp", lambda e: e.dma_start(out=posi[:, :], in_=pos_d[:, :]), writes=[posi], dma=True)
    add("dve", lambda e: e.tensor_copy(ident_b[:, :], ident_f), reads=[cst], writes=[ident_b])
    add("dve", lambda e: e.memset(ones_b[:, :], 1.0), writes=[ones_b])
    add("dve", lambda e: e.memset(epsc[:, 0:1], 1024 * EPS), writes=[epsc.sub(0, 4)])
    add("dve", lambda e: e.memset(epsc[:, 1:2], EPS), writes=[epsc.sub(4, 8)])
    add("dve", lambda e: e.memset(epsc[:, 2:3], float(np.pi / 2)), writes=[epsc.sub(8, 12)])
    add("dve", lambda e: e.memset(epsc[:, 3:4], 512 * EPS), writes=[epsc.sub(12, 16)])
    add("dve", lambda e: e.memset(state_f[:, :, :], 0.0), writes=[state_f])
    add("dve", lambda e: e.memset(state_b[0][:, :, :], 0.0), writes=[state_b[0]])
    add("dve", lambda e: e.memset(pbuf[:, :, :], 0.0), writes=[pbuf])
    add("dve", lambda e: e.memset(uhalo[:, :, :, :], 0.0), writes=[uhalo])
    for g in range(3):
        n = min(R_TOT, (g + 1) * 128) - g * 128
        add("pe", lambda e, g=g: e.matmul(pb[2].t[:, g * 128:(g + 1) * 128], pstage[:, g, :], ident_f,
                                            start=True, stop=True),
            reads=[pstage, cst], writes=[pb[2]])
    add("act", lambda e: e.copy(pvT[:, 0:R_TOT], pb[2].t[:, 0:R_TOT]), reads=[pb[2]], writes=[pvT])
    add("dve", lambda e: e.tensor_copy(posff[:, :], posi[:, :]), reads=[posi], writes=[posff])
    add("pe", lambda e: e.matmul(pb[3].t[:, 0:32], posff[:, :], ident_f[0:32, 0:32], start=True, stop=True),
        reads=[posff, cst], writes=[pb[3]])
    add("act", lambda e: e.copy(posf[:, :], pb[3].t[:, 0:32]), reads=[pb[3]], writes=[posf])
    add("act", lambda e: e.activation(out=s_col[:, :], in_=pvT[:, R_C:R_C + 8], func=AF.Silu),
        reads=[pvT], writes=[s_col])
    add("dve", lambda e: e.tensor_copy(s_bc[:, :, :], s_col[:, :].unsqueeze(2).to_broadcast([128, 8, 128])),
        reads=[s_col], writes=[s_bc])
    for g in range(12):
        wb = wada[g % 2]
        add("sp", lambda e, g=g, wb=wb: e.dma_start(out=wb[:, :, :], in_=wada_d[g]), writes=[wb], dma=True)
        if g in (4, 5, 10, 11):
            bank = pb[4 + (g % 2)]
            for k in range(8):
                add("pe", lambda e, k=k, wb=wb, bank=bank: e.matmul(bank.t[:, :], s_bc[:, k, :], wb[:, k, :],
                                                                    start=(k == 0), stop=(k == 7)),
                    reads=[s_bc, wb], writes=[bank])
            c0 = (C_BG1 if g < 6 else C_BG2) + (g % 2) * 512
            add("dve", lambda e, bank=bank, c0=c0: e.tensor_tensor(bro[:, c0:c0 + 512], bank.t[:, :],
                                                                    bro[:, c0:c0 + 512], ALU.add),
                reads=[bank, bro.sub(c0 * 4, (c0 + 512) * 4)], writes=[bro.sub(c0 * 4, (c0 + 512) * 4)])
        else:
            for m in range(4):
                col = g * 4 + m
                for k in range(8):
                    add("pe", lambda e, k=k, m=m, wb=wb, col=col: e.matmul(
                        pb[6].t[:, col:col + 1], wb[:, k, m * 128:(m + 1) * 128], s_col[:, k:k + 1],
                        start=(k == 0), stop=(k == 7)),
                        reads=[wb, s_col], writes=[pb[6].sub(col * 4, col * 4 + 4)])
    add("dve", lambda e: e.tensor_tensor(modT[:, :], pb[6].t[:, 0:48], pvT[:, R_BADA:R_BADA + 48], ALU.add),
        reads=[pb[6], pvT], writes=[modT])
    add("dve", lambda e: e.scalar_tensor_tensor(out=ab[:, 0:8], in0=modT[:, 8:16], scalar=1.0, in1=pvT[:, R_G1:R_G1 + 8],
                                                op0=ALU.add, op1=ALU.mult), reads=[modT, pvT], writes=[ab.sub(0, 32)])
    add("dve", lambda e: e.tensor_copy(ab[:, 8:16], modT[:, 0:8]), reads=[modT], writes=[ab.sub(32, 64)])
    add("dve", lambda e: e.scalar_tensor_tensor(out=ab[:, 16:24], in0=modT[:, 32:40], scalar=1.0,
                                                in1=pvT[:, R_G2:R_G2 + 8], op0=ALU.add, op1=ALU.mult),
        reads=[modT, pvT], writes=[ab.sub(64, 96)])
    add("dve", lambda e: e.tensor_copy(ab[:, 24:32], modT[:, 24:32]), reads=[modT], writes=[ab.sub(96, 128)])
    add("dve", lambda e: e.tensor_scalar(bro[:, C_FG:C_FG + 1024], bro[:, C_FG:C_FG + 1024], 32.0, None, ALU.mult),
        reads=[bro.sub(C_FG * 4, (C_FG + 1024) * 4)], writes=[bro.sub(C_FG * 4, (C_FG + 1024) * 4)])

    rng_bc = bro.t[:, C_RNG:C_RNG + 512]
    kst0 = kstop <= 0
    fg_bc = bro.t[:, C_FG:C_FG + 1024]
    gt1_bc = bro.t[:, C_BG1:C_BG1 + 1024]
    gt2_bc = bro.t[:, C_BG2:C_BG2 + 1024]

    sc_inb = nc.dram_tensor("sc_inb", [4, 128, 8, 512], BF16).ap()
    sc_ina = nc.dram_tensor("sc_ina", [12, 128, 8, 128], BF16).ap()
    sc_out = nc.dram_tensor("sc_out", [2, 128, 8, 512], BF16).ap()
    sc_up = nc.dram_tensor("sc_up", [NFF, 128, 8, 2, 128], BF16).ap()
    sc_dn = nc.dram_tensor("sc_dn", [2, 11, 128, 2, 512], BF16).ap()

    def stream(first, name, idx, dst_fn, src_f32, src_bf, wbuf):
        scr = Buf(None, "dram_" + name, idx, idx + 1)
        if first:
            add("pool", lambda e: e.dma_start(out=dst_fn(), in_=src_f32), writes=[wbuf], dma=True)
            add("sp", lambda e: e.dma_start(out=src_bf, in_=dst_fn()), reads=[wbuf], writes=[scr], dma=True)
        else:
            add("sp", lambda e: e.dma_start(out=dst_fn(), in_=src_bf), reads=[scr], writes=[wbuf], dma=True)

    def _issue_a(g, slot):
        i = g % 12
        cc, part = i // 3, i % 3
        k = part * 4 + cc
        stream(g < 12, "ina", k, lambda: slot[:, :, :], wina_d[k], sc_ina[k], slot)

    def _issue_u(g, slot):
        c = g % NFF
        stream(g < NFF, "up", c, lambda: slot[:, :, :, :], wup_d[c], sc_up[c], slot)

    def _issue_d(g, slot):
        i = g % 22
        hf, cg = i // 11, i % 11
        stream(g < 22, "dn", i, lambda: slot[:, :, :], wdn_d[hf, cg], sc_dn[hf, cg], slot)

    rg_a = Ring(ring_a, 12 * ntiles, _issue_a)
    rg_u = Ring(ring_u, NFF * ntiles, _issue_u)
    rg_d = Ring(ring_d, 22 * ntiles, _issue_d)

    def load_mixer_weights(first):
        for g in range(4):
            stream(first, "inb", g, lambda g=g: w_inb[:, g, :, :], winb_d[g], sc_inb[g], w_inb.sub(g * 8192, (g + 1) * 8192))
        for hf in range(2):
            stream(first, "out", hf, lambda hf=hf: w_outs[:, hf, :, :], wout_d[hf], sc_out[hf],
                   w_outs.sub(hf * 8192, (hf + 1) * 8192))

    def rms_rstd32(src_buf, src_ap, col):
        add("act", lambda e: e.activation(out=junk[:, :], in_=src_ap, func=AF.Square, accum_out=stat[:, col:col + 1]),
            reads=[src_buf], writes=[junk, stat.sub(col * 4, col * 4 + 4)])
        add("act", lambda e: e.activation(out=stat[:, col:col + 1], in_=stat[:, col:col + 1], func=AF.Sqrt,
                                          bias=epsc[:, 0:1], scale=1.0),
            reads=[stat.sub(col * 4, col * 4 + 4), epsc], writes=[stat.sub(col * 4, col * 4 + 4)])
        add("dve", lambda e: e.reciprocal(stat[:, col:col + 1], stat[:, col:col + 1]),
            reads=[stat.sub(col * 4, col * 4 + 4)], writes=[stat.sub(col * 4, col * 4 + 4)])

    def norm_to_hT(aoff):
        for j in range(4):
            rms_rstd32(x_res.sub(j * 4096, (j + 1) * 4096), x_res[:, j, :], j)
            add("dve", lambda e, j=j: e.tensor_scalar(xn[:, j, :], x_res[:, j, :], stat[:, j:j + 1], 32.0,
                                                       ALU.mult, ALU.mult),
                reads=[x_res.sub(j * 4096, (j + 1) * 4096), stat.sub(j * 4, j * 4 + 4)],
                writes=[xn.sub(j * 2048, (j + 1) * 2048)])
        chk(0.6)
        for c in range(8):
            if c == 1:
                chk(0.7)
            if c == 2:
                chk(0.8)
            tr = tra if c % 2 == 0 else trb
            for j in range(4):
                add("pe", lambda e, c=c, j=j, tr=tr: e.transpose(
                    tr.t[:, j * 128:(j + 1) * 128], xn[:, j, c * 128:(c + 1) * 128], ident_b[:, :]),
                    reads=[xn.sub(j * 2048, (j + 1) * 2048), ident_b], writes=[tr])
            add("act", lambda e, c=c, tr=tr: e.activation(
                out=hT[:, c, :], in_=tr.t[:, 0:512], func=AF.Identity,
                bias=ab[:, aoff + 8 + c:aoff + 9 + c], scale=ab[:, aoff + c:aoff + c + 1]),
                reads=[tr, ab], writes=[hT.sub(c * 1024, (c + 1) * 1024)])

    def tile(tau):
        t0 = tau * T
        for j in range(4):
            add("sp", lambda e, j=j: e.dma_start(out=x_res[:, j, :], in_=x_d[t0 + j * 128:t0 + (j + 1) * 128, :]),
                writes=[x_res.sub(j * 4096, (j + 1) * 4096)], dma=True)
        if tau == 0:
            load_mixer_weights(True)
        rg_a.prefetch()
        rg_u.prefetch()
        rg_d.prefetch()
        chk(0.2)
        pj = posf.t[:, tau * 4:tau * 4 + 4].unsqueeze(2).to_broadcast([128, 4, 64])
        iv = invf.unsqueeze(1).to_broadcast([128, 4, 64])
        add("dve", lambda e: e.tensor_tensor(cs_t[:, :, :], pj, iv, ALU.mult), reads=[posf, cst], writes=[cs_t])
        add("dve", lambda e: e.tensor_scalar(rtmp[:, :, :], cs_t[:, :, :], float(1.0 / (2 * np.pi)), None, ALU.mult),
            reads=[cs_t], writes=[rtmp])
        add("dve", lambda e: e.tensor_copy(rki[:, :, :], rtmp[:, :, :]), reads=[rtmp], writes=[rki])
        add("dve", lambda e: e.tensor_copy(rtmp[:, :, :], rki[:, :, :]), reads=[rki], writes=[rtmp])
        add("dve", lambda e: e.scalar_tensor_tensor(out=cs_t[:, :, :], in0=rtmp[:, :, :], scalar=-6.28125,
                                                    in1=cs_t[:, :, :], op0=ALU.mult, op1=ALU.add),
            reads=[rtmp, cs_t], writes=[cs_t])
        add("dve", lambda e: e.scalar_tensor_tensor(out=cs_t[:, :, :], in0=rtmp[:, :, :],
                                                    scalar=-0.0019353071795864769, in1=cs_t[:, :, :],
                                                    op0=ALU.mult, op1=ALU.add), reads=[rtmp, cs_t], writes=[cs_t])
        add("dve", lambda e: e.tensor_scalar(cs_t[:, :, :], cs_t[:, :, :], -3.141592, 3.141592, ALU.max, ALU.min),
            reads=[cs_t], writes=[cs_t])
        add("act", lambda e: e.activation(out=sn_t[:, :, :], in_=cs_t[:, :, :], func=AF.Sin), reads=[cs_t], writes=[sn_t])
        add("act", lambda e: e.activation(out=rtmp[:, :, :], in_=cs_t[:, :, :], func=AF.Abs), reads=[cs_t], writes=[rtmp])
        add("act", lambda e: e.activation(out=cs_t[:, :, :], in_=rtmp[:, :, :], func=AF.Sin, scale=-1.0,
                                          bias=epsc[:, 2:3]), reads=[rtmp, epsc], writes=[cs_t])

        chk(0.5)
        norm_to_hT(0)
        chk(1)

        def stage_d(j):
            p = j % 2
            banks = [pb[2], pb[3], pb[4], pb[5]]
            for g in range(4):
                for k in range(8):
                    add("pe", lambda e, g=g, k=k: e.matmul(banks[g].t[:, :], hT[:, k, j * 128:(j + 1) * 128],
                                                            w_inb[:, g, k, :], start=(k == 0), stop=(k == 7)),
                        reads=[hT, w_inb.sub(g * 8192, (g + 1) * 8192)], writes=[banks[g]])
            chk(1.2)
            add("act", lambda e: e.copy(qf[p][:, :], pb[2].t[:, :]), reads=[pb[2]], writes=[qf[p]])
            add("act", lambda e: e.copy(kf[p][:, :], pb[3].t[:, :]), reads=[pb[3]], writes=[kf[p]])
            add("act", lambda e: e.copy(v_bf[p][:, :], pb[4].t[:, :]), reads=[pb[4]], writes=[v_bf[p]])
            chk(1.4)
            for h in range(4):
                add("dve", lambda e, h=h: e.tensor_scalar(v_kw[p][:, h * 128:(h + 1) * 128], pb[4].t[:, h * 128:(h + 1) * 128],
                                                           kwtab[:, h:h + 1], None, ALU.mult),
                    reads=[pb[4], cst], writes=[v_kw[p]])
            chk(1.5)
            add("act", lambda e: e.activation(out=sg[p][:, :], in_=pb[5].t[:, :], func=AF.Silu),
                reads=[pb[5]], writes=[sg[p]])
            chk(1.6)
            cosb = cs_t.t[:, j, :].unsqueeze(1).to_broadcast([128, 4, 64])
            sinb = sn_t.t[:, j, :].unsqueeze(1).to_broadcast([128, 4, 64])
            for src, dst, en, tt in ((qf[p], q_rot[p], "dve", 0), (kf[p], k_rot[p], _pe("rot"), 1)):
                s4 = src.t[:, :].rearrange("p (h s d) -> p h s d", h=4, s=2)
                a4 = ta[tt].t[:, :].rearrange("p (h s d) -> p h s d", h=4, s=2)
                b4 = tb[tt].t[:, :].rearrange("p (h s d) -> p h s d", h=4, s=2)
                d4 = dst.t[:, :].rearrange("p (h s d) -> p h s d", h=4, s=2)
                for sidx in range(2):
                    add(en, lambda e, s4=s4, a4=a4, sidx=sidx: e.tensor_tensor(a4[:, :, sidx, :], s4[:, :, sidx, :],
                                                                               cosb, ALU.mult),
                        reads=[src, cs_t], writes=[ta[tt]])
                add(en, lambda e, s4=s4, b4=b4: e.tensor_tensor(b4[:, :, 0, :], s4[:, :, 1, :], sinb, ALU.mult),
                    reads=[src, sn_t], writes=[tb[tt]])
                add(en, lambda e, s4=s4, b4=b4: e.tensor_tensor(b4[:, :, 1, :], s4[:, :, 0, :], sinb, ALU.mult),
                    reads=[src, sn_t], writes=[tb[tt]])
                add(en, lambda e, a4=a4, b4=b4, d4=d4: e.tensor_tensor(d4[:, :, 0, :], a4[:, :, 0, :], b4[:, :, 0, :],
                                                                       ALU.subtract),
                    reads=[ta[tt], tb[tt]], writes=[dst])
                add(en, lambda e, a4=a4, b4=b4, d4=d4: e.tensor_tensor(d4[:, :, 1, :], a4[:, :, 1, :], b4[:, :, 1, :],
                                                                       ALU.add),
                    reads=[ta[tt], tb[tt]], writes=[dst])

        def stage_e(j, blk):
            p = j % 2
            sb_cur = state_b[blk % 2]
            sb_nxt = state_b[(blk + 1) % 2]
            for h in range(4):
                add("pe", lambda e, h=h: e.transpose(trb.t[:, h * 128:(h + 1) * 128], q_rot[p][:, h * 128:(h + 1) * 128],
                                                      ident_b[:, :]), reads=[q_rot[p], ident_b], writes=[trb])
            for h in range(4):
                add("pe", lambda e, h=h: e.transpose(tra.t[:, h * 128:(h + 1) * 128],
                                                      k_rot[p][:, h * 128:(h + 1) * 128], ident_b[:, :]),
                    reads=[k_rot[p], ident_b], writes=[tra])
            add("dve", lambda e: e.tensor_tensor(qTs[:, :, :], trb.t[:, 0:512].rearrange("p (h i) -> p h i", h=4),
                                                 qwT, ALU.mult), reads=[trb, cst], writes=[qTs])
            add("act", lambda e: e.copy(kTs[:, :, :], tra.t[:, 0:512].rearrange("p (h i) -> p h i", h=4)),
                reads=[tra], writes=[kTs])
            for h in range(4):
                add("pe", lambda e, h=h: e.matmul(pb[6].t[:, h * 128:(h + 1) * 128], kTs[:, h, :], qTs[:, h, :],
                                                   start=True, stop=True), reads=[kTs, qTs], writes=[pb[6]])
            add("dve", lambda e: e.tensor_tensor(smT[:, :, :], pb[6].t[:, :].rearrange("p (h i) -> p h i", h=4),
                                                 maskT, ALU.mult), reads=[pb[6], cst], writes=[smT])
            for h in range(4):
                add("pe", lambda e, h=h: e.matmul(pb[7].t[:, h * 128:(h + 1) * 128], smT[:, h, :],
                                                   v_bf[p][:, h * 128:(h + 1) * 128], start=True, stop=False),
                    reads=[smT, v_bf[p]], writes=[pb[7]])
                add("pe", lambda e, h=h: e.matmul(pb[7].t[:, h * 128:(h + 1) * 128], qTs[:, h, :], sb_cur[:, h, :],
                                                   start=False, stop=True), reads=[qTs, sb_cur], writes=[pb[7]])
            for h in range(4):
                add("pe", lambda e, h=h: e.matmul(pb[6].t[:, h * 128:(h + 1) * 128], k_rot[p][:, h * 128:(h + 1) * 128],
                                                   v_kw[p][:, h * 128:(h + 1) * 128], start=True, stop=True),
                    reads=[k_rot[p], v_kw[p]], writes=[pb[6]])
            for h in range(4):
                add("dve", lambda e, h=h: e.scalar_tensor_tensor(
                    out=state_f[:, h, :], in0=state_f[:, h, :], scalar=dec128[h], in1=pb[6].t[:, h * 128:(h + 1) * 128],
                    op0=ALU.mult, op1=ALU.add), reads=[state_f, pb[6]], writes=[state_f])
            add("act", lambda e: e.copy(sb_nxt[:, :, :], state_f[:, :, :]), reads=[state_f], writes=[sb_nxt])
            for h in range(4):
                add("dve", lambda e, h=h: e.bn_stats(bnst[:, h, :], pb[7].t[:, h * 128:(h + 1) * 128]),
                    reads=[pb[7]], writes=[bnst])
            for h in range(4):
                add("dve", lambda e, h=h: e.bn_aggr(bnag[:, h, :], bnst[:, h, :]), reads=[bnst], writes=[bnag])
            add("act", lambda e: e.activation(out=stat[:, 8:12], in_=bnag[:, :, 1], func=AF.Sqrt, bias=epsc[:, 1:2],
                                              scale=1.0), reads=[bnag, epsc], writes=[stat.sub(32, 48)])
            add("dve", lambda e: e.reciprocal(stat[:, 8:12], stat[:, 8:12]), reads=[stat.sub(32, 48)],
                writes=[stat.sub(32, 48)])
            for h in range(4):
                add("dve", lambda e, h=h: e.tensor_scalar(o_n[:, h * 128:(h + 1) * 128], pb[7].t[:, h * 128:(h + 1) * 128],
                                                           bnag[:, h, 0:1], stat[:, 8 + h:9 + h], ALU.subtract, ALU.mult),
                    reads=[pb[7], bnag, stat.sub(32, 48)], writes=[o_n])
            add(_pe("sg"), lambda e: e.tensor_tensor(sg[p][:, :], sg[p][:, :], rng_bc, ALU.mult),
                reads=[sg[p], bro.sub(0, 2048)], writes=[sg[p]])
            add(_pe("sg"), lambda e: e.tensor_tensor(ret_o[:, :], o_n[:, :], sg[p][:, :], ALU.mult),
                reads=[o_n, sg[p]], writes=[ret_o])
            def tail():
                for h in range(4):
                    add("pe", lambda e, h=h: e.transpose(tra.t[:, h * 128:(h + 1) * 128], ret_o[:, h * 128:(h + 1) * 128],
                                                          ident_b[:, :]), reads=[ret_o, ident_b], writes=[tra])
                add("act", lambda e: e.copy(catT[:, 0:4, j * 128:(j + 1) * 128],
                                            tra.t[:, 0:512].rearrange("p (h i) -> p h i", h=4)),
                    reads=[tra], writes=[catT.sub(0, 4096)])
            return tail

        stage_d(0)
        chk(2)
        stage_d(1)
        tl0 = stage_e(0, tau * 4 + 0)
        stage_d(2)
        tl0()
        tl1 = stage_e(1, tau * 4 + 1)
        stage_d(3)
        tl1()
        tl2 = stage_e(2, tau * 4 + 2)
        tl2()
        tl3 = stage_e(3, tau * 4 + 3)
        tl3()
        chk(3)

        def conv_chunk(cc):
            rot = [pb[2], pb[3], pb[4], pb[6], pb[7]]
            banks = [rot[(cc * 3 + part) % 5] for part in range(3)]
            b_hc, b_gb, b_gc = banks
            for part in range(3):
                slot = rg_a.get()
                for k in range(8):
                    add("pe", lambda e, part=part, k=k, slot=slot: e.matmul(banks[part].t[:, :], slot[:, k, :], hT[:, k, :],
                                                                             start=(k == 0), stop=(k == 7)),
                        reads=[slot, hT], writes=[banks[part]])
            pc = pbuf.sub(cc * 2056, (cc + 1) * 2056)
            add("act", lambda e: e.copy(gc_sb[:, :], b_gc.t[:, :]), reads=[b_gc], writes=[gc_sb])
            add("dve", lambda e, cc=cc: e.tensor_copy(pbuf[:, cc, 0:2], pbuf[:, cc, 512:514]), reads=[pc], writes=[pc])
            add("dve", lambda e, cc=cc: e.tensor_tensor(pbuf[:, cc, 2:514], b_hc.t[:, :], gc_sb[:, :], ALU.mult),
                reads=[b_hc, gc_sb, pc], writes=[pc])
            cyc = cy.sub(cc * 2048, (cc + 1) * 2048)
            add("act", lambda e, cc=cc: e.activation(out=cy[:, cc, :], in_=pbuf[:, cc, 2:514], func=AF.Identity,
                                                     bias=pvT[:, R_CMB + cc:R_CMB + cc + 1],
                                                     scale=pvT[:, R_CMW + 8 + cc:R_CMW + 9 + cc]),
                reads=[pc, pvT], writes=[cyc])
            add("dve", lambda e, cc=cc: e.scalar_tensor_tensor(out=cy[:, cc, :], in0=pbuf[:, cc, 1:513],
                                                               scalar=pvT[:, R_CMW + 4 + cc:R_CMW + 5 + cc],
                                                               in1=cy[:, cc, :], op0=ALU.mult, op1=ALU.add),
                reads=[pc, pvT, cyc], writes=[cyc])
            add("dve", lambda e, cc=cc: e.scalar_tensor_tensor(out=cy[:, cc, :], in0=pbuf[:, cc, 0:512],
                                                               scalar=pvT[:, R_CMW + cc:R_CMW + cc + 1],
                                                               in1=cy[:, cc, :], op0=ALU.mult, op1=ALU.add),
                reads=[pc, pvT, cyc], writes=[cyc])
            add("dve", lambda e, cc=cc: e.tensor_tensor(cy[:, cc, :], b_gb.t[:, :], cy[:, cc, :], ALU.mult),
                reads=[b_gb, cyc], writes=[cyc])

        def conv_tail(cc):
            cyc = cy.sub(cc * 2048, (cc + 1) * 2048)
            add("act", lambda e: e.activation(out=sq[:, :], in_=cy[:, cc, :], func=AF.Square), reads=[cyc], writes=[sq])
            add("pe", lambda e: e.matmul(pb[5].t[:, :], ones_b[:, :], sq[:, :], start=(cc == 0), stop=(cc == 3)),
                reads=[ones_b, sq], writes=[pb[5]])

        for cc in range(4):
            conv_chunk(cc)
            if cc > 0:
                conv_tail(cc - 1)
        conv_tail(3)
        add("act", lambda e: e.activation(out=rstd_bc[:, :], in_=pb[5].t[:, :], func=AF.Sqrt, bias=epsc[:, 3:4], scale=1.0),
            reads=[pb[5], epsc], writes=[rstd_bc])
        add("dve", lambda e: e.reciprocal(rstd_bc[:, :], rstd_bc[:, :]), reads=[rstd_bc], writes=[rstd_bc])
        add("dve", lambda e: e.tensor_scalar(rstd_bc[:, :], rstd_bc[:, :], float(np.sqrt(512.0)), None, ALU.mult),
            reads=[rstd_bc], writes=[rstd_bc])
        for cc in range(4):
            add("dve", lambda e, cc=cc: e.scalar_tensor_tensor(out=catT[:, 4 + cc, :], in0=cy[:, cc, :],
                                                               scalar=pvT[:, R_CNG + cc:R_CNG + cc + 1],
                                                               in1=rstd_bc[:, :], op0=ALU.mult, op1=ALU.mult),
                reads=[cy.sub(cc * 2048, (cc + 1) * 2048), pvT, rstd_bc], writes=[catT.sub((4 + cc) * 1024, (5 + cc) * 1024)])

        def wout_block(j):
            for hf in range(2):
                bank = [pb[6], pb[7], pb[2], pb[3]][(j * 2 + hf) % 4]
                for k in range(8):
                    add("pe", lambda e, k=k, hf=hf, bank=bank: e.matmul(bank.t[:, :], catT[:, k, j * 128:(j + 1) * 128],
                                                                        w_outs[:, hf, k, :], start=(k == 0), stop=(k == 7)),
                        reads=[catT, w_outs.sub(hf * 8192, (hf + 1) * 8192)], writes=[bank])
                xr = x_res.sub(j * 4096 + hf * 2048, j * 4096 + (hf + 1) * 2048)
                gtmp = gtmp2[hf]
                add("dve", lambda e, hf=hf, bank=bank, gtmp=gtmp: e.tensor_tensor(gtmp[:, :], bank.t[:, :],
                                                                                  gt1_bc[:, hf * 512:(hf + 1) * 512], ALU.mult),
                    reads=[bank, bro], writes=[gtmp])
                add(_pe("res"), lambda e, hf=hf, gtmp=gtmp: e.tensor_tensor(x_res[:, j, hf * 512:(hf + 1) * 512],
                                                                            x_res[:, j, hf * 512:(hf + 1) * 512], gtmp[:, :], ALU.add),
                    reads=[xr, gtmp], writes=[xr])

        chk(4)
        for j in range(4):
            wout_block(j)
        chk(5)

        norm_to_hT(16)
        chk(6)

        def up_chunk(c):
            slot = rg_u.get()
            pr = c % 3
            banks = [pb[2 + 2 * pr], pb[3 + 2 * pr]]
            for a in range(2):
                for k in range(8):
                    add("pe", lambda e, a=a, k=k, slot=slot: e.matmul(banks[a].t[:, :], slot[:, k, a, :], hT[:, k, :],
                                                                      start=(k == 0), stop=(k == 7)),
                        reads=[slot, hT], writes=[banks[a]])
            us = u_sb[c % 3]
            ys = yab[c % 3]
            uh = uhalo.sub(c * 16, (c + 1) * 16)
            if c == 0:
                up_halo_in(0)
            usa = [us.sub(a * 2056, (a + 1) * 2056) for a in range(2)]
            ysa = [ys.sub(a * 2048, (a + 1) * 2048) for a in range(2)]
            for a in range(2):
                add("act", lambda e, a=a, us=us: e.copy(us[:, a, 2:514], banks[a].t[:, :]), reads=[banks[a]], writes=[usa[a]])
            add(_pe("halo"), lambda e, c=c, us=us: e.tensor_copy(uhalo[:, c, :, :], us[:, :, 512:514]), reads=[us], writes=[uh])
            for a in range(2):
                ch = a * NFF + c
                add("act", lambda e, a=a, ch=ch, us=us, ys=ys: e.activation(
                    out=ys[:, a, :], in_=us[:, a, 2:514], func=AF.Identity,
                    bias=pvT[:, R_CFB + ch:R_CFB + ch + 1], scale=pvT[:, R_CFW + 88 + ch:R_CFW + 89 + ch]),
                    reads=[usa[a], pvT], writes=[ysa[a]])
            for a in range(2):
                ch = a * NFF + c
                add("dve", lambda e, a=a, ch=ch, us=us, ys=ys: e.scalar_tensor_tensor(
                    out=ys[:, a, :], in0=us[:, a, 1:513], scalar=pvT[:, R_CFW + 44 + ch:R_CFW + 45 + ch],
                    in1=ys[:, a, :], op0=ALU.mult, op1=ALU.add), reads=[usa[a], pvT, ysa[a]], writes=[ysa[a]])
                add("dve", lambda e, a=a, ch=ch, us=us, ys=ys: e.scalar_tensor_tensor(
                    out=ys[:, a, :], in0=us[:, a, 0:512], scalar=pvT[:, R_CFW + ch:R_CFW + ch + 1],
                    in1=ys[:, a, :], op0=ALU.mult, op1=ALU.add), reads=[usa[a], pvT, ysa[a]], writes=[ysa[a]])
            if c + 1 < NFF:
                up_halo_in(c + 1)
            return ys

        def up_halo_in(c):
            us = u_sb[c % 3]
            add(_pe("halo"), lambda e: e.tensor_copy(us[:, :, 0:2], uhalo[:, c, :, :]),
                reads=[uhalo.sub(c * 16, (c + 1) * 16)], writes=[us])

        def up_tail(c, ys):
            add("act", lambda e: e.activation(out=ys[:, 0, :], in_=ys[:, 0, :], func=AF.Silu),
                reads=[ys.sub(0, 2048)], writes=[ys.sub(0, 2048)])
            add(_pe("gmul"), lambda e: e.tensor_tensor(gT[:, c, :], ys[:, 0, :], ys[:, 1, :], ALU.mult),
                reads=[ys], writes=[gT.sub(c * 1024, (c + 1) * 1024)])

        if tau + 1 < ntiles:
            load_mixer_weights(False)
        prev = None
        for c in range(NFF):
            ys_c = up_chunk(c)
            if prev is not None:
                up_tail(*prev)
            prev = (c, ys_c)
        up_tail(*prev)
        chk(7)

        acc_banks = {0: [pb[2], pb[3], pb[4], pb[5]], 1: [pb[6], pb[7], pb[2], pb[3]]}
        for hf in range(2):
            for cg in range(11):
                slot = rg_d.get()
                for j in range(4):
                    bank = acc_banks[hf][j]
                    for cl in range(2):
                        c = cg * 2 + cl
                        add("pe", lambda e, j=j, cl=cl, c=c, slot=slot, bank=bank: e.matmul(
                            bank.t[:, :], gT[:, c, j * 128:(j + 1) * 128], slot[:, cl, :],
                            start=(c == 0), stop=(c == NFF - 1)), reads=[gT, slot], writes=[bank])
            for j in range(4):
                bank = acc_banks[hf][j]
                xr = x_res.sub(j * 4096 + hf * 2048, j * 4096 + (hf + 1) * 2048)
                ftmp = ftmp2[j % 2]
                add("dve", lambda e, hf=hf, bank=bank, ftmp=ftmp: e.tensor_tensor(ftmp[:, :], bank.t[:, :],
                                                                                  gt2_bc[:, hf * 512:(hf + 1) * 512], ALU.mult),
                    reads=[bank, bro], writes=[ftmp])
                add(_pe("res"), lambda e, j=j, hf=hf, ftmp=ftmp: e.tensor_tensor(x_res[:, j, hf * 512:(hf + 1) * 512],
                                                                             x_res[:, j, hf * 512:(hf + 1) * 512], ftmp[:, :], ALU.add),
                    reads=[xr, ftmp], writes=[xr])

        chk(8)
        def final_block(j):
            xr = x_res.sub(j * 4096, (j + 1) * 4096)
            rms_rstd32(xr, x_res[:, j, :], 16 + j)
            add("dve", lambda e, j=j: e.scalar_tensor_tensor(out=x_res[:, j, :], in0=x_res[:, j, :],
                                                             scalar=stat[:, 16 + j:17 + j], in1=fg_bc,
                                                             op0=ALU.mult, op1=ALU.mult),
                reads=[xr, stat.sub((16 + j) * 4, (17 + j) * 4), bro], writes=[xr])
            add("sp", lambda e, j=j: e.dma_start(out=y_d[t0 + j * 128:t0 + (j + 1) * 128, :], in_=x_res[:, j, :]),
                reads=[xr], writes=[dram("y", t0 + j * 128, t0 + (j + 1) * 128)], dma=True)

        for j in range(4):
            final_block(j)

    try:
        if kst0:
            raise _Stop()
        for tau in range(ntiles):
            tile(tau)
    except _Stop:
        pass

    S.emit({"sp": ["sp"]})
    return nc


_CACHE = {}


def kernel(x, c, positions, w_ada, b_ada, norm1_g, w_in, conv_mix_w, conv_mix_b, ret_norm_g, conv_norm_g,
           w_out, norm2_g, w_up, conv_ffn_w, conv_ffn_b, w_down, final_g):
    f = lambda a: np.ascontiguousarray(np.asarray(a), dtype=np.float32)
    x, c, w_ada, b_ada, w_in, w_out, w_up, w_down = map(f, (x, c, w_ada, b_ada, w_in, w_out, w_up, w_down))
    positions = np.ascontiguousarray(np.asarray(positions), dtype=np.int32)
    cst, dec128 = _host_consts()
    if "nc" not in _CACHE:
        _CACHE["nc"] = build_nc(dec128)
    nc = _CACHE["nc"]

    def kmaj(w):
        K, N = w.shape
        return np.ascontiguousarray(w.reshape(K // 128, 128, N).transpose(1, 0, 2))

    wada_l = np.ascontiguousarray(kmaj(w_ada).reshape(128, 8, 12, 512).transpose(2, 0, 1, 3))
    win = kmaj(w_in)
    winb_l = np.ascontiguousarray(win[:, :, 0:2048].reshape(128, 8, 4, 512).transpose(2, 0, 1, 3))
    wina_l = np.ascontiguousarray(win[:, :, 2048:3584].reshape(128, 8, 12, 128).transpose(2, 0, 1, 3))
    wout_l = np.ascontiguousarray(kmaj(w_out).reshape(128, 8, 2, 512).transpose(2, 0, 1, 3))
    wup = kmaj(w_up).reshape(128, 8, 2, NFF, 128)
    wup_l = np.ascontiguousarray(wup.transpose(3, 0, 1, 2, 4))
    wdn = w_down.reshape(11, 2, 128, 2, 512)
    wdn_l = np.ascontiguousarray(wdn.transpose(3, 0, 2, 1, 4))

    pv = np.zeros((R_TOT, 128), np.float32)
    pv[R_BADA:R_BADA + 48] = b_ada.reshape(48, 128)
    pv[R_G1:R_G1 + 8] = f(norm1_g).reshape(8, 128)
    pv[R_G2:R_G2 + 8] = f(norm2_g).reshape(8, 128)
    pv[R_CMW:R_CMW + 12] = f(conv_mix_w).reshape(12, 128)
    pv[R_CMB:R_CMB + 4] = f(conv_mix_b).reshape(4, 128)
    pv[R_CNG:R_CNG + 4] = f(conv_norm_g).reshape(4, 128)
    pv[R_CFW:R_CFW + 132] = f(conv_ffn_w).reshape(132, 128)
    pv[R_CFB:R_CFB + 44] = f(conv_ffn_b).reshape(44, 128)
    brow = np.concatenate([f(ret_norm_g), f(final_g), b_ada[2048:3072], b_ada[5120:6144]])
    bro = np.ascontiguousarray(np.broadcast_to(brow[None, :], (128, C_TOT)))

    in_maps = []
    for b in range(NB):
        pvb = pv.copy()
        pvb[R_C:R_C + 8] = c[b].reshape(8, 128)
        in_maps.append({
            "x": x[b], "pos": positions[b].reshape(32, 128), "pv": pvb, "bro": bro, "cst": cst,
            "w_ada": wada_l, "w_in_b": winb_l, "w_in_a": wina_l, "w_out": wout_l, "w_up": wup_l, "w_down": wdn_l,
        })
    res = run_bass_kernel_spmd(nc, in_maps, core_ids=list(range(NB)))
    return np.stack([np.asarray(r["y"], dtype=np.float32) for r in res.results], axis=0)
```
